# Optimizing a Trainium2 kernel written in Bass

```python
import jax
import jax.numpy as jnp
from jax import lax
import numpy as np

D_MODEL = 1024
BATCH = 8
SEQ = 2048
DEPTH = 2

D_FF = 2816
DN_HEADS = 4
DN_HEAD_DIM = 128
DN_KEY_DIM = DN_HEADS * DN_HEAD_DIM
DN_VAL_DIM = DN_HEADS * DN_HEAD_DIM
DN_CONV = 4
DN_CHUNK = 64
CV_CHANNELS = 512
CV_KERNEL = 31
MLA_HEADS = 4
MLA_Q_LORA = 384
MLA_KV_LORA = 256
MLA_NOPE = 128
MLA_ROPE = 64
MLA_V = 128
ATTN_BLOCK = 128
ROPE_THETA = 10000.0
N_BRANCHES = 3
DEEPNORM_ALPHA = (2 * DEPTH) ** 0.25
DEEPNORM_BETA = (8 * DEPTH) ** -0.25
NORM_EPS = 1e-5
IN_SIZES = (DN_KEY_DIM, DN_KEY_DIM, DN_VAL_DIM, DN_VAL_DIM, DN_HEADS, DN_HEADS,
            2 * CV_CHANNELS, MLA_Q_LORA, MLA_KV_LORA, MLA_ROPE, N_BRANCHES * D_MODEL)
D_IN = sum(IN_SIZES)

kernel_name = 'hybrid_deltanet_conformer_mla_block'


def _split_points():
    return [int(v) for v in np.cumsum(IN_SIZES)[:-1]]


def layer_norm(x, g, b):
    xf = x.astype(jnp.float32)
    mu = jnp.mean(xf, axis=-1, keepdims=True)
    var = jnp.mean(jnp.square(xf - mu), axis=-1, keepdims=True)
    return ((xf - mu) * lax.rsqrt(var + NORM_EPS) * g + b).astype(x.dtype)


def rms_norm(x, w):
    xf = x.astype(jnp.float32)
    return (xf * lax.rsqrt(jnp.mean(jnp.square(xf), axis=-1, keepdims=True) + NORM_EPS) * w).astype(x.dtype)


def l2_normalize(x):
    xf = x.astype(jnp.float32)
    return xf * lax.rsqrt(jnp.sum(jnp.square(xf), axis=-1, keepdims=True) + 1e-6)


def swiglu(x, w_gate, w_up, w_down):
    return (jax.nn.silu(x @ w_gate) * (x @ w_up)) @ w_down


def causal_depthwise_conv(x, w):
    k, c = w.shape
    return lax.conv_general_dilated(
        x, w[:, None, :].astype(x.dtype), window_strides=(1,), padding=[(k - 1, 0)],
        dimension_numbers=('NWC', 'WIO', 'NWC'), feature_group_count=c)


def rope_tables(positions):
    half = MLA_ROPE // 2
    inv_freq = ROPE_THETA ** (-jnp.arange(half, dtype=jnp.float32) / half)
    ang = positions.astype(jnp.float32)[..., None] * inv_freq
    return jnp.cos(ang), jnp.sin(ang)


def apply_rope(x, cos, sin):
    half = x.shape[-1] // 2
    xf = x.astype(jnp.float32)
    x1, x2 = xf[..., :half], xf[..., half:]
    return jnp.concatenate([x1 * cos - x2 * sin, x2 * cos + x1 * sin], axis=-1).astype(x.dtype)


def gated_delta_rule(q, k, v, g, beta):
    b, t, h, dk = q.shape
    dv = v.shape[-1]
    c = DN_CHUNK
    n = t // c
    f32 = jnp.float32

    def chunks(a):
        a = a.reshape(b, n, c, h, *a.shape[3:])
        return jnp.moveaxis(a, (1, 3), (0, 2))

    q = chunks(q.astype(f32)) * (dk ** -0.5)
    k = chunks(k.astype(f32))
    v = chunks(v.astype(f32))
    beta = chunks(beta.astype(f32))
    g = jnp.cumsum(chunks(g.astype(f32)), axis=-1)
    causal = jnp.tril(jnp.ones((c, c), dtype=bool))
    strict = jnp.tril(jnp.ones((c, c), dtype=bool), -1)
    decay = jnp.exp(jnp.where(causal, g[..., :, None] - g[..., None, :], -jnp.inf))
    kk = jnp.einsum('nbhik,nbhjk->nbhij', k, k)
    a_mat = jnp.where(strict, beta[..., None] * kk * decay, 0.0) + jnp.eye(c, dtype=f32)
    rhs = jnp.concatenate([v * beta[..., None], k * (beta * jnp.exp(g))[..., None]], axis=-1)
    sol = lax.linalg.triangular_solve(a_mat, rhs, left_side=True, lower=True, unit_diagonal=True)
    u, w = sol[..., :dv], sol[..., dv:]
    qk = jnp.where(causal, jnp.einsum('nbhik,nbhjk->nbhij', q, k) * decay, 0.0)
    q_dec = q * jnp.exp(g)[..., None]
    g_last = g[..., -1]
    k_dec = k * jnp.exp(g_last[..., None] - g)[..., None]

    def step(state, xs):
        u_i, w_i, qk_i, q_dec_i, k_dec_i, gl_i = xs
        v_new = u_i - jnp.einsum('bhck,bhkv->bhcv', w_i, state)
        o = jnp.einsum('bhck,bhkv->bhcv', q_dec_i, state) + jnp.einsum('bhij,bhjv->bhiv', qk_i, v_new)
        state = state * jnp.exp(gl_i)[..., None, None] + jnp.einsum('bhck,bhcv->bhkv', k_dec_i, v_new)
        return state, o

    s0 = jnp.zeros((b, h, dk, dv), f32)
    _, o = lax.scan(step, s0, (u, w, qk, q_dec, k_dec, g_last))
    return jnp.moveaxis(o, (0, 2), (1, 3)).reshape(b, t, h, dv)


def deltanet_branch(q, k, v, z, a, beta_logit, conv_w, a_log, dt_bias, norm_w, w_o):
    b, t, _ = q.shape
    qkv = jax.nn.silu(causal_depthwise_conv(jnp.concatenate([q, k, v], axis=-1), conv_w))
    q, k, v = jnp.split(qkv, [DN_KEY_DIM, 2 * DN_KEY_DIM], axis=-1)
    q = l2_normalize(q.reshape(b, t, DN_HEADS, DN_HEAD_DIM))
    k = l2_normalize(k.reshape(b, t, DN_HEADS, DN_HEAD_DIM))
    v = v.reshape(b, t, DN_HEADS, DN_HEAD_DIM)
    beta = jax.nn.sigmoid(beta_logit.astype(jnp.float32))
    g = -jnp.exp(a_log.astype(jnp.float32)) * jax.nn.softplus(a.astype(jnp.float32) + dt_bias.astype(jnp.float32))
    o = gated_delta_rule(q, k, v, g, beta)
    o = rms_norm(o, norm_w) * jax.nn.silu(z.reshape(b, t, DN_HEADS, DN_HEAD_DIM).astype(jnp.float32))
    return o.reshape(b, t, DN_VAL_DIM).astype(z.dtype) @ w_o


def conv_branch(u, glu_b, dw_w, dw_b, ln_g, ln_b, w_pw2, b_pw2):
    u = u + glu_b
    h = u[..., :CV_CHANNELS] * jax.nn.sigmoid(u[..., CV_CHANNELS:])
    h = causal_depthwise_conv(h, dw_w) + dw_b
    h = jax.nn.silu(layer_norm(h, ln_g, ln_b))
    return h @ w_pw2 + b_pw2


def causal_block_attention(q_nope, q_rope, k_nope, k_rope, v):
    t = q_nope.shape[1]
    scale = (MLA_NOPE + MLA_ROPE) ** -0.5
    outs = []
    for i in range(t // ATTN_BLOCK):
        q0, q1 = i * ATTN_BLOCK, (i + 1) * ATTN_BLOCK
        s = (jnp.einsum('bqhd,bkhd->bhqk', q_nope[:, q0:q1], k_nope[:, :q1])
             + jnp.einsum('bqhr,bkr->bhqk', q_rope[:, q0:q1], k_rope[:, :q1])).astype(jnp.float32) * scale
        mask = jnp.arange(q1)[None, :] <= jnp.arange(q0, q1)[:, None]
        p = jax.nn.softmax(jnp.where(mask, s, -jnp.inf), axis=-1).astype(v.dtype)
        outs.append(jnp.einsum('bhqk,bkhv->bqhv', p, v[:, :q1]))
    return jnp.concatenate(outs, axis=1)


def mla_branch(c_q, c_kv, k_r, cos, sin, q_norm_w, w_uq, kv_norm_w, w_ukv, w_o):
    b, t, _ = c_q.shape
    q = (rms_norm(c_q, q_norm_w) @ w_uq).reshape(b, t, MLA_HEADS, MLA_NOPE + MLA_ROPE)
    q_nope = q[..., :MLA_NOPE]
    q_rope = apply_rope(q[..., MLA_NOPE:], cos[:, :, None, :], sin[:, :, None, :])
    kv = (rms_norm(c_kv, kv_norm_w) @ w_ukv).reshape(b, t, MLA_HEADS, MLA_NOPE + MLA_V)
    k_nope, v = kv[..., :MLA_NOPE], kv[..., MLA_NOPE:]
    k_rope = apply_rope(k_r, cos, sin)
    o = causal_block_attention(q_nope, q_rope, k_nope, k_rope, v)
    return o.reshape(b, t, MLA_HEADS * MLA_V) @ w_o


def hybrid_mixer(x, cos, sin, w_in, b_gate,
                 dn_conv_w, dn_a_log, dn_dt_bias, dn_norm_w, dn_w_o,
                 cv_glu_b, cv_dw_w, cv_dw_b, cv_ln_g, cv_ln_b, cv_w_pw2, cv_b_pw2,
                 mla_q_norm_w, mla_w_uq, mla_kv_norm_w, mla_w_ukv, mla_w_o, w_out):
    b, t, d = x.shape
    proj = x @ w_in
    (dq, dk, dv, dz, da, dbeta, glu, c_q, c_kv, k_r, gate_logits) = jnp.split(proj, _split_points(), axis=-1)
    y_dn = deltanet_branch(dq, dk, dv, dz, da, dbeta, dn_conv_w, dn_a_log, dn_dt_bias, dn_norm_w, dn_w_o)
    y_cv = conv_branch(glu, cv_glu_b, cv_dw_w, cv_dw_b, cv_ln_g, cv_ln_b, cv_w_pw2, cv_b_pw2)
    y_mla = mla_branch(c_q, c_kv, k_r, cos, sin, mla_q_norm_w, mla_w_uq, mla_kv_norm_w, mla_w_ukv, mla_w_o)
    gates = jax.nn.sigmoid((gate_logits + b_gate).astype(jnp.float32)).astype(x.dtype).reshape(b, t, N_BRANCHES, d)
    merged = gates[:, :, 0] * y_dn + gates[:, :, 1] * y_cv + gates[:, :, 2] * y_mla
    return merged @ w_out


def setup_inputs(seed: int = 0) -> dict:
    key = jax.random.key(seed)
    ks = iter(jax.random.split(key, 48))
    L = DEPTH
    f32 = jnp.float32

    def w(shape, fan_in, scale=1.0):
        return jax.random.normal(next(ks), shape, f32) * (scale * fan_in ** -0.5)

    def gain(shape):
        return 1.0 + 0.02 * jax.random.normal(next(ks), shape, f32)

    def bias(shape):
        return 0.02 * jax.random.normal(next(ks), shape, f32)

    x = jax.random.normal(next(ks), (BATCH, SEQ, D_MODEL), f32)
    positions = (jax.random.randint(next(ks), (BATCH, 1), 0, 4096, dtype=jnp.int32)
                 + jnp.arange(SEQ, dtype=jnp.int32)[None, :])
    dt = jnp.exp(jax.random.uniform(next(ks), (L, DN_HEADS), f32, np.log(1e-3), np.log(1e-1)))
    return {
        'x': x,
        'positions': positions,
        'ln1_g': gain((L, D_MODEL)),
        'ln1_b': bias((L, D_MODEL)),
        'ffn1_w_gate': w((L, D_MODEL, D_FF), D_MODEL),
        'ffn1_w_up': w((L, D_MODEL, D_FF), D_MODEL),
        'ffn1_w_down': w((L, D_FF, D_MODEL), D_FF, DEEPNORM_BETA),
        'w_in': w((L, D_MODEL, D_IN), D_MODEL),
        'b_gate': bias((L, N_BRANCHES * D_MODEL)),
        'dn_conv_w': w((L, DN_CONV, 2 * DN_KEY_DIM + DN_VAL_DIM), DN_CONV),
        'dn_a_log': jnp.log(jax.random.uniform(next(ks), (L, DN_HEADS), f32, 1.0, 16.0)),
        'dn_dt_bias': dt + jnp.log(-jnp.expm1(-dt)),
        'dn_norm_w': gain((L, DN_HEAD_DIM)),
        'dn_w_o': w((L, DN_VAL_DIM, D_MODEL), DN_VAL_DIM, DEEPNORM_BETA),
        'cv_glu_b': bias((L, 2 * CV_CHANNELS)),
        'cv_dw_w': w((L, CV_KERNEL, CV_CHANNELS), CV_KERNEL),
        'cv_dw_b': bias((L, CV_CHANNELS)),
        'cv_ln_g': gain((L, CV_CHANNELS)),
        'cv_ln_b': bias((L, CV_CHANNELS)),
        'cv_w_pw2': w((L, CV_CHANNELS, D_MODEL), CV_CHANNELS, DEEPNORM_BETA),
        'cv_b_pw2': bias((L, D_MODEL)),
        'mla_q_norm_w': gain((L, MLA_Q_LORA)),
        'mla_w_uq': w((L, MLA_Q_LORA, MLA_HEADS * (MLA_NOPE + MLA_ROPE)), MLA_Q_LORA),
        'mla_kv_norm_w': gain((L, MLA_KV_LORA)),
        'mla_w_ukv': w((L, MLA_KV_LORA, MLA_HEADS * (MLA_NOPE + MLA_V)), MLA_KV_LORA),
        'mla_w_o': w((L, MLA_HEADS * MLA_V, D_MODEL), MLA_HEADS * MLA_V, DEEPNORM_BETA),
        'w_out': w((L, D_MODEL, D_MODEL), D_MODEL, DEEPNORM_BETA),
        'ln2_g': gain((L, D_MODEL)),
        'ln2_b': bias((L, D_MODEL)),
        'ffn2_w_gate': w((L, D_MODEL, D_FF), D_MODEL),
        'ffn2_w_up': w((L, D_MODEL, D_FF), D_MODEL),
        'ffn2_w_down': w((L, D_FF, D_MODEL), D_FF, DEEPNORM_BETA),
        'ln3_g': gain((L, D_MODEL)),
        'ln3_b': bias((L, D_MODEL)),
    }


def reference(x, positions, ln1_g, ln1_b, ffn1_w_gate, ffn1_w_up, ffn1_w_down, w_in, b_gate,
              dn_conv_w, dn_a_log, dn_dt_bias, dn_norm_w, dn_w_o,
              cv_glu_b, cv_dw_w, cv_dw_b, cv_ln_g, cv_ln_b, cv_w_pw2, cv_b_pw2,
              mla_q_norm_w, mla_w_uq, mla_kv_norm_w, mla_w_ukv, mla_w_o,
              w_out, ln2_g, ln2_b, ffn2_w_gate, ffn2_w_up, ffn2_w_down, ln3_g, ln3_b):
    cos, sin = rope_tables(positions)
    h = x
    for l in range(DEPTH):
        h = layer_norm(DEEPNORM_ALPHA * h + 0.5 * swiglu(h, ffn1_w_gate[l], ffn1_w_up[l], ffn1_w_down[l]),
                       ln1_g[l], ln1_b[l])
        mix = hybrid_mixer(h, cos, sin, w_in[l], b_gate[l],
                           dn_conv_w[l], dn_a_log[l], dn_dt_bias[l], dn_norm_w[l], dn_w_o[l],
                           cv_glu_b[l], cv_dw_w[l], cv_dw_b[l], cv_ln_g[l], cv_ln_b[l], cv_w_pw2[l], cv_b_pw2[l],
                           mla_q_norm_w[l], mla_w_uq[l], mla_kv_norm_w[l], mla_w_ukv[l], mla_w_o[l], w_out[l])
        h = layer_norm(DEEPNORM_ALPHA * h + mix, ln2_g[l], ln2_b[l])
        h = layer_norm(DEEPNORM_ALPHA * h + 0.5 * swiglu(h, ffn2_w_gate[l], ffn2_w_up[l], ffn2_w_down[l]),
                       ln3_g[l], ln3_b[l])
    return h
```

```python
from contextlib import ExitStack
DN_STOP = 9
import numpy as np
import concourse.bass as bass
import concourse.mybir as mybir
from concourse.bass_utils import run_bass_kernel_spmd

F32 = mybir.dt.float32
BF16 = mybir.dt.bfloat16
I32 = mybir.dt.int32
AF = mybir.ActivationFunctionType
ALU = mybir.AluOpType
AX = mybir.AxisListType

D = 1024
T = 2048
DEPTH = 2
DFF = 2816
NFF = DFF // 128
DIN = 6856
ALPHA = (2 * DEPTH) ** 0.25
EPS = 1e-5
NEG = -30000.0

C_DQ, C_DK, C_DV, C_DZ, C_DA, C_DB = 0, 512, 1024, 1536, 2048, 2052
C_GLU, C_CQ, C_CKV, C_KR, C_GATE = 2056, 3080, 3464, 3720, 3784


class Tok:
    __slots__ = ("name", "lw", "rs", "sem", "semv", "excl")

    def __init__(self, name="", excl=False):
        self.name = name
        self.excl = excl
        self.lw = None
        self.rs = []
        self.sem = None
        self.semv = 0


ENGS = ("pe", "act", "dve", "pool", "sp")


class Prog:
    def __init__(self, nc, stack):
        self.nc = nc
        self.stack = stack
        self.ops = {e: [] for e in ENGS}
        self.cnt = {e: 0 for e in ENGS}
        self.sem = {e: stack.enter_context(nc.semaphore("s_" + e)) for e in ENGS}
        self.known = {e: {} for e in ENGS}
        self.semobj = {}
        for e in ENGS:
            self.semobj[e] = self.sem[e]
        self.free_dma_sems = []
        self.used_dma_sems = []
        self.gen = 0
        self.ndma = 0
        self.bar = stack.enter_context(nc.semaphore("s_bar"))
        self.barv = 0
        self.out_waits = []
        self._semv = {}

    def tok(self, name=""):
        return Tok(name)

    def toks(self, n, name=""):
        return [Tok(name + str(i)) for i in range(n)]

    def _deps(self, eng, reads, writes):
        deps = {}

        def add(d):
            if d is None:
                return
            k, v = d
            if deps.get(k, 0) < v:
                deps[k] = v

        for b in reads:
            add(b.lw)
            if b.excl:
                for r in b.rs:
                    if r[0] != eng:
                        add(r)
        for b in writes:
            add(b.lw)
            for r in b.rs:
                add(r)
        out = []
        kn = self.known[eng]
        for k, v in deps.items():
            if k == eng and eng == "pe":
                continue
            if kn.get(k, 0) >= v:
                continue
            kn[k] = v
            out.append((k, v))
        return out

    def call(self, eng, method, reads, writes, signal=True, **kw):
        kw = {k: v for k, v in kw.items() if v is not None}
        return self.op(eng, lambda e, m=method, kw=kw: getattr(e, m)(**kw), reads, writes, signal)

    def mm(self, out, lhsT, rhs, start, stop, reads, writes, signal=True):
        return self.call("pe", "matmul", reads, writes, signal, out=out, lhsT=lhsT, rhs=rhs, start=start, stop=stop)

    def tr(self, out, in_, identity, reads, writes):
        return self.call("pe", "transpose", reads, writes, True, out=out, in_=in_, identity=identity)

    def act(self, out, in_, func, reads, writes, bias=None, scale=None, accum_out=None, eng="act"):
        return self.call(eng, "activation", reads, writes, True, out=out, in_=in_, func=func, bias=bias, scale=scale,
                         accum_out=accum_out)

    def tt(self, eng, out, in0, in1, op, reads, writes):
        return self.call(eng, "tensor_tensor", reads, writes, True, out=out, in0=in0, in1=in1, op=op)

    def ts(self, eng, out, in0, s1, s2, op0, op1, reads, writes, accum_out=None):
        kw = dict(out=out, in0=in0, scalar1=s1, scalar2=s2, op0=op0)
        if op1 is not None:
            kw["op1"] = op1
        if accum_out is not None:
            kw["accum_out"] = accum_out
        return self.op(eng, lambda e, kw=kw: e.tensor_scalar(**kw), reads, writes, True)

    def stt(self, eng, out, in0, scalar, in1, op0, op1, reads, writes):
        return self.call(eng, "scalar_tensor_tensor", reads, writes, True, out=out, in0=in0, scalar=scalar, in1=in1,
                         op0=op0, op1=op1)

    def copy(self, eng, out, in_, reads, writes):
        return self.call(eng, "tensor_copy", reads, writes, True, out=out, in_=in_)

    def op(self, eng, fn, reads=(), writes=(), signal=True):
        waits = self._deps(eng, reads, writes)
        if signal:
            self.cnt[eng] += 1
            me = (eng, self.cnt[eng])
        else:
            me = (eng, self.cnt[eng] + 1)
        self.ops[eng].append((waits, fn, signal, None))
        for b in writes:
            b.lw = me
            b.rs = []
        for b in reads:
            if b not in writes:
                b.rs.append(me)
        return me

    def dma(self, q, out, in_, reads=(), writes=()):
        wt = writes[0]
        if wt.sem is None or wt.sem[1] != self.gen:
            free = [k for k in self.free_dma_sems if k.startswith(q)]
            if free:
                key = free[-1]
                self.free_dma_sems.remove(key)
            else:
                key = "%s_d%d" % (q, self.ndma)
                self.ndma += 1
                self.semobj[key] = self.stack.enter_context(self.nc.semaphore("s_" + key))
            wt.sem = (key, self.gen)
            wt.semv = self._semv.get(key, 0)
            self.used_dma_sems.append(key)
        assert wt.sem[0].startswith(q), "a token's DMAs must stay on one queue between barriers"
        waits = []
        deps = {}
        for b in reads:
            if b.lw is not None:
                deps[b.lw[0]] = max(deps.get(b.lw[0], 0), b.lw[1])
        for b in writes:
            if b.lw is not None and (b.sem is None or b.lw[0] != b.sem[0]):
                deps[b.lw[0]] = max(deps.get(b.lw[0], 0), b.lw[1])
            for r in b.rs:
                deps[r[0]] = max(deps.get(r[0], 0), r[1])
        kn = self.known[q]
        for k, v in deps.items():
            if kn.get(k, 0) >= v:
                continue
            kn[k] = v
            waits.append((k, v))
        wt.semv += 16
        me = (wt.sem[0], wt.semv)
        self._semv[wt.sem[0]] = wt.semv
        self.ops[q].append((waits, lambda e, o=out, i=in_: e.dma_start(out=o, in_=i), False, me))
        for b in writes:
            if b.lw is None or b.sem is None or b.lw[0] != b.sem[0]:
                b.rs = []
            b.lw = me
        for b in reads:
            b.rs.append(me)
        return me

    def barrier(self):
        self.barv += 1
        target = self.barv * len(ENGS)
        drain = [(k, self._semv[k]) for k in self._semv]
        for e in ENGS:
            own = [(e, self.cnt[e])] if (e != "sp" and self.cnt[e] > 0) else []
            self.ops[e].append(("bar", target, own + (drain if e == "sp" else []), None))
        for e in ENGS:
            for e2 in ENGS:
                self.known[e][e2] = self.cnt[e2]
            for k, v in drain:
                self.known[e][k] = v
        self.gen += 1
        self.free_dma_sems.extend(self.used_dma_sems)
        self.used_dma_sems = []

    def flush(self, final_waits=()):
        nc = self.nc
        ops = self.ops
        semobj = self.semobj
        sems = self.sem
        bar = self.bar

        def replay(eng_name):
            def body(e):
                for waits, fn, signal, dmasem in ops[eng_name]:
                    if waits == "bar":
                        for k, v in signal:
                            e.wait_ge(semobj[k], v)
                        e.sem_inc(bar, 1)
                        e.wait_ge(bar, fn)
                        continue
                    for k, v in waits:
                        e.wait_ge(semobj[k], v)
                    ins = fn(e)
                    if dmasem is not None:
                        ins.then_inc(semobj[dmasem[0]], 16)
                    elif signal:
                        ins.then_inc(sems[eng_name], 1)
                if eng_name == "sp":
                    for k, v in final_waits:
                        e.wait_ge(semobj[k], v)
            return body

        with nc.Block() as block:
            block.tensor(replay("pe"))
            block.scalar(replay("act"))
            block.vector(replay("dve"))
            block.gpsimd(replay("pool"))
            block.sync(replay("sp"))
        self.ops = {e: [] for e in ENGS}


def col_chunks(n, step):
    return [(i, min(step, n - i)) for i in range(0, n, step)]


class Ctx:
    pass


(K_ID, K_ONE, K_I1024, K_I512, K_I384, K_I256, K_I128, K_TRI, K_TRIU, K_MNEG, K_MNEGT, K_STRICT,
 K_NEGONE, K_CHUNK, K_AMASK, K_MISC, K_SEL63, K_SEL127) = range(18)
NCONST = 18


def make_consts():
    c = np.zeros((NCONST, 128, 128), np.float32)
    i = np.arange(128)[:, None]
    j = np.arange(128)[None, :]
    same = (i // 64) == (j // 64)
    c[K_ID] = np.eye(128)
    c[K_ONE] = 1.0
    c[K_I1024] = 1.0 / 1024
    c[K_I512] = 1.0 / 512
    c[K_I384] = 1.0 / 384
    c[K_I256] = 1.0 / 256
    c[K_I128] = 1.0 / 128
    c[K_TRI] = (same & (i <= j))
    c[K_TRIU] = (same & (i > j))
    c[K_MNEG] = np.where(same & (j <= i), 0.0, NEG)
    c[K_MNEGT] = np.where(same & (j >= i), 0.0, NEG)
    c[K_STRICT] = (same & (j < i))
    c[K_NEGONE] = -1.0
    c[K_CHUNK] = ((i // 64) == j)
    c[K_AMASK] = np.where(i <= j, 0.0, NEG)
    half = 32
    inv_freq = (10000.0 ** (-np.arange(half, dtype=np.float32) / half)).astype(np.float32)
    c[K_MISC][:64, 0] = np.concatenate([inv_freq, inv_freq])
    c[K_MISC][:64, 1] = np.concatenate([-np.ones(32), np.ones(32)])
    c[K_SEL63][63, :] = 1.0
    c[K_SEL127][127, :] = 1.0
    return np.ascontiguousarray(c.transpose(1, 0, 2))


class ColMap:
    def __init__(self):
        self.n = 0
        self.off = {}
        self.parts = []

    def add(self, name, arr2d):
        self.off[name] = (self.n, arr2d.shape[1])
        self.n += arr2d.shape[1]
        self.parts.append(np.asarray(arr2d, np.float32))

    def build(self):
        return np.ascontiguousarray(np.concatenate(self.parts, axis=1))


def chunked(v):
    return np.asarray(v).reshape(-1, 128).T


def make_cols(inp):
    cm = ColMap()
    for l in range(DEPTH):
        for nm in ("ln1_g", "ln1_b", "ln2_g", "ln2_b", "ln3_g", "ln3_b", "b_gate", "cv_glu_b", "cv_dw_b",
                   "cv_ln_g", "cv_ln_b", "cv_b_pw2", "mla_q_norm_w", "mla_kv_norm_w", "dn_norm_w"):
            cm.add("%s%d" % (nm, l), chunked(inp[nm][l]))
        cw = np.asarray(inp["dn_conv_w"][l])
        cm.add("dn_conv_w%d" % l, cw.reshape(4, 12, 128).transpose(2, 0, 1).reshape(128, 48))
        dw = np.asarray(inp["cv_dw_w"][l])
        cm.add("cv_dw_w%d" % l, dw.reshape(31, 4, 128).transpose(2, 0, 1).reshape(128, 124))
        cm.add("dn_a_log%d" % l, np.broadcast_to(np.asarray(inp["dn_a_log"][l])[None, :], (128, 4)))
        cm.add("dn_dt_bias%d" % l, np.broadcast_to(np.asarray(inp["dn_dt_bias"][l])[None, :], (128, 4)))
    return cm


def colmap_layout():
    fake = {
        "ln1_g": np.zeros((2, 1024)), "ln1_b": np.zeros((2, 1024)), "ln2_g": np.zeros((2, 1024)),
        "ln2_b": np.zeros((2, 1024)), "ln3_g": np.zeros((2, 1024)), "ln3_b": np.zeros((2, 1024)),
        "b_gate": np.zeros((2, 3072)), "cv_glu_b": np.zeros((2, 1024)), "cv_dw_b": np.zeros((2, 512)),
        "cv_ln_g": np.zeros((2, 512)), "cv_ln_b": np.zeros((2, 512)), "cv_b_pw2": np.zeros((2, 1024)),
        "mla_q_norm_w": np.zeros((2, 384)), "mla_kv_norm_w": np.zeros((2, 256)), "dn_norm_w": np.zeros((2, 128)),
        "dn_conv_w": np.zeros((2, 4, 1536)), "cv_dw_w": np.zeros((2, 31, 512)),
        "dn_a_log": np.zeros((2, 4)), "dn_dt_bias": np.zeros((2, 4)),
    }
    return make_cols(fake)


WEIGHTS = {
    "ffn1_w_gate": (DEPTH, D, DFF), "ffn1_w_up": (DEPTH, D, DFF), "ffn1_w_down": (DEPTH, DFF, D),
    "ffn2_w_gate": (DEPTH, D, DFF), "ffn2_w_up": (DEPTH, D, DFF), "ffn2_w_down": (DEPTH, DFF, D),
    "w_in": (DEPTH, D, DIN), "dn_w_o": (DEPTH, 512, D), "cv_w_pw2": (DEPTH, 512, D),
    "mla_w_uq": (DEPTH, 384, 768), "mla_w_ukv": (DEPTH, 256, 1024), "mla_w_o": (DEPTH, 512, D),
    "w_out": (DEPTH, D, D),
}


class Arena:
    def __init__(self, handle, nwords):
        self.h = handle
        self.n = nwords
        self.off = 0

    def reset(self):
        self.off = 0

    def f32(self, *shape):
        n = int(np.prod(shape))
        assert self.off + n <= self.n, ("arena overflow", self.off + n, self.n)
        v = self.h[:, self.off:self.off + n]
        self.off += n
        return self._shape(v, shape)

    def bf16(self, *shape):
        n = int(np.prod(shape))
        w = (n + 1) // 2
        assert self.off + w <= self.n, ("arena overflow", self.off + w, self.n)
        v = self.h[:, self.off:self.off + w].bitcast(BF16)
        if 2 * w != n:
            v = v[:, 0:n]
        self.off += w
        return self._shape(v, shape)

    @staticmethod
    def _shape(v, shape):
        if len(shape) == 1:
            return v
        if len(shape) == 2:
            return v.rearrange("p (a b) -> p a b", a=shape[0])
        if len(shape) == 3:
            return v.rearrange("p (a b c) -> p a b c", a=shape[0], b=shape[1])
        raise ValueError(shape)


AR_WORDS = 33200


def build_program(upto=None, dbg=None):
    nc = bass.Bass("TRN2", target_bir_lowering=False)
    stack = ExitStack()
    cx = Ctx()
    cx.nc = nc
    cx.dbg = dbg
    cx.cm = colmap_layout()
    dr = {}
    dr["xT"] = nc.dram_tensor("xT", [128, 8, T], F32, kind="ExternalInput").ap()
    dr["pos"] = nc.dram_tensor("pos", [1, T], I32, kind="ExternalInput").ap()
    dr["consts"] = nc.dram_tensor("consts", [128, NCONST, 128], F32, kind="ExternalInput").ap()
    dr["cols"] = nc.dram_tensor("cols", [128, cx.cm.n], F32, kind="ExternalInput").ap()
    for k, shp in WEIGHTS.items():
        dr[k] = nc.dram_tensor(k, list(shp), F32, kind="ExternalInput").ap()
    dr["outT"] = nc.dram_tensor("outT", [128, 8, T], F32, kind="ExternalOutput").ap()
    dr["xs"] = nc.dram_tensor("xs", [128, 8, T], F32).ap()
    if dbg is not None:
        dr["dbg"] = nc.dram_tensor("dbg", [128, 8, T], F32, kind="ExternalOutput").ap()
    cx.dr = dr

    X = stack.enter_context(nc.sbuf_tensor("X", [128, 8, T], F32))
    CST = stack.enter_context(nc.sbuf_tensor("CST", [128, NCONST, 128], F32))
    CSTB = stack.enter_context(nc.sbuf_tensor("CSTB", [128, 3, 128], BF16))
    COLS = stack.enter_context(nc.sbuf_tensor("COLS", [128, cx.cm.n], F32))
    ARH = stack.enter_context(nc.sbuf_tensor("AR", [128, AR_WORDS], F32))
    EPSC = stack.enter_context(nc.sbuf_tensor("EPSC", [128, 8], F32))
    cx.POSI = stack.enter_context(nc.sbuf_tensor("POSI", [64, 512], I32))
    cx.EPSC = EPSC
    cx.X, cx.CST, cx.CSTB, cx.COLS = X, CST, CSTB, COLS
    cx.ar = Arena(ARH, AR_WORDS)
    cx.ps = [stack.enter_context(nc.psum_tensor("ps%d" % i, [128, 512], F32)) for i in range(8)]

    P = Prog(nc, stack)
    cx.P = P
    cx.tk_ps = [Tok("ps%d" % i, excl=True) for i in range(8)]
    cx.tk_const = P.tok("const")
    cx.tk_X = [[P.tok("X%d_%d" % (c, t)) for t in range(4)] for c in range(8)]
    cx.tk_out = P.tok("out")
    cx.tk_dbg = P.tok("dbg")
    cx.ones_inv = {8: CST[:, K_I1024, :], 4: CST[:, K_I512, :], 3: CST[:, K_I384, :], 2: CST[:, K_I256, :],
                   1: CST[:, K_I128, :]}

    def col(name, l):
        o, n = cx.cm.off["%s%d" % (name, l)]
        return COLS[:, o:o + n]
    cx.col = col

    eps_vals = [EPS, 1e-6, 0.0, 1.0, -np.pi, 0.0, 0.0, 0.0]
    for i, v in enumerate(eps_vals):
        P.call("dve", "memset", [], [cx.tk_const], True, ap=EPSC[:, i:i + 1], constant=float(v))
    cx.epscol = lambda eps: EPSC[:, eps_vals.index(eps):eps_vals.index(eps) + 1]
    P.dma("sp", CST[:], dr["consts"], writes=[cx.tk_const])
    P.dma("sp", COLS[:], dr["cols"], writes=[cx.tk_const])
    for c in range(8):
        P.dma("sp", X[:, c, 0:512], dr["xT"][:, c, 0:512], writes=[cx.tk_X[c][0]])
    cx.pending_x = [1, 2, 3]
    P.copy("dve", CSTB[:, 0, :], CST[:, K_ID, :], [cx.tk_const], [cx.tk_const])
    P.copy("dve", CSTB[:, 1, :], CST[:, K_AMASK, :], [cx.tk_const], [cx.tk_const])
    P.copy("dve", CSTB[:, 2, :], CST[:, K_ONE, :], [cx.tk_const], [cx.tk_const])

    def done():
        return finish(cx, stack)

    for l in range(DEPTH):
        phase_ffn(cx, l, 1)
        if upto == ("ffn1", l):
            return done()
        phase_mixer(cx, l, upto)
        if upto is not None and upto[1] == l and upto[0] in ("dn", "cv", "mla", "mix"):
            return done()
        phase_ffn(cx, l, 2)
        if upto == ("ffn2", l):
            return done()
    return done()


def finish(cx, stack):
    P = cx.P
    P.barrier()
    last = None
    for c in range(8):
        last = P.dma("sp", cx.dr["outT"][:, c, :], cx.X[:, c, :], reads=cx.tk_X[c], writes=[cx.tk_out])
    fw = [last]
    if cx.dbg is not None and cx.tk_dbg.lw is not None:
        fw.append(cx.tk_dbg.lw)
    P.flush(final_waits=fw)
    stack.close()
    return cx.nc


def dbg_dump(cx, src_ap, reads, c0=0):
    n = src_ap.shape[-1]
    cx.P.dma("sp", cx.dr["dbg"][:, c0, 0:n], src_ap, reads=reads, writes=[cx.tk_dbg])


def phase_ffn(cx, l, which):
    P, X, ar = cx.P, cx.X, cx.ar
    P.barrier()
    ar.reset()
    pre = "ffn%d_" % which
    wg_d, wu_d, wd_d = cx.dr[pre + "w_gate"][l], cx.dr[pre + "w_up"][l], cx.dr[pre + "w_down"][l]
    lng = cx.col("ln1_g" if which == 1 else "ln3_g", l)
    lnb = cx.col("ln1_b" if which == 1 else "ln3_b", l)
    xb = ar.bf16(8, 512)
    hT = ar.bf16(NFF, 512)
    wd = ar.bf16(NFF, 1024)
    GW = 512
    groups = col_chunks(DFF, GW)
    wgu = [[ar.bf16(8, GW) for _ in range(2)] for _ in range(2)]
    sg = [ar.f32(512) for _ in range(2)]
    lntmp = ar.f32(4, 512)
    tk_xb, tk_wd = P.tok("xb"), P.tok("wd")
    tk_h = P.toks(NFF, "h")
    tk_w = [[P.tok("wg"), P.tok("wu")] for _ in range(2)]
    tk_sg = P.toks(2, "sg")
    tk_ln = P.toks(4, "ln")
    ps, tkp = cx.ps, cx.tk_ps

    wdv = wd_d.rearrange("(c p) d -> p c d", p=128)
    wd_loaded = [False]

    def load_wd():
        if not wd_loaded[0]:
            wd_loaded[0] = True
            for c0, n in col_chunks(NFF, 6):
                P.dma("pool", wd[:, c0:c0 + n, :], wdv[:, c0:c0 + n, :], writes=[tk_wd])

    gi = 0
    deferred = []
    ln_gen = None
    for tt in range(4):
        ts = slice(tt * 512, (tt + 1) * 512)
        for c in range(8):
            P.act(xb[:, c, :], X[:, c, ts], AF.Copy, [cx.tk_X[c][tt]], [tk_xb])
        for gidx, (f0, fn_) in enumerate(groups):
            if gidx == 1 and deferred:
                tt_, ts_ = deferred.pop(0)
                ln_gen = layernorm_stages(P, cx, X, [cx.tk_X[c][tt_] for c in range(8)], ts_, lng, lnb, 8,
                                          [tkp[6], tkp[7]], ps[6], ps[7], lntmp, tk_ln)
            if gidx >= 1 and ln_gen is not None:
                if next(ln_gen, "done") == "done":
                    ln_gen = None
            b = gi % 2
            gi += 1
            P.dma("pool", wgu[b][0][:, :, 0:fn_], wg_d[:, f0:f0 + fn_].rearrange("(kc p) f -> p kc f", p=128),
                  writes=[tk_w[b][0]])
            P.dma("pool", wgu[b][1][:, :, 0:fn_], wu_d[:, f0:f0 + fn_].rearrange("(kc p) f -> p kc f", p=128),
                  writes=[tk_w[b][1]])
            if gi == 2:
                for t_ in cx.pending_x:
                    for c in range(8):
                        P.dma("pool", X[:, c, t_ * 512:(t_ + 1) * 512], cx.dr["xT"][:, c, t_ * 512:(t_ + 1) * 512],
                              writes=[cx.tk_X[c][t_]])
                cx.pending_x = []
                load_wd()
            for ci in range(fn_ // 128):
                c = f0 // 128 + ci
                pb = (c % 2) * 2
                for k in range(8):
                    P.mm(ps[pb][:, :], wgu[b][0][:, k, ci * 128:(ci + 1) * 128], xb[:, k, :], k == 0, k == 7,
                         [tk_w[b][0], tk_xb], [tkp[pb]], signal=(k == 7))
                for k in range(8):
                    P.mm(ps[pb + 1][:, :], wgu[b][1][:, k, ci * 128:(ci + 1) * 128], xb[:, k, :], k == 0, k == 7,
                         [tk_w[b][1], tk_xb], [tkp[pb + 1]], signal=(k == 7))
                s = c % 2
                P.act(sg[s], ps[pb][:, :], AF.Silu, [tkp[pb]], [tk_sg[s]])
                P.stt("dve", hT[:, c, :], sg[s], 0.5, ps[pb + 1][:, :], ALU.mult, ALU.mult, [tk_sg[s], tkp[pb + 1]], [tk_h[c]])
        if ln_gen is not None:
            for _ in ln_gen:
                pass
            ln_gen = None
        for d in range(8):
            pb = 4 + d % 2
            for c in range(NFF):
                P.mm(ps[pb][:, :], wd[:, c, d * 128:(d + 1) * 128], hT[:, c, :], c == 0, c == NFF - 1,
                     [tk_wd, tk_h[c]], [tkp[pb]], signal=(c == NFF - 1))
            P.stt("dve", X[:, d, ts], X[:, d, ts], ALPHA, ps[pb][:, :], ALU.mult, ALU.add, [tkp[pb]], [cx.tk_X[d][tt]])
        deferred.append((tt, ts))
    for (tt_, ts_) in deferred:
        emit_layernorm_tile(P, cx, X, [cx.tk_X[c][tt_] for c in range(8)], ts_, lng, lnb, 8,
                            [tkp[6], tkp[7]], ps[6], ps[7], lntmp, tk_ln)


def wload(P, dst, src2d, tok, q="pool"):
    P.dma(q, dst, src2d.rearrange("(kc p) f -> p kc f", p=128), writes=[tok])


def flat(ap3):
    return ap3.rearrange("p a b -> p (a b)")


def phase_mixer(cx, l, upto):
    P, X, ar = cx.P, cx.X, cx.ar
    stage = upto[0] if (upto is not None and upto[1] == l) else None
    P.barrier()
    ar.reset()
    a2 = Arena(X[:, :, :].rearrange("p c t -> p (c t)"), 8 * T)
    xbm = ar.bf16(8, T)
    outs = [ar.bf16(4, T) for _ in range(3)]
    tk_xbm = P.tok("xbm")
    tk_outs = P.toks(3, "bo")
    tk_xs = P.tok("xs")
    mark = ar.off
    for c in range(8):
        if c % 2 == 0:
            P.act(xbm[:, c, :], X[:, c, :], AF.Copy, cx.tk_X[c], [tk_xbm])
        else:
            P.copy("dve", xbm[:, c, :], X[:, c, :], cx.tk_X[c], [tk_xbm])
        P.dma("sp", cx.dr["xs"][:, c, :], X[:, c, :], reads=cx.tk_X[c], writes=[tk_xs])
    P.barrier()

    def dump(idx):
        P.barrier()
        a2.reset()
        tmp = a2.f32(4, T)
        tk = P.tok("dmp")
        for c in range(4):
            P.act(tmp[:, c, :], outs[idx][:, c, :], AF.Copy, [tk_outs[idx]], [tk])
            P.dma("sp", cx.dr["dbg"][:, c, :], tmp[:, c, :], reads=[tk], writes=[cx.tk_dbg])
        P.barrier()

    if stage in (None, "dn", "mix"):
        ar.off = mark
        a2.reset()
        branch_dn(cx, l, xbm, tk_xbm, outs[0], tk_outs[0], ar, a2)
        if stage == "dn":
            dump(0)
    if stage in (None, "cv", "mix"):
        P.barrier()
        ar.off = mark
        a2.reset()
        branch_cv(cx, l, xbm, tk_xbm, outs[1], tk_outs[1], ar, a2)
        if stage == "cv":
            dump(1)
    if stage in (None, "mla", "mix"):
        P.barrier()
        ar.off = mark
        a2.reset()
        branch_mla(cx, l, xbm, tk_xbm, outs[2], tk_outs[2], ar, a2)
        if stage == "mla":
            dump(2)
    P.barrier()
    ar.off = mark
    a2.reset()
    if stage in ("dn", "cv", "mla"):
        for c in range(8):
            P.dma("sp", X[:, c, :], cx.dr["xs"][:, c, :], reads=[tk_xs], writes=cx.tk_X[c])
        return
    merge_phase(cx, l, xbm, tk_xbm, outs, tk_outs, tk_xs, ar, a2)


def branch_dn(cx, l, xbm, tk_xbm, o_dn, tk_odn, a1, a2):
    P, CST, ps, tkp = cx.P, cx.CST, cx.ps, cx.tk_ps
    kc_ = cx.tk_const
    w_in = cx.dr["w_in"][l]
    ID, ONE, NEGONE = CST[:, K_ID, :], CST[:, K_ONE, :], CST[:, K_NEGONE, :]
    TRI, TRIU, MNEG, MNEGT, STRICT = (CST[:, k, :] for k in (K_TRI, K_TRIU, K_MNEG, K_MNEGT, K_STRICT))
    wab = a1.bf16(8, 8)
    wq = a1.bf16(4, 8, 128)
    sm = {n: a1.f32(64) for n in ("xa", "ax", "e", "ln", "sp", "g", "beta", "nbeta", "gc", "egc", "bege", "ekd")}
    egl = a1.f32(2, 64)
    ea = a1.f32(4)
    v3 = lambda a: a.rearrange("p (t h) -> p t h", h=4)
    pres = [a2.f32(T + 4) for _ in range(3)]
    qT = a2.f32(T)
    kT = a2.f32(T)
    vT = a2.f32(T)
    k_tm = a2.bf16(16, 128)
    v_tm = a2.bf16(16, 128)
    oT = pres[1][:, 0:T]
    sq = qT[:, 0:512]
    rin = qT[:, 512:1024]
    qbf, kbf, vbf = a1.bf16(T), a1.bf16(T), a1.bf16(T)
    f32_names = ("GTri", "decay", "decayT", "egrow", "t1", "MTa", "MTb")
    bf_names = ("Nm", "NT", "P0", "P1", "PT0", "PT1", "TT0", "TT1", "vb", "kb", "u", "w", "qkT", "QeffT", "kdec")
    tmp_names = f32_names + bf_names
    tb = {}

    def pick(nw):
        return a1 if (a1.n - a1.off) >= nw else a2
    for n in f32_names:
        tb[n] = pick(512).f32(4, 128)
    for n in bf_names:
        tb[n] = pick(256).bf16(4, 128)
    Sbuf = [pick(128).f32(128) for _ in range(2)]
    Sbf = [pick(64).bf16(128) for _ in range(2)]
    tk = {n: P.tok(n) for n in tmp_names}
    tk_wab, tk_sm, tk_q, tk_k, tk_v, tk_ktm, tk_vtm, tk_sq, tk_rin = (
        P.tok(n) for n in ("wab", "sm", "q", "k", "v", "ktm", "vtm", "sq", "rin"))
    tk_pres = P.toks(3, "pre")
    tk_oT = tk_pres[1]
    tk_pre = tk_pres[0]
    tk_wq = P.toks(4, "wq")
    tk_S = P.toks(2, "S")
    tk_Sb = P.toks(2, "Sb")
    tk_qb, tk_kb, tk_vb = P.tok("qb"), P.tok("kb_"), P.tok("vb_")
    tk_sq = tk_rin = tk_q

    P.dma("pool", wab, w_in[:, C_DA:C_DA + 8].rearrange("(kc p) f -> p kc f", p=128), writes=[tk_wab])
    ab = ps[0][:, 0:128].rearrange("p (t e) -> p t e", e=8)
    for t in range(16):
        for k in range(8):
            P.mm(ab[:, t, :], xbm[:, k, t * 128:(t + 1) * 128], wab[:, k, :], k == 0, k == 7, [tk_xbm, tk_wab], [tkp[0]],
                 signal=(k == 7))
    dtb = cx.col("dn_dt_bias", l)
    alog = cx.col("dn_a_log", l)
    for h in range(4):
        P.ts("dve", v3(sm["xa"])[:, :, h], ab[:, :, h], dtb[:, h:h + 1], None, ALU.add, None, [tkp[0], kc_], [tk_sm])
    P.act(sm["ax"], sm["xa"], AF.Abs, [tk_sm], [tk_sm])
    P.act(sm["e"], sm["ax"], AF.Exp, [tk_sm], [tk_sm], scale=-1.0)
    P.act(sm["ln"], sm["e"], AF.Ln, [tk_sm], [tk_sm], bias=cx.epscol(1.0), scale=1.0)
    P.stt("dve", sm["sp"], sm["xa"], 0.0, sm["ln"], ALU.max, ALU.add, [tk_sm], [tk_sm])
    P.act(ea, alog, AF.Exp, [kc_], [tk_sm])
    for h in range(4):
        P.ts("dve", v3(sm["g"])[:, :, h], v3(sm["sp"])[:, :, h], ea[:, h:h + 1], -1.0, ALU.mult, ALU.mult, [tk_sm], [tk_sm])
    P.act(v3(sm["beta"]), ab[:, :, 4:8], AF.Sigmoid, [tkp[0]], [tk_sm])
    P.ts("dve", sm["nbeta"], sm["beta"], -1.0, None, ALU.mult, None, [tk_sm], [tk_sm])
    P.mm(ps[1][:, 0:64], TRI, sm["g"], True, True, [tk_sm, kc_], [tkp[1]])
    P.mm(ps[1][:, 64:128], TRIU, sm["g"], True, True, [tk_sm, kc_], [tkp[1]])
    P.act(sm["gc"], ps[1][:, 0:64], AF.Copy, [tkp[1]], [tk_sm])
    P.act(sm["egc"], ps[1][:, 0:64], AF.Exp, [tkp[1]], [tk_sm])
    P.act(sm["ekd"], ps[1][:, 64:128], AF.Exp, [tkp[1]], [tk_sm])
    P.tt("dve", sm["bege"], sm["beta"], sm["egc"], ALU.mult, [tk_sm], [tk_sm])
    P.mm(ps[1][:, 128:192], CST[:, K_SEL63, :], sm["gc"], True, True, [tk_sm, kc_], [tkp[1]])
    P.mm(ps[1][:, 192:256], CST[:, K_SEL127, :], sm["gc"], True, True, [tk_sm, kc_], [tkp[1]])
    P.act(flat(egl), ps[1][:, 128:256], AF.Exp, [tkp[1]], [tk_sm])

    cw = cx.col("dn_conv_w", l)
    normw = cx.col("dn_norm_w", l)
    for h in range(4):
        for s_, c0 in enumerate((C_DQ, C_DK, C_DV, C_DZ)):
            wload(P, wq[:, s_], w_in[:, c0 + h * 128:c0 + (h + 1) * 128], tk_wq[s_])
        dsts = ((qT, tk_q), (kT, tk_k), (vT, tk_v))
        for s_ in range(3):
            P.call("dve", "memset", [], [tk_pres[s_]], True, ap=pres[s_][:, 0:3], constant=0.0)
            for tt in range(4):
                ts = slice(tt * 512, (tt + 1) * 512)
                bank = 6 + tt % 2
                for k in range(8):
                    P.mm(ps[bank][:, :], wq[:, s_, k, :], xbm[:, k, ts], k == 0, k == 7, [tk_wq[s_], tk_xbm], [tkp[bank]],
                         signal=(k == 7))
                P.act(pres[s_][:, 3 + tt * 512:3 + (tt + 1) * 512], ps[bank][:, :], AF.Copy, [tkp[bank]], [tk_pres[s_]])
        for s_ in range(3):
            cc = s_ * 4 + h
            dst, tkd = dsts[s_]
            pre = pres[s_]
            P.ts("dve", dst, pre[:, 0:T], cw[:, cc:cc + 1], None, ALU.mult, None, [tk_pres[s_], kc_], [tkd])
            for j in range(1, 4):
                P.stt("dve", dst, pre[:, j:j + T], cw[:, j * 12 + cc:j * 12 + cc + 1], dst, ALU.mult, ALU.add,
                      [tk_pres[s_], kc_], [tkd])
            if s_ < 2:
                P.act(dst, dst, AF.Silu, [], [tkd])
            else:
                P.act(vbf, dst, AF.Silu, [tkd], [tk_vb])
        for s_ in range(2):
            dst, tkd = dsts[s_]
            scr = pres[s_][:, 0:T]
            tks = tk_pres[s_]
            P.act(scr, dst, AF.Square, [tkd], [tks])
            for tt in range(4):
                ts = slice(tt * 512, (tt + 1) * 512)
                P.mm(ps[4 + tt][:, :], ONE, scr[:, ts], True, True, [tks, kc_], [tkp[4 + tt]])
            for tt in range(4):
                ts = slice(tt * 512, (tt + 1) * 512)
                P.act(scr[:, ts], ps[4 + tt][:, :], AF.Sqrt, [tkp[4 + tt], kc_], [tks], bias=cx.epscol(1e-6), scale=1.0)
            P.call("dve", "reciprocal", [tks], [tks], True, out=scr, in_=scr)
            if s_ == 0:
                P.stt("dve", qbf, dst, 128.0 ** -0.5, scr, ALU.mult, ALU.mult, [tks, tkd], [tk_qb])
            else:
                P.tt("dve", kbf, dst, scr, ALU.mult, [tks, tkd], [tk_kb])
        IDb_ = cx.CSTB[:, 0, :]
        for src_, dstm, tks, tkd in ((kbf, k_tm, tk_kb, tk_ktm), (vbf, v_tm, tk_vb, tk_vtm)):
            for g4 in range(4):
                bank = 6 + g4 % 2
                psb_ = ps[bank][:, 0:256].bitcast(BF16)
                for q in range(4):
                    t = g4 * 4 + q
                    P.tr(psb_[:, q * 128:(q + 1) * 128], src_[:, t * 128:(t + 1) * 128], IDb_, [tks, kc_], [tkp[bank]])
                P.act(flat(dstm[:, g4 * 4:(g4 + 1) * 4, :]), psb_, AF.Copy, [tkp[bank]], [tkd])
        P.call("dve", "memset", [], [tk_S[0]], True, ap=Sbuf[0], constant=0.0)
        P.call("dve", "memset", [], [tk_Sb[0]], True, ap=Sbf[0], constant=0.0)
        cur = 0
        IDb = cx.CSTB[:, 0, :]
        bc4 = lambda ap2: ap2.unsqueeze(1).to_broadcast([128, 4, 128])
        for G in range(4):
            gs = slice(G * 512, (G + 1) * 512)
            F = {n: flat(tb[n]) for n in tmp_names}
            ps4b = ps[4][:, 0:256].bitcast(BF16)
            for q in range(4):
                t = G * 4 + q
                gcol = v3(sm["g"])[:, t, h:h + 1]
                P.ts("dve", tb["GTri"][:, q, :], TRI, gcol, None, ALU.mult, None, [tk_sm, kc_], [tk["GTri"]])
            for q in range(4):
                t = G * 4 + q
                tq = slice(q * 128, (q + 1) * 128)
                tcol = slice(t * 128, (t + 1) * 128)
                P.mm(ps[2][:, tq], ONE, tb["GTri"][:, q, :], True, True, [tk["GTri"], kc_], [tkp[2]])
                P.mm(ps[3][:, tq], kbf[:, tcol], kbf[:, tcol], True, True, [tk_kb], [tkp[3]])
            P.act(F["t1"], ps[2][:, :], AF.Copy, [tkp[2]], [tk["t1"]])
            P.act(F["egrow"], F["t1"], AF.Exp, [tk["t1"]], [tk["egrow"]])
            for q in range(4):
                t = G * 4 + q
                gccol = v3(sm["gc"])[:, t, h:h + 1]
                P.act(tb["decay"][:, q, :], tb["t1"][:, q, :], AF.Identity, [tk["t1"], tk_sm], [tk["decay"]], bias=gccol, scale=-1.0)
                P.stt("dve", tb["decayT"][:, q, :], tb["t1"][:, q, :], gccol, MNEGT, ALU.subtract, ALU.min,
                      [tk["t1"], tk_sm, kc_], [tk["decayT"]])
            for q in range(4):
                P.tt("dve", tb["decay"][:, q, :], tb["decay"][:, q, :], MNEG, ALU.min, [kc_], [tk["decay"]])
            P.act(F["decayT"], F["decayT"], AF.Exp, [], [tk["decayT"]])
            P.act(F["decay"], F["decay"], AF.Exp, [], [tk["decay"]])
            P.tt("dve", F["t1"], ps[3][:, :], F["decay"], ALU.mult, [tkp[3], tk["decay"]], [tk["t1"]])
            if DN_STOP <= 1:
                continue
            for q in range(4):
                t = G * 4 + q
                nbcol = v3(sm["nbeta"])[:, t, h:h + 1]
                P.stt("dve", tb["Nm"][:, q, :], tb["t1"][:, q, :], nbcol, STRICT, ALU.mult, ALU.mult,
                      [tk["t1"], tk_sm, kc_], [tk["Nm"]])
            for q in range(4):
                P.tr(ps4b[:, q * 128:(q + 1) * 128], tb["Nm"][:, q, :], IDb, [tk["Nm"], kc_], [tkp[4]])
            P.act(F["NT"], ps4b, AF.Copy, [tkp[4]], [tk["NT"]])
            for q in range(4):
                P.tt("dve", tb["TT0"][:, q, :], tb["NT"][:, q, :], ID, ALU.add, [tk["NT"], kc_], [tk["TT0"]])
            if DN_STOP <= 2:
                continue
            Pc, PTc, TTc = "Nm", "NT", "TT0"
            for s_ in range(1, 6):
                Pn, PTn, TTn = ("P%d" % (s_ % 2)), ("PT%d" % (s_ % 2)), ("TT%d" % (s_ % 2))
                for q in range(4):
                    tq = slice(q * 128, (q + 1) * 128)
                    P.mm(ps[5][:, tq], tb[PTc][:, q, :], tb[Pc][:, q, :], True, True, [tk[PTc], tk[Pc]], [tkp[5]])
                if s_ < 5:
                    for q in range(4):
                        tq = slice(q * 128, (q + 1) * 128)
                        P.mm(ps[6][:, tq], tb[Pc][:, q, :], tb[PTc][:, q, :], True, True, [tk[PTc], tk[Pc]], [tkp[6]])
                P.act(F[Pn], ps[5][:, :], AF.Copy, [tkp[5]], [tk[Pn]])
                if s_ < 5:
                    P.copy("dve", F[PTn], ps[6][:, :], [tkp[6]], [tk[PTn]])
                for q in range(4):
                    tq = slice(q * 128, (q + 1) * 128)
                    P.mm(ps[7][:, tq], tb[Pn][:, q, :], tb[TTc][:, q, :], True, True, [tk[Pn], tk[TTc]], [tkp[7]])
                P.tt("dve", F[TTn], F[TTc], ps[7][:, :], ALU.add, [tk[TTc], tkp[7]], [tk[TTn]])
                Pc, PTc, TTc = Pn, PTn, TTn
            TT = tb[TTc]
            tkTT = tk[TTc]
            if DN_STOP <= 3:
                continue
            for q in range(4):
                t = G * 4 + q
                P.act(tb["vb"][:, q, :], v_tm[:, t, :], AF.Copy, [tk_vtm, tk_sm], [tk["vb"]], scale=v3(sm["beta"])[:, t, h:h + 1])
                P.act(tb["kb"][:, q, :], k_tm[:, t, :], AF.Copy, [tk_ktm, tk_sm], [tk["kb"]], scale=v3(sm["bege"])[:, t, h:h + 1])
                P.act(tb["kdec"][:, q, :], k_tm[:, t, :], AF.Copy, [tk_ktm, tk_sm], [tk["kdec"]], scale=v3(sm["ekd"])[:, t, h:h + 1])
            for q in range(4):
                t = G * 4 + q
                tq = slice(q * 128, (q + 1) * 128)
                tcol = slice(t * 128, (t + 1) * 128)
                P.mm(ps[0][:, tq], TT[:, q, :], tb["vb"][:, q, :], True, True, [tkTT, tk["vb"]], [tkp[0]])
                P.mm(ps[1][:, tq], TT[:, q, :], tb["kb"][:, q, :], True, True, [tkTT, tk["kb"]], [tkp[1]])
                P.mm(ps[2][:, tq], kbf[:, tcol], qbf[:, tcol], True, True, [tk_kb, tk_qb], [tkp[2]])
            P.act(F["u"], ps[0][:, :], AF.Copy, [tkp[0]], [tk["u"]])
            P.act(F["w"], ps[1][:, :], AF.Copy, [tkp[1]], [tk["w"]])
            P.tt("dve", F["qkT"], ps[2][:, :], F["decayT"], ALU.mult, [tkp[2], tk["decayT"]], [tk["qkT"]])
            if DN_STOP <= 4:
                continue
            for q in range(4):
                tq = slice(q * 128, (q + 1) * 128)
                P.mm(ps[3][:, tq], tb["w"][:, q, :], tb["qkT"][:, q, :], True, True, [tk["w"], tk["qkT"]], [tkp[3]])
            for q in range(4):
                tq = slice(q * 128, (q + 1) * 128)
                for ch in range(2):
                    r = slice(ch * 64, ch * 64 + 64)
                    P.mm(ps[4 + ch][:, tq], tb["w"][r, q, :], tb["kdec"][r, q, :], True, True, [tk["w"], tk["kdec"]], [tkp[4 + ch]])
            P.tt("dve", F["t1"], qbf[:, gs], F["egrow"], ALU.mult, [tk_qb, tk["egrow"], tk["Nm"]], [tk["t1"]])
            P.tt("dve", F["QeffT"], F["t1"], ps[3][:, :], ALU.subtract, [tkp[3], tk["t1"]], [tk["QeffT"]])
            for q in range(4):
                t = G * 4 + q
                tq = slice(q * 128, (q + 1) * 128)
                for ch in range(2):
                    eglcol = egl[:, ch, t * 4 + h:t * 4 + h + 1]
                    nm = "MTa" if ch == 0 else "MTb"
                    P.stt("dve", tb[nm][:, q, :], ID, eglcol, ps[4 + ch][:, tq], ALU.mult, ALU.subtract,
                          [tkp[4 + ch], tk_sm, kc_], [tk[nm]])
            if DN_STOP <= 5:
                continue
            for q in range(4):
                for ch in range(2):
                    r = slice(ch * 64, ch * 64 + 64)
                    tc = slice(q * 128 + ch * 64, q * 128 + ch * 64 + 64)
                    nm = "MTa" if ch == 0 else "MTb"
                    S_c, S_n = Sbuf[cur], Sbuf[1 - cur]
                    sb = 6 + cur
                    P.mm(ps[sb][:, 0:128], tb[nm][:, q, :], S_c, True, False, [tk[nm], tk_S[cur]], [tkp[sb]])
                    P.mm(ps[sb][:, 0:128], tb["kdec"][r, q, :], tb["u"][r, q, :], False, True, [tk["kdec"], tk["u"]], [tkp[sb]])
                    P.act(S_n, ps[sb][:, 0:128], AF.Copy, [tkp[sb]], [tk_S[1 - cur]])
                    P.copy("dve", Sbf[1 - cur], S_n, [tk_S[1 - cur]], [tk_Sb[1 - cur]])
                    P.mm(ps[3][:, tc], Sbf[cur], F["QeffT"][:, tc], True, False, [tk_Sb[cur], tk["QeffT"]], [tkp[3]])
                    P.mm(ps[3][:, tc], tb["u"][r, q, :], tb["qkT"][r, q, ch * 64:ch * 64 + 64], False, True,
                         [tk["u"], tk["qkT"]], [tkp[3]])
                    cur = 1 - cur
            P.act(oT[:, gs], ps[3][:, :], AF.Copy, [tkp[3]], [tk_oT])
        zs = pres[0][:, 0:T]
        sqf, rinf = qT, kT
        for tt in range(4):
            ts = slice(tt * 512, (tt + 1) * 512)
            bank = 6 + tt % 2
            for k in range(8):
                P.mm(ps[bank][:, :], wq[:, 3, k, :], xbm[:, k, ts], k == 0, k == 7, [tk_wq[3], tk_xbm], [tkp[bank]],
                     signal=(k == 7))
            P.act(zs[:, ts], ps[bank][:, :], AF.Silu, [tkp[bank]], [tk_pre])
        P.act(sqf, oT, AF.Square, [tk_oT], [tk_q])
        for tt in range(4):
            ts = slice(tt * 512, (tt + 1) * 512)
            P.mm(ps[tt][:, :], CST[:, K_I128, :], sqf[:, ts], True, True, [tk_q, kc_], [tkp[tt]])
        for tt in range(4):
            ts = slice(tt * 512, (tt + 1) * 512)
            P.act(rinf[:, ts], ps[tt][:, :], AF.Sqrt, [tkp[tt], kc_], [tk_k], bias=cx.epscol(EPS), scale=1.0)
        P.call("dve", "reciprocal", [tk_k], [tk_k], True, out=rinf, in_=rinf)
        P.tt("dve", rinf, oT, rinf, ALU.mult, [tk_oT, tk_k], [tk_k])
        P.stt("dve", o_dn[:, h, :], rinf, normw[:, 0:1], zs, ALU.mult, ALU.mult, [tk_k, tk_pre, kc_], [tk_odn])


def branch_cv(cx, l, xbm, tk_xbm, h_cv, tk_hcv, a1, a2):
    P, CST, ps, tkp = cx.P, cx.CST, cx.ps, cx.tk_ps
    kc_ = cx.tk_const
    w_in = cx.dr["w_in"][l]
    PADW = T + 32
    hpad = a2.bf16(4, PADW)
    Dg = a2.bf16(124, 128)
    wa = [a2.bf16(8, 128) for _ in range(2)]
    wg = [a2.bf16(8, 128) for _ in range(2)]
    ta = [a2.f32(512) for _ in range(2)]
    tsg = [a2.f32(512) for _ in range(2)]
    acc = a1.f32(4, T)
    lntmp = a1.f32(4, 512)
    tk_hp = P.toks(4, "hp")
    tk_acc = P.toks(4, "cacc")
    tk_wa, tk_wg = P.toks(2, "wa"), P.toks(2, "wg")
    tk_ta, tk_tsg = P.toks(2, "ta"), P.toks(2, "tsg")
    tk_ln = P.toks(4, "cln")
    tk_dg = P.tok("dg")
    glub = cx.col("cv_glu_b", l)
    dww = cx.col("cv_dw_w", l)
    dwb = cx.col("cv_dw_b", l)
    lng, lnb = cx.col("cv_ln_g", l), cx.col("cv_ln_b", l)
    ID = CST[:, K_ID, :]
    it = 0
    ci = 0
    for cc in range(4):
        s = cc % 2
        wload(P, wa[s], w_in[:, C_GLU + cc * 128:C_GLU + (cc + 1) * 128], tk_wa[s])
        wload(P, wg[s], w_in[:, C_GLU + 512 + cc * 128:C_GLU + 512 + (cc + 1) * 128], tk_wg[s])
        P.call("dve", "memset", [], [tk_hp[cc]], True, ap=hpad[:, cc, 0:30], constant=0.0)
        for tt in range(4):
            ts = slice(tt * 512, (tt + 1) * 512)
            b0 = (it % 2) * 2
            u = it % 2
            it += 1
            for k in range(8):
                P.mm(ps[b0][:, :], wa[s][:, k, :], xbm[:, k, ts], k == 0, k == 7, [tk_wa[s], tk_xbm], [tkp[b0]], signal=(k == 7))
            for k in range(8):
                P.mm(ps[b0 + 1][:, :], wg[s][:, k, :], xbm[:, k, ts], k == 0, k == 7, [tk_wg[s], tk_xbm], [tkp[b0 + 1]],
                     signal=(k == 7))
            P.act(ta[u], ps[b0][:, :], AF.Identity, [tkp[b0], kc_], [tk_ta[u]], bias=glub[:, cc:cc + 1], scale=1.0)
            P.act(tsg[u], ps[b0 + 1][:, :], AF.Sigmoid, [tkp[b0 + 1], kc_], [tk_tsg[u]], bias=glub[:, 4 + cc:5 + cc], scale=1.0)
            P.tt("dve", hpad[:, cc, 30 + tt * 512:30 + (tt + 1) * 512], ta[u], tsg[u], ALU.mult, [tk_ta[u], tk_tsg[u]],
                 [tk_hp[cc]])
        for j in range(31):
            idx = j * 4 + cc
            P.ts("dve", Dg[:, idx, :], ID, dww[:, idx:idx + 1], None, ALU.mult, None, [kc_], [tk_dg])
        for tt in range(4):
            ts = slice(tt * 512, (tt + 1) * 512)
            bank = 4 + ci % 2
            ci += 1
            for j in range(31):
                P.mm(ps[bank][:, :], Dg[:, j * 4 + cc, :], hpad[:, cc, j + tt * 512:j + tt * 512 + 512], j == 0, j == 30,
                     [tk_dg, tk_hp[cc]], [tkp[bank]], signal=(j == 30))
            P.act(acc[:, cc, ts], ps[bank][:, :], AF.Identity, [tkp[bank], kc_], [tk_acc[cc]], bias=dwb[:, cc:cc + 1], scale=1.0)
    for tt in range(4):
        ts = slice(tt * 512, (tt + 1) * 512)

        def out_fn(c, t_ap, tk_t, ts=ts):
            P.act(h_cv[:, c, ts], t_ap, AF.Silu, [tk_t, kc_], [tk_hcv], bias=lnb[:, c:c + 1], scale=lng[:, c:c + 1])
        emit_layernorm_tile(P, cx, acc, tk_acc, ts, lng, lnb, 4, [tkp[6], tkp[7]], ps[6], ps[7], lntmp, tk_ln, out_fn)


def branch_mla(cx, l, xbm, tk_xbm, o_mla, tk_omla, a1, a2):
    P, CST, CSTB, ps, tkp = cx.P, cx.CST, cx.CSTB, cx.ps, cx.tk_ps
    kc_ = cx.tk_const
    w_in, w_uq, w_ukv = cx.dr["w_in"][l], cx.dr["mla_w_uq"][l], cx.dr["mla_w_ukv"][l]
    IDb, AMb, ONEb = CSTB[:, 0, :], CSTB[:, 1, :], CSTB[:, 2, :]
    scale = 192.0 ** -0.5
    cqn = a2.bf16(3, T)
    ckvn = a2.bf16(2, T)
    CS = a2.f32(T)
    SS = a2.f32(T)
    krT = a2.bf16(T)
    Vaug = a2.bf16(64, 130)
    qnT, qrT, knT = a1.bf16(T), a1.bf16(T), a1.bf16(T)
    PT = a1.bf16(16, 512)
    wc = [a1.bf16(8, 128) for _ in range(2)]
    wuqn, wuqA, wuqB = a1.bf16(3, 128), a1.bf16(3, 64), a1.bf16(3, 64)
    wukv = a1.bf16(2, 1024)
    wkrA, wkrB = a1.bf16(8, 64), a1.bf16(8, 64)
    tmp = a1.f32(3, 512)
    sq, rstd = a1.f32(512), a1.f32(512)
    o_n = a2.bf16(128)
    o_n2 = [o_n, rstd.bitcast(BF16)[:, 0:128]]
    smx = a2.f32(16)
    posi = tmp[:, 0, :].bitcast(I32)
    t1, t2 = tmp[:, 1, :], tmp[:, 2, :]
    (tk_cqn, tk_ckvn, tk_cs, tk_kr, tk_v, tk_qn, tk_qr, tk_kn, tk_tmp, tk_sq, tk_rstd, tk_on, tk_smx, tk_wuq,
     tk_wukv, tk_wkr, tk_t1, tk_t2, tk_pos) = (P.tok(n) for n in (
         "cqn", "ckvn", "cs", "kr", "v", "qn", "qr", "kn", "tmp", "sq", "rstd", "on", "smx", "wuq", "wukv", "wkr",
         "t1", "t2", "pos"))
    tk_t1 = tk_t2 = tk_pos = tk_tmp
    tk_wc = P.toks(2, "wc")
    tk_PT = P.toks(16, "PT")
    tk_on2 = [tk_on, tk_rstd]
    tk_sqs = [P.tok("sqs0"), P.tok("sqs1"), P.tok("sqs2"), P.tok("sqs3")]
    qw, kvw = cx.col("mla_q_norm_w", l), cx.col("mla_kv_norm_w", l)
    invf, sgn = CST[:, K_MISC, 0:1], CST[:, K_MISC, 1:2]

    wload(P, wukv, w_ukv[:, :], tk_wukv)
    wload(P, wkrA, w_in[:, C_KR:C_KR + 64], tk_wkr)
    P.dma("pool", wkrB[:, :, 0:32], w_in[:, C_KR + 32:C_KR + 64].rearrange("(kc p) f -> p kc f", p=128), writes=[tk_wkr])
    P.dma("pool", wkrB[:, :, 32:64], w_in[:, C_KR:C_KR + 32].rearrange("(kc p) f -> p kc f", p=128), writes=[tk_wkr])
    P.call("dve", "memset", [], [tk_v], True, ap=Vaug[:, :, 128:130], constant=1.0)

    PTf = flat(PT).bitcast(F32)
    kvtmp = PTf[:, 0:1024].rearrange("p (a b) -> p a b", a=2)
    sq2, rstd2 = PTf[:, 1024:1536], PTf[:, 1536:2048]
    rs = [PTf[:, 2048 + i_ * 512:2048 + (i_ + 1) * 512] for i_ in range(4)]
    tk_kvtmp, tk_sq2, tk_rstd2 = P.tok("kvtmp"), P.tok("sq2"), P.tok("rstd2")
    tk_rs = P.tok("rs")
    def rope_tile(tt):
        ts = slice(tt * 512, (tt + 1) * 512)
        pi_, y_, A_, B_ = cx.POSI[:, :], rs[1][0:64, :], rs[2][0:64, :], rs[3][0:64, :]
        P.dma("sp", pi_, cx.dr["pos"][:, ts].partition_broadcast(64), reads=[tk_rs], writes=[tk_pos])
        P.copy("dve", y_, pi_, [tk_pos], [tk_rs])
        P.ts("dve", y_, y_, invf[0:64, :], None, ALU.mult, None, [kc_], [tk_rs])
        P.ts("dve", y_, y_, float(1.0 / (2 * np.pi)), None, ALU.mult, None, [], [tk_rs])
        P.copy("dve", pi_, y_, [tk_pos], [tk_rs])
        P.copy("dve", A_, pi_, [], [tk_rs])
        P.tt("dve", y_, y_, A_, ALU.subtract, [], [tk_rs])
        for (dstT, shift) in ((SS, 0.0), (CS, 0.25)):
            if shift != 0.0:
                P.ts("dve", y_, y_, shift, None, ALU.add, None, [], [tk_rs])
            P.ts("dve", A_, y_, 0.5, None, ALU.is_gt, None, [], [tk_rs])
            P.tt("dve", B_, y_, A_, ALU.subtract, [], [tk_rs])
            P.ts("dve", A_, y_, -0.5, None, ALU.is_lt, None, [], [tk_rs])
            P.tt("dve", B_, B_, A_, ALU.add, [], [tk_rs])
            P.act(dstT[0:64, ts], B_, AF.Sin, [tk_rs], [tk_cs], scale=float(2 * np.pi))
        P.ts("dve", SS[0:64, ts], SS[0:64, ts], sgn[0:64, :], None, ALU.mult, None, [kc_], [tk_cs])
    wi = 0
    for tt in range(4):
        ts = slice(tt * 512, (tt + 1) * 512)
        rope_tile(tt)
        for (c0, nch, dstn, tkd, wcol, inv, tbuf, tkt, sq_, tksq, rstd_, tkr, pb) in (
                (C_CQ, 3, cqn, tk_cqn, qw, CST[:, K_I384, :], tmp, tk_tmp, sq, tk_sq, rstd, tk_rstd, 2),
                (C_CKV, 2, ckvn, tk_ckvn, kvw, CST[:, K_I256, :], kvtmp, tk_kvtmp, sq2, tk_sq2, rstd2, tk_rstd2, 5)):
            for ch in range(nch):
                s = wi % 2
                wi += 1
                wload(P, wc[s], w_in[:, c0 + ch * 128:c0 + (ch + 1) * 128], tk_wc[s])
                for k in range(8):
                    P.mm(ps[s][:, :], wc[s][:, k, :], xbm[:, k, ts], k == 0, k == 7, [tk_wc[s], tk_xbm], [tkp[s]], signal=(k == 7))
                P.act(tbuf[:, ch, :], ps[s][:, :], AF.Copy, [tkp[s]], [tkt])
            for ch in range(nch):
                P.act(sq_, tbuf[:, ch, :], AF.Square, [tkt], [tksq])
                P.mm(ps[pb][:, :], inv, sq_, ch == 0, ch == nch - 1, [tksq, kc_], [tkp[pb]])
            P.act(rstd_, ps[pb][:, :], AF.Sqrt, [tkp[pb], kc_], [tkr], bias=cx.epscol(EPS), scale=1.0)
            P.call("dve", "reciprocal", [tkr], [tkr], True, out=rstd_, in_=rstd_)
            for ch in range(nch):
                P.tt("dve", tbuf[:, ch, :], tbuf[:, ch, :], rstd_, ALU.mult, [tkr], [tkt])
                P.act(dstn[:, ch, ts], tbuf[:, ch, :], AF.Copy, [tkt, kc_], [tkd], scale=wcol[:, ch:ch + 1])
    for tt in range(4):
        ts = slice(tt * 512, (tt + 1) * 512)
        for k in range(8):
            P.mm(ps[3][0:64, :], wkrA[:, k, :], xbm[:, k, ts], k == 0, k == 7, [tk_wkr, tk_xbm], [tkp[3]], signal=(k == 7))
        for k in range(8):
            P.mm(ps[4][0:64, :], wkrB[:, k, :], xbm[:, k, ts], k == 0, k == 7, [tk_wkr, tk_xbm], [tkp[4]], signal=(k == 7))
        P.tt("dve", rs[2][0:64, :], ps[3][0:64, :], CS[0:64, ts], ALU.mult, [tkp[3], tk_cs], [tk_rs])
        P.tt("dve", rs[3][0:64, :], ps[4][0:64, :], SS[0:64, ts], ALU.mult, [tkp[4], tk_cs], [tk_rs])
        P.tt("dve", krT[0:64, ts], rs[2][0:64, :], rs[3][0:64, :], ALU.add, [tk_rs], [tk_kr])
    for t in range(16):
        pb = 6 + t % 2
        for kc in range(2):
            P.mm(ps[pb][:, :].rearrange("p (h d) -> p h d", h=4), ckvn[:, kc, t * 128:(t + 1) * 128],
                 wukv[:, kc, :].rearrange("p (h e) -> p h e", h=4)[:, :, 128:256], kc == 0, kc == 1,
                 [tk_ckvn, tk_wukv], [tkp[pb]])
        vv = Vaug.rearrange("p (h t) d -> p h t d", h=4)[:, :, t, 0:128]
        P.act(vv, ps[pb][:, :].rearrange("p (h d) -> p h d", h=4), AF.Copy, [tkp[pb]], [tk_v])
    tk_dummy = P.tok("dummy")
    P.call("dve", "memset", [tk_kvtmp, tk_sq2, tk_rstd2, tk_rs], list(tk_PT) + [tk_dummy], True, ap=smx[:, 15:16], constant=0.0)
    P.call("dve", "memset", [tk_tmp, tk_sq], tk_sqs + [tk_dummy], True, ap=smx[:, 15:16], constant=0.0)

    for h in range(4):
        P.dma("pool", wuqn, w_uq[:, h * 192:h * 192 + 128].rearrange("(kc p) f -> p kc f", p=128), writes=[tk_wuq])
        P.dma("pool", wuqA, w_uq[:, h * 192 + 128:h * 192 + 192].rearrange("(kc p) f -> p kc f", p=128), writes=[tk_wuq])
        P.dma("pool", wuqB[:, :, 0:32], w_uq[:, h * 192 + 160:h * 192 + 192].rearrange("(kc p) f -> p kc f", p=128),
              writes=[tk_wuq])
        P.dma("pool", wuqB[:, :, 32:64], w_uq[:, h * 192 + 128:h * 192 + 160].rearrange("(kc p) f -> p kc f", p=128),
              writes=[tk_wuq])
        for tt in range(4):
            ts = slice(tt * 512, (tt + 1) * 512)
            for kc in range(3):
                P.mm(ps[0][:, :], wuqn[:, kc, :], cqn[:, kc, ts], kc == 0, kc == 2, [tk_wuq, tk_cqn], [tkp[0]])
            P.act(qnT[:, ts], ps[0][:, :], AF.Copy, [tkp[0]], [tk_qn])
            for kc in range(3):
                P.mm(ps[3][0:64, :], wuqA[:, kc, :], cqn[:, kc, ts], kc == 0, kc == 2, [tk_wuq, tk_cqn], [tkp[3]])
            for kc in range(3):
                P.mm(ps[4][0:64, :], wuqB[:, kc, :], cqn[:, kc, ts], kc == 0, kc == 2, [tk_wuq, tk_cqn], [tkp[4]])
            P.tt("dve", t1[0:64, :], ps[3][0:64, :], CS[0:64, ts], ALU.mult, [tkp[3], tk_cs], [tk_t1])
            P.tt("dve", t2[0:64, :], ps[4][0:64, :], SS[0:64, ts], ALU.mult, [tkp[4], tk_cs], [tk_t2])
            P.tt("dve", qrT[0:64, ts], t1[0:64, :], t2[0:64, :], ALU.add, [tk_t1, tk_t2], [tk_qr])
            for kc in range(2):
                P.mm(ps[1][:, :], wukv[:, kc, h * 256:h * 256 + 128], ckvn[:, kc, ts], kc == 0, kc == 1,
                     [tk_wukv, tk_ckvn], [tkp[1]])
            P.act(knT[:, ts], ps[1][:, :], AF.Copy, [tkp[1]], [tk_kn])
        sbufs = (tmp[:, 0, :].bitcast(BF16), sq.bitcast(BF16))
        for tt in range(4):
            ts = slice(tt * 512, (tt + 1) * 512)
            for bi, (nT, rT, tkn, tkr, col) in enumerate(((qnT, qrT, tk_qn, tk_qr, tt), (knT, krT, tk_kn, tk_kr, 4 + tt))):
                s1 = sbufs[bi][:, 0:512]
                s2 = sbufs[bi][:, 512:1024]
                pb = 2 if bi == 0 else 7
                P.act(s1, nT[:, ts], AF.Square, [tkn], [tk_sqs[2 * bi]])
                P.act(s2[0:64, :], rT[0:64, ts], AF.Square, [tkr], [tk_sqs[2 * bi + 1]])
                P.mm(ps[pb][:, :], ONEb, s1, True, False, [tk_sqs[2 * bi], kc_], [tkp[pb]])
                P.mm(ps[pb][:, :], ONEb[0:64, :], s2[0:64, :], False, True, [tk_sqs[2 * bi + 1], kc_], [tkp[pb]])
                P.call("dve", "tensor_reduce", [tkp[pb]], [tk_smx], True, out=smx[:, col:col + 1], in_=ps[pb][:, :],
                       axis=AX.X, op=ALU.max)
        P.call("dve", "tensor_reduce", [tk_smx], [tk_smx], True, out=smx[:, 8:9], in_=smx[:, 0:4], axis=AX.X, op=ALU.max)
        P.call("dve", "tensor_reduce", [tk_smx], [tk_smx], True, out=smx[:, 9:10], in_=smx[:, 4:8], axis=AX.X, op=ALU.max)
        P.tt("dve", smx[:, 10:11], smx[:, 8:9], smx[:, 9:10], ALU.mult, [tk_smx], [tk_smx])
        P.act(smx[:, 11:12], smx[:, 10:11], AF.Sqrt, [tk_smx], [tk_smx])
        P.ts("dve", smx[:, 12:13], smx[:, 11:12], -1.05 * scale, None, ALU.mult, None, [tk_smx], [tk_smx])
        negm = smx[:, 12:13]
        for G in range(4):
            for j in range(4 * G + 4):
                qs = max(j * 128, G * 512)
                n = (G + 1) * 512 - qs
                off = qs - G * 512
                bank = j % 4
                diag = j >= 4 * G
                kt = slice(j * 128, (j + 1) * 128)
                P.mm(ps[bank][:, 0:n], knT[:, kt], qnT[:, qs:qs + n], True, False, [tk_kn, tk_qn], [tkp[bank]])
                P.mm(ps[bank][:, 0:n], krT[0:64, kt], qrT[0:64, qs:qs + n], False, not diag, [tk_kr, tk_qr], [tkp[bank]])
                if diag:
                    P.mm(ps[bank][:, 0:128], IDb, AMb, False, True, [kc_], [tkp[bank]])
                P.act(PT[:, j, off:off + n], ps[bank][:, 0:n], AF.Exp, [tkp[bank], tk_smx], [tk_PT[j]], bias=negm, scale=scale)
            def epilogue(qb):
                i = 4 * G + qb
                bank = 4 + qb % 2
                rc = smx[:, 13 + qb % 2:14 + qb % 2]
                on = o_n2[qb % 2]
                P.call("dve", "reciprocal", [tkp[bank]], [tk_smx], True, out=rc, in_=ps[bank][:, 128:129])
                P.ts("dve", on, ps[bank][:, 0:128], rc, None, ALU.mult, None, [tkp[bank], tk_smx], [tk_on2[qb % 2]])
                return i, on

            def transpose_out(i, on, qb):
                psb = ps[6 + qb % 2][:, 0:64].bitcast(BF16)
                P.tr(psb, on, IDb, [tk_on2[qb % 2], kc_], [tkp[6 + qb % 2]])
                P.act(o_mla[:, h, i * 128:(i + 1) * 128], psb, AF.Copy, [tkp[6 + qb % 2]], [tk_omla])

            pend = None
            for qb in range(4):
                i = 4 * G + qb
                bank = 4 + qb % 2
                for j in range(i + 1):
                    P.mm(ps[bank][:, 0:129], PT[:, j, qb * 128:(qb + 1) * 128], Vaug[:, h * 16 + j, 0:129], j == 0, j == i,
                         [tk_PT[j], tk_v], [tkp[bank]], signal=(j == i))
                if pend is not None:
                    transpose_out(*pend)
                i_, on_ = epilogue(qb)
                pend = (i_, on_, qb)
            transpose_out(*pend)


def merge_phase(cx, l, xbm, tk_xbm, outs, tk_outs, tk_xs, a1, a2):
    P, X, ps, tkp = cx.P, cx.X, cx.ps, cx.tk_ps
    kc_ = cx.tk_const
    w_in = cx.dr["w_in"][l]
    wsrc = (cx.dr["dn_w_o"][l], cx.dr["cv_w_pw2"][l], cx.dr["mla_w_o"][l])
    merged = a1.bf16(8, T)
    mark1 = a1.off
    wgate = [a2.bf16(3, 8, 128) for _ in range(2)]
    wbo = [a2.bf16(3, 4, 128) for _ in range(2)]
    gt = [a2.f32(512) for _ in range(3)]
    macc, tmpm = a2.f32(512), a2.f32(512)
    tk_wg, tk_wb = P.toks(2, "mwg"), P.toks(2, "mwb")
    tk_gt = P.toks(3, "gt")
    tk_macc, tk_tmpm, tk_mg = P.tok("macc"), P.tok("tmpm"), P.tok("merged")
    bg, bpw = cx.col("b_gate", l), cx.col("cv_b_pw2", l)
    for dc in range(8):
        s = dc % 2
        for i in range(3):
            c0 = C_GATE + i * 1024 + dc * 128
            wload(P, wgate[s][:, i], w_in[:, c0:c0 + 128], tk_wg[s])
            wload(P, wbo[s][:, i], wsrc[i][:, dc * 128:(dc + 1) * 128], tk_wb[s])
        for tt in range(4):
            ts = slice(tt * 512, (tt + 1) * 512)
            for i in range(3):
                for k in range(8):
                    P.mm(ps[i][:, :], wgate[s][:, i, k, :], xbm[:, k, ts], k == 0, k == 7, [tk_wg[s], tk_xbm], [tkp[i]],
                         signal=(k == 7))
                for k in range(4):
                    P.mm(ps[3 + i][:, :], wbo[s][:, i, k, :], outs[i][:, k, ts], k == 0, k == 3, [tk_wb[s], tk_outs[i]],
                         [tkp[3 + i]], signal=(k == 3))
            for i in range(3):
                P.act(gt[i], ps[i][:, :], AF.Sigmoid, [tkp[i], kc_], [tk_gt[i]], bias=bg[:, i * 8 + dc:i * 8 + dc + 1], scale=1.0)
            P.tt("dve", macc, gt[0], ps[3][:, :], ALU.mult, [tk_gt[0], tkp[3]], [tk_macc])
            P.stt("dve", tmpm, ps[4][:, :], bpw[:, dc:dc + 1], gt[1], ALU.add, ALU.mult, [tkp[4], tk_gt[1], kc_], [tk_tmpm])
            P.tt("dve", macc, macc, tmpm, ALU.add, [tk_tmpm], [tk_macc])
            P.tt("dve", tmpm, gt[2], ps[5][:, :], ALU.mult, [tk_gt[2], tkp[5]], [tk_tmpm])
            P.tt("dve", merged[:, dc, ts], macc, tmpm, ALU.add, [tk_macc, tk_tmpm], [tk_mg])
    P.barrier()
    for tt in range(4):
        ts = slice(tt * 512, (tt + 1) * 512)
        for c in range(8):
            P.dma("sp", X[:, c, ts], cx.dr["xs"][:, c, ts], reads=[tk_xs], writes=[cx.tk_X[c][tt]])
    wo = flat(xbm)[:, 0:8 * 1024].rearrange("p (k d) -> p k d", k=8)
    lntmp = a1.f32(4, 512)
    tk_wo = P.tok("wo")
    tk_ln = P.toks(4, "mln")
    w_out = cx.dr["w_out"][l]
    for d0 in range(0, 1024, 256):
        P.dma("pool", wo[:, :, d0:d0 + 256], w_out[:, d0:d0 + 256].rearrange("(kc p) f -> p kc f", p=128), writes=[tk_wo])
    lng, lnb = cx.col("ln2_g", l), cx.col("ln2_b", l)
    it = 0
    pending = None
    ln_gen = None
    for tt in range(4):
        ts = slice(tt * 512, (tt + 1) * 512)
        for dc in range(8):
            bank = it % 4
            it += 1
            for k in range(8):
                P.mm(ps[bank][:, :], wo[:, k, dc * 128:(dc + 1) * 128], merged[:, k, ts], k == 0, k == 7, [tk_wo, tk_mg],
                     [tkp[bank]], signal=(k == 7))
            P.stt("dve", X[:, dc, ts], X[:, dc, ts], ALPHA, ps[bank][:, :], ALU.mult, ALU.add, [tkp[bank]], [cx.tk_X[dc][tt]])
            if dc == 1 and pending is not None:
                ptt, pts = pending
                ln_gen = layernorm_stages(P, cx, X, [cx.tk_X[c][ptt] for c in range(8)], pts, lng, lnb, 8,
                                          [tkp[6], tkp[7]], ps[6], ps[7], lntmp, tk_ln)
                pending = None
            if dc >= 1 and dc % 2 == 1 and ln_gen is not None:
                if next(ln_gen, "done") == "done":
                    ln_gen = None
        if ln_gen is not None:
            for _ in ln_gen:
                pass
            ln_gen = None
        pending = (tt, ts)
    ptt, pts = pending
    emit_layernorm_tile(P, cx, X, [cx.tk_X[c][ptt] for c in range(8)], pts, lng, lnb, 8, [tkp[6], tkp[7]], ps[6], ps[7],
                        lntmp, tk_ln)


def emit_rsqrt(P, cx, out, in_, eps, reads, writes):
    P.act(out, in_, AF.Sqrt, reads, writes, bias=cx.epscol(eps), scale=1.0)
    P.call("dve", "reciprocal", writes, writes, True, out=out, in_=out)


def layernorm_stages(P, cx, Xt, xtoks, tcols, g_ap, b_ap, nch, ps_toks, ps_s1, ps_s2, tmp, tmp_toks, out_fn=None):
    n = tcols.stop - tcols.start
    inv = cx.ones_inv[nch]
    sq, mean, rstd, tt = (tmp[:, i, 0:n] for i in range(4))
    tk_sq, tk_mean, tk_rstd, tk_t = tmp_toks
    for c in range(nch):
        P.act(sq, Xt[:, c, tcols], AF.Square, [xtoks[c]], [tk_sq])
        P.mm(ps_s1[:, 0:n], inv, Xt[:, c, tcols], c == 0, c == nch - 1, [xtoks[c], cx.tk_const], [ps_toks[0]])
        P.mm(ps_s2[:, 0:n], inv, sq, c == 0, c == nch - 1, [tk_sq, cx.tk_const], [ps_toks[1]])
    yield
    P.act(mean, ps_s1[:, 0:n], AF.Copy, [ps_toks[0]], [tk_mean])
    P.tt("dve", rstd, mean, mean, ALU.mult, [tk_mean], [tk_rstd])
    P.tt("dve", rstd, ps_s2[:, 0:n], rstd, ALU.subtract, [ps_toks[1], tk_rstd], [tk_rstd])
    emit_rsqrt(P, cx, rstd, rstd, EPS, [tk_rstd], [tk_rstd])
    yield
    for c in range(nch):
        if c == nch // 2:
            yield
        P.tt("dve", tt, Xt[:, c, tcols], mean, ALU.subtract, [xtoks[c], tk_mean], [tk_t])
        P.tt("dve", tt, tt, rstd, ALU.mult, [tk_t, tk_rstd], [tk_t])
        if out_fn is None:
            P.act(Xt[:, c, tcols], tt, AF.Identity, [tk_t, cx.tk_const], [xtoks[c]],
                  bias=b_ap[:, c:c + 1], scale=g_ap[:, c:c + 1])
        else:
            out_fn(c, tt, tk_t)


def emit_layernorm_tile(*args, **kw):
    for _ in layernorm_stages(*args, **kw):
        pass


_CACHE = {}


def make_in_maps(inputs):
    consts = make_consts()
    cols = make_cols(inputs).build()
    x = np.asarray(inputs["x"], np.float32)
    pos = np.asarray(inputs["positions"], np.int32)
    shared = {"consts": consts, "cols": cols}
    for k in WEIGHTS:
        shared[k] = np.ascontiguousarray(np.asarray(inputs[k], np.float32))
    maps = []
    for b in range(8):
        m = dict(shared)
        m["xT"] = np.ascontiguousarray(x[b].reshape(T, 8, 128).transpose(2, 1, 0))
        m["pos"] = np.ascontiguousarray(pos[b].reshape(1, T))
        maps.append(m)
    return maps


def from_fm(a):
    return np.ascontiguousarray(a.transpose(2, 1, 0).reshape(T, D))


def kernel(**inputs):
    if "nc" not in _CACHE:
        _CACHE["nc"] = build_program()
    nc = _CACHE["nc"]
    maps = make_in_maps(inputs)
    res = run_bass_kernel_spmd(nc, maps, core_ids=list(range(8)))
    out = np.stack([from_fm(np.asarray(r["outT"])) for r in res.results], axis=0)
    return out.astype(np.float32)
```

```python
from contextlib import ExitStack
DN_STOP = 9
import numpy as np
import concourse.bass as bass
import concourse.mybir as mybir
from concourse.bass_utils import run_bass_kernel_spmd

F32 = mybir.dt.float32
BF16 = mybir.dt.bfloat16
I32 = mybir.dt.int32
AF = mybir.ActivationFunctionType
ALU = mybir.AluOpType
AX = mybir.AxisListType

D = 1024
T = 2048
DEPTH = 2
DFF = 2816
NFF = DFF // 128
DIN = 6856
ALPHA = (2 * DEPTH) ** 0.25
EPS = 1e-5
NEG = -30000.0

C_DQ, C_DK, C_DV, C_DZ, C_DA, C_DB = 0, 512, 1024, 1536, 2048, 2052
C_GLU, C_CQ, C_CKV, C_KR, C_GATE = 2056, 3080, 3464, 3720, 3784


class Tok:
    __slots__ = ("name", "lw", "rs", "sem", "semv", "excl")

    def __init__(self, name="", excl=False):
        self.name = name
        self.excl = excl
        self.lw = None
        self.rs = []
        self.sem = None
        self.semv = 0


ENGS = ("pe", "act", "dve", "pool", "sp")


class Prog:
    def __init__(self, nc, stack):
        self.nc = nc
        self.stack = stack
        self.ops = {e: [] for e in ENGS}
        self.cnt = {e: 0 for e in ENGS}
        self.sem = {e: stack.enter_context(nc.semaphore("s_" + e)) for e in ENGS}
        self.known = {e: {} for e in ENGS}
        self.semobj = {}
        for e in ENGS:
            self.semobj[e] = self.sem[e]
        self.free_dma_sems = []
        self.used_dma_sems = []
        self.gen = 0
        self.ndma = 0
        self.bar = stack.enter_context(nc.semaphore("s_bar"))
        self.barv = 0
        self.out_waits = []
        self._semv = {}

    def tok(self, name=""):
        return Tok(name)

    def toks(self, n, name=""):
        return [Tok(name + str(i)) for i in range(n)]

    def _deps(self, eng, reads, writes):
        deps = {}

        def add(d):
            if d is None:
                return
            k, v = d
            if deps.get(k, 0) < v:
                deps[k] = v

        for b in reads:
            add(b.lw)
            if b.excl:
                for r in b.rs:
                    if r[0] != eng:
                        add(r)
        for b in writes:
            add(b.lw)
            for r in b.rs:
                add(r)
        out = []
        kn = self.known[eng]
        for k, v in deps.items():
            if k == eng and eng == "pe":
                continue
            if kn.get(k, 0) >= v:
                continue
            kn[k] = v
            out.append((k, v))
        return out

    def call(self, eng, method, reads, writes, signal=True, **kw):
        kw = {k: v for k, v in kw.items() if v is not None}
        return self.op(eng, lambda e, m=method, kw=kw: getattr(e, m)(**kw), reads, writes, signal)

    def mm(self, out, lhsT, rhs, start, stop, reads, writes, signal=True):
        return self.call("pe", "matmul", reads, writes, signal, out=out, lhsT=lhsT, rhs=rhs, start=start, stop=stop)

    def tr(self, out, in_, identity, reads, writes):
        return self.call("pe", "transpose", reads, writes, True, out=out, in_=in_, identity=identity)

    def act(self, out, in_, func, reads, writes, bias=None, scale=None, accum_out=None, eng="act"):
        return self.call(eng, "activation", reads, writes, True, out=out, in_=in_, func=func, bias=bias, scale=scale,
                         accum_out=accum_out)

    def tt(self, eng, out, in0, in1, op, reads, writes):
        return self.call(eng, "tensor_tensor", reads, writes, True, out=out, in0=in0, in1=in1, op=op)

    def ts(self, eng, out, in0, s1, s2, op0, op1, reads, writes, accum_out=None):
        kw = dict(out=out, in0=in0, scalar1=s1, scalar2=s2, op0=op0)
        if op1 is not None:
            kw["op1"] = op1
        if accum_out is not None:
            kw["accum_out"] = accum_out
        return self.op(eng, lambda e, kw=kw: e.tensor_scalar(**kw), reads, writes, True)

    def stt(self, eng, out, in0, scalar, in1, op0, op1, reads, writes):
        return self.call(eng, "scalar_tensor_tensor", reads, writes, True, out=out, in0=in0, scalar=scalar, in1=in1,
                         op0=op0, op1=op1)

    def copy(self, eng, out, in_, reads, writes):
        return self.call(eng, "tensor_copy", reads, writes, True, out=out, in_=in_)

    def op(self, eng, fn, reads=(), writes=(), signal=True):
        waits = self._deps(eng, reads, writes)
        if signal:
            self.cnt[eng] += 1
            me = (eng, self.cnt[eng])
        else:
            me = (eng, self.cnt[eng] + 1)
        self.ops[eng].append((waits, fn, signal, None))
        for b in writes:
            b.lw = me
            b.rs = []
        for b in reads:
            if b not in writes:
                b.rs.append(me)
        return me

    def dma(self, q, out, in_, reads=(), writes=()):
        wt = writes[0]
        if wt.sem is None or wt.sem[1] != self.gen:
            free = [k for k in self.free_dma_sems if k.startswith(q)]
            if free:
                key = free[-1]
                self.free_dma_sems.remove(key)
            else:
                key = "%s_d%d" % (q, self.ndma)
                self.ndma += 1
                self.semobj[key] = self.stack.enter_context(self.nc.semaphore("s_" + key))
            wt.sem = (key, self.gen)
            wt.semv = self._semv.get(key, 0)
            self.used_dma_sems.append(key)
        assert wt.sem[0].startswith(q), "a token's DMAs must stay on one queue between barriers"
        waits = []
        deps = {}
        for b in reads:
            if b.lw is not None:
                deps[b.lw[0]] = max(deps.get(b.lw[0], 0), b.lw[1])
        for b in writes:
            if b.lw is not None and (b.sem is None or b.lw[0] != b.sem[0]):
                deps[b.lw[0]] = max(deps.get(b.lw[0], 0), b.lw[1])
            for r in b.rs:
                deps[r[0]] = max(deps.get(r[0], 0), r[1])
        kn = self.known[q]
        for k, v in deps.items():
            if kn.get(k, 0) >= v:
                continue
            kn[k] = v
            waits.append((k, v))
        wt.semv += 16
        me = (wt.sem[0], wt.semv)
        self._semv[wt.sem[0]] = wt.semv
        self.ops[q].append((waits, lambda e, o=out, i=in_: e.dma_start(out=o, in_=i), False, me))
        for b in writes:
            if b.lw is None or b.sem is None or b.lw[0] != b.sem[0]:
                b.rs = []
            b.lw = me
        for b in reads:
            b.rs.append(me)
        return me

    def barrier(self):
        self.barv += 1
        target = self.barv * len(ENGS)
        drain = [(k, self._semv[k]) for k in self._semv]
        for e in ENGS:
            own = [(e, self.cnt[e])] if (e != "sp" and self.cnt[e] > 0) else []
            self.ops[e].append(("bar", target, own + (drain if e == "sp" else []), None))
        for e in ENGS:
            for e2 in ENGS:
                self.known[e][e2] = self.cnt[e2]
            for k, v in drain:
                self.known[e][k] = v
        self.gen += 1
        self.free_dma_sems.extend(self.used_dma_sems)
        self.used_dma_sems = []

    def flush(self, final_waits=()):
        nc = self.nc
        ops = self.ops
        semobj = self.semobj
        sems = self.sem
        bar = self.bar

        def replay(eng_name):
            def body(e):
                for waits, fn, signal, dmasem in ops[eng_name]:
                    if waits == "bar":
                        for k, v in signal:
                            e.wait_ge(semobj[k], v)
                        e.sem_inc(bar, 1)
                        e.wait_ge(bar, fn)
                        continue
                    for k, v in waits:
                        e.wait_ge(semobj[k], v)
                    ins = fn(e)
                    if dmasem is not None:
                        ins.then_inc(semobj[dmasem[0]], 16)
                    elif signal:
                        ins.then_inc(sems[eng_name], 1)
                if eng_name == "sp":
                    for k, v in final_waits:
                        e.wait_ge(semobj[k], v)
            return body

        with nc.Block() as block:
            block.tensor(replay("pe"))
            block.scalar(replay("act"))
            block.vector(replay("dve"))
            block.gpsimd(replay("pool"))
            block.sync(replay("sp"))
        self.ops = {e: [] for e in ENGS}


def col_chunks(n, step):
    return [(i, min(step, n - i)) for i in range(0, n, step)]


class Ctx:
    pass


(K_ID, K_ONE, K_I1024, K_I512, K_I384, K_I256, K_I128, K_TRI, K_TRIU, K_MNEG, K_MNEGT, K_STRICT,
 K_NEGONE, K_CHUNK, K_AMASK, K_MISC, K_SEL63, K_SEL127) = range(18)
NCONST = 18


def make_consts():
    c = np.zeros((NCONST, 128, 128), np.float32)
    i = np.arange(128)[:, None]
    j = np.arange(128)[None, :]
    same = (i // 64) == (j // 64)
    c[K_ID] = np.eye(128)
    c[K_ONE] = 1.0
    c[K_I1024] = 1.0 / 1024
    c[K_I512] = 1.0 / 512
    c[K_I384] = 1.0 / 384
    c[K_I256] = 1.0 / 256
    c[K_I128] = 1.0 / 128
    c[K_TRI] = (same & (i <= j))
    c[K_TRIU] = (same & (i > j))
    c[K_MNEG] = np.where(same & (j <= i), 0.0, NEG)
    c[K_MNEGT] = np.where(same & (j >= i), 0.0, NEG)
    c[K_STRICT] = (same & (j < i))
    c[K_NEGONE] = -1.0
    c[K_CHUNK] = ((i // 64) == j)
    c[K_AMASK] = np.where(i <= j, 0.0, NEG)
    half = 32
    inv_freq = (10000.0 ** (-np.arange(half, dtype=np.float32) / half)).astype(np.float32)
    c[K_MISC][:64, 0] = np.concatenate([inv_freq, inv_freq])
    c[K_MISC][:64, 1] = np.concatenate([-np.ones(32), np.ones(32)])
    c[K_SEL63][63, :] = 1.0
    c[K_SEL127][127, :] = 1.0
    return np.ascontiguousarray(c.transpose(1, 0, 2))


class ColMap:
    def __init__(self):
        self.n = 0
        self.off = {}
        self.parts = []

    def add(self, name, arr2d):
        self.off[name] = (self.n, arr2d.shape[1])
        self.n += arr2d.shape[1]
        self.parts.append(np.asarray(arr2d, np.float32))

    def build(self):
        return np.ascontiguousarray(np.concatenate(self.parts, axis=1))


def chunked(v):
    return np.asarray(v).reshape(-1, 128).T


def make_cols(inp):
    cm = ColMap()
    for l in range(DEPTH):
        for nm in ("ln1_g", "ln1_b", "ln2_g", "ln2_b", "ln3_g", "ln3_b", "b_gate", "cv_glu_b", "cv_dw_b",
                   "cv_ln_g", "cv_ln_b", "cv_b_pw2", "mla_q_norm_w", "mla_kv_norm_w", "dn_norm_w"):
            cm.add("%s%d" % (nm, l), chunked(inp[nm][l]))
        cw = np.asarray(inp["dn_conv_w"][l])
        cm.add("dn_conv_w%d" % l, cw.reshape(4, 12, 128).transpose(2, 0, 1).reshape(128, 48))
        dw = np.asarray(inp["cv_dw_w"][l])
        cm.add("cv_dw_w%d" % l, dw.reshape(31, 4, 128).transpose(2, 0, 1).reshape(128, 124))
        cm.add("dn_a_log%d" % l, np.broadcast_to(np.asarray(inp["dn_a_log"][l])[None, :], (128, 4)))
        cm.add("dn_dt_bias%d" % l, np.broadcast_to(np.asarray(inp["dn_dt_bias"][l])[None, :], (128, 4)))
    return cm


def colmap_layout():
    fake = {
        "ln1_g": np.zeros((2, 1024)), "ln1_b": np.zeros((2, 1024)), "ln2_g": np.zeros((2, 1024)),
        "ln2_b": np.zeros((2, 1024)), "ln3_g": np.zeros((2, 1024)), "ln3_b": np.zeros((2, 1024)),
        "b_gate": np.zeros((2, 3072)), "cv_glu_b": np.zeros((2, 1024)), "cv_dw_b": np.zeros((2, 512)),
        "cv_ln_g": np.zeros((2, 512)), "cv_ln_b": np.zeros((2, 512)), "cv_b_pw2": np.zeros((2, 1024)),
        "mla_q_norm_w": np.zeros((2, 384)), "mla_kv_norm_w": np.zeros((2, 256)), "dn_norm_w": np.zeros((2, 128)),
        "dn_conv_w": np.zeros((2, 4, 1536)), "cv_dw_w": np.zeros((2, 31, 512)),
        "dn_a_log": np.zeros((2, 4)), "dn_dt_bias": np.zeros((2, 4)),
    }
    return make_cols(fake)


WEIGHTS = {
    "ffn1_w_gate": (DEPTH, D, DFF), "ffn1_w_up": (DEPTH, D, DFF), "ffn1_w_down": (DEPTH, DFF, D),
    "ffn2_w_gate": (DEPTH, D, DFF), "ffn2_w_up": (DEPTH, D, DFF), "ffn2_w_down": (DEPTH, DFF, D),
    "w_in": (DEPTH, D, DIN), "dn_w_o": (DEPTH, 512, D), "cv_w_pw2": (DEPTH, 512, D),
    "mla_w_uq": (DEPTH, 384, 768), "mla_w_ukv": (DEPTH, 256, 1024), "mla_w_o": (DEPTH, 512, D),
    "w_out": (DEPTH, D, D),
}


class Arena:
    def __init__(self, handle, nwords):
        self.h = handle
        self.n = nwords
        self.off = 0

    def reset(self):
        self.off = 0

    def f32(self, *shape):
        n = int(np.prod(shape))
        assert self.off + n <= self.n, ("arena overflow", self.off + n, self.n)
        v = self.h[:, self.off:self.off + n]
        self.off += n
        return self._shape(v, shape)

    def bf16(self, *shape):
        n = int(np.prod(shape))
        w = (n + 1) // 2
        assert self.off + w <= self.n, ("arena overflow", self.off + w, self.n)
        v = self.h[:, self.off:self.off + w].bitcast(BF16)
        if 2 * w != n:
            v = v[:, 0:n]
        self.off += w
        return self._shape(v, shape)

    @staticmethod
    def _shape(v, shape):
        if len(shape) == 1:
            return v
        if len(shape) == 2:
            return v.rearrange("p (a b) -> p a b", a=shape[0])
        if len(shape) == 3:
            return v.rearrange("p (a b c) -> p a b c", a=shape[0], b=shape[1])
        raise ValueError(shape)


AR_WORDS = 33200


def build_program(upto=None, dbg=None):
    nc = bass.Bass("TRN2", target_bir_lowering=False)
    stack = ExitStack()
    cx = Ctx()
    cx.nc = nc
    cx.dbg = dbg
    cx.cm = colmap_layout()
    dr = {}
    dr["xT"] = nc.dram_tensor("xT", [128, 8, T], F32, kind="ExternalInput").ap()
    dr["pos"] = nc.dram_tensor("pos", [1, T], I32, kind="ExternalInput").ap()
    dr["consts"] = nc.dram_tensor("consts", [128, NCONST, 128], F32, kind="ExternalInput").ap()
    dr["cols"] = nc.dram_tensor("cols", [128, cx.cm.n], F32, kind="ExternalInput").ap()
    for k, shp in WEIGHTS.items():
        dr[k] = nc.dram_tensor(k, list(shp), F32, kind="ExternalInput").ap()
    dr["outT"] = nc.dram_tensor("outT", [128, 8, T], F32, kind="ExternalOutput").ap()
    dr["xs"] = nc.dram_tensor("xs", [128, 8, T], F32).ap()
    if dbg is not None:
        dr["dbg"] = nc.dram_tensor("dbg", [128, 8, T], F32, kind="ExternalOutput").ap()
    cx.dr = dr

    X = stack.enter_context(nc.sbuf_tensor("X", [128, 8, T], F32))
    CST = stack.enter_context(nc.sbuf_tensor("CST", [128, NCONST, 128], F32))
    CSTB = stack.enter_context(nc.sbuf_tensor("CSTB", [128, 3, 128], BF16))
    COLS = stack.enter_context(nc.sbuf_tensor("COLS", [128, cx.cm.n], F32))
    ARH = stack.enter_context(nc.sbuf_tensor("AR", [128, AR_WORDS], F32))
    EPSC = stack.enter_context(nc.sbuf_tensor("EPSC", [128, 8], F32))
    cx.POSI = stack.enter_context(nc.sbuf_tensor("POSI", [64, 512], I32))
    cx.EPSC = EPSC
    cx.X, cx.CST, cx.CSTB, cx.COLS = X, CST, CSTB, COLS
    cx.ar = Arena(ARH, AR_WORDS)
    cx.ps = [stack.enter_context(nc.psum_tensor("ps%d" % i, [128, 512], F32)) for i in range(8)]

    P = Prog(nc, stack)
    cx.P = P
    cx.tk_ps = [Tok("ps%d" % i, excl=True) for i in range(8)]
    cx.tk_const = P.tok("const")
    cx.tk_X = [[P.tok("X%d_%d" % (c, t)) for t in range(4)] for c in range(8)]
    cx.tk_out = P.tok("out")
    cx.tk_dbg = P.tok("dbg")
    cx.ones_inv = {8: CST[:, K_I1024, :], 4: CST[:, K_I512, :], 3: CST[:, K_I384, :], 2: CST[:, K_I256, :],
                   1: CST[:, K_I128, :]}

    def col(name, l):
        o, n = cx.cm.off["%s%d" % (name, l)]
        return COLS[:, o:o + n]
    cx.col = col

    eps_vals = [EPS, 1e-6, 0.0, 1.0, -np.pi, 0.0, 0.0, 0.0]
    for i, v in enumerate(eps_vals):
        P.call("dve", "memset", [], [cx.tk_const], True, ap=EPSC[:, i:i + 1], constant=float(v))
    cx.epscol = lambda eps: EPSC[:, eps_vals.index(eps):eps_vals.index(eps) + 1]
    P.dma("sp", CST[:], dr["consts"], writes=[cx.tk_const])
    P.dma("sp", COLS[:], dr["cols"], writes=[cx.tk_const])
    for c in range(8):
        P.dma("sp", X[:, c, 0:512], dr["xT"][:, c, 0:512], writes=[cx.tk_X[c][0]])
    cx.pending_x = [1, 2, 3]
    P.copy("dve", CSTB[:, 0, :], CST[:, K_ID, :], [cx.tk_const], [cx.tk_const])
    P.copy("dve", CSTB[:, 1, :], CST[:, K_AMASK, :], [cx.tk_const], [cx.tk_const])
    P.copy("dve", CSTB[:, 2, :], CST[:, K_ONE, :], [cx.tk_const], [cx.tk_const])

    def done():
        return finish(cx, stack)

    for l in range(DEPTH):
        phase_ffn(cx, l, 1)
        if upto == ("ffn1", l):
            return done()
        phase_mixer(cx, l, upto)
        if upto is not None and upto[1] == l and upto[0] in ("dn", "cv", "mla", "mix"):
            return done()
        phase_ffn(cx, l, 2)
        if upto == ("ffn2", l):
            return done()
    return done()


def finish(cx, stack):
    P = cx.P
    P.barrier()
    last = None
    for c in range(8):
        last = P.dma("sp", cx.dr["outT"][:, c, :], cx.X[:, c, :], reads=cx.tk_X[c], writes=[cx.tk_out])
    fw = [last]
    if cx.dbg is not None and cx.tk_dbg.lw is not None:
        fw.append(cx.tk_dbg.lw)
    P.flush(final_waits=fw)
    stack.close()
    return cx.nc


def dbg_dump(cx, src_ap, reads, c0=0):
    n = src_ap.shape[-1]
    cx.P.dma("sp", cx.dr["dbg"][:, c0, 0:n], src_ap, reads=reads, writes=[cx.tk_dbg])


def phase_ffn(cx, l, which):
    P, X, ar = cx.P, cx.X, cx.ar
    P.barrier()
    ar.reset()
    pre = "ffn%d_" % which
    wg_d, wu_d, wd_d = cx.dr[pre + "w_gate"][l], cx.dr[pre + "w_up"][l], cx.dr[pre + "w_down"][l]
    lng = cx.col("ln1_g" if which == 1 else "ln3_g", l)
    lnb = cx.col("ln1_b" if which == 1 else "ln3_b", l)
    xb = ar.bf16(8, 512)
    hT = ar.bf16(NFF, 512)
    wd = ar.bf16(NFF, 1024)
    GW = 512
    groups = col_chunks(DFF, GW)
    wgu = [[ar.bf16(8, GW) for _ in range(2)] for _ in range(2)]
    sg = [ar.f32(512) for _ in range(2)]
    lntmp = ar.f32(4, 512)
    tk_xb, tk_wd = P.tok("xb"), P.tok("wd")
    tk_h = P.toks(NFF, "h")
    tk_w = [[P.tok("wg"), P.tok("wu")] for _ in range(2)]
    tk_sg = P.toks(2, "sg")
    tk_ln = P.toks(4, "ln")
    ps, tkp = cx.ps, cx.tk_ps

    wdv = wd_d.rearrange("(c p) d -> p c d", p=128)
    wd_loaded = [False]

    def load_wd():
        if not wd_loaded[0]:
            wd_loaded[0] = True
            for c0, n in col_chunks(NFF, 6):
                P.dma("pool", wd[:, c0:c0 + n, :], wdv[:, c0:c0 + n, :], writes=[tk_wd])

    gi = 0
    deferred = []
    ln_gen = None
    for tt in range(4):
        ts = slice(tt * 512, (tt + 1) * 512)
        for c in range(8):
            P.act(xb[:, c, :], X[:, c, ts], AF.Copy, [cx.tk_X[c][tt]], [tk_xb])
        for gidx, (f0, fn_) in enumerate(groups):
            if gidx == 1 and deferred:
                tt_, ts_ = deferred.pop(0)
                ln_gen = layernorm_stages(P, cx, X, [cx.tk_X[c][tt_] for c in range(8)], ts_, lng, lnb, 8,
                                          [tkp[6], tkp[7]], ps[6], ps[7], lntmp, tk_ln)
            if gidx >= 1 and ln_gen is not None:
                if next(ln_gen, "done") == "done":
                    ln_gen = None
            b = gi % 2
            gi += 1
            P.dma("pool", wgu[b][0][:, :, 0:fn_], wg_d[:, f0:f0 + fn_].rearrange("(kc p) f -> p kc f", p=128),
                  writes=[tk_w[b][0]])
            P.dma("pool", wgu[b][1][:, :, 0:fn_], wu_d[:, f0:f0 + fn_].rearrange("(kc p) f -> p kc f", p=128),
                  writes=[tk_w[b][1]])
            if gi == 2:
                for t_ in cx.pending_x:
                    for c in range(8):
                        P.dma("pool", X[:, c, t_ * 512:(t_ + 1) * 512], cx.dr["xT"][:, c, t_ * 512:(t_ + 1) * 512],
                              writes=[cx.tk_X[c][t_]])
                cx.pending_x = []
                load_wd()
            for ci in range(fn_ // 128):
                c = f0 // 128 + ci
                pb = (c % 2) * 2
                for k in range(8):
                    P.mm(ps[pb][:, :], wgu[b][0][:, k, ci * 128:(ci + 1) * 128], xb[:, k, :], k == 0, k == 7,
                         [tk_w[b][0], tk_xb], [tkp[pb]], signal=(k == 7))
                for k in range(8):
                    P.mm(ps[pb + 1][:, :], wgu[b][1][:, k, ci * 128:(ci + 1) * 128], xb[:, k, :], k == 0, k == 7,
                         [tk_w[b][1], tk_xb], [tkp[pb + 1]], signal=(k == 7))
                s = c % 2
                P.act(sg[s], ps[pb][:, :], AF.Silu, [tkp[pb]], [tk_sg[s]])
                P.stt("dve", hT[:, c, :], sg[s], 0.5, ps[pb + 1][:, :], ALU.mult, ALU.mult, [tk_sg[s], tkp[pb + 1]], [tk_h[c]])
        if ln_gen is not None:
            for _ in ln_gen:
                pass
            ln_gen = None
        for d in range(8):
            pb = 4 + d % 2
            for c in range(NFF):
                P.mm(ps[pb][:, :], wd[:, c, d * 128:(d + 1) * 128], hT[:, c, :], c == 0, c == NFF - 1,
                     [tk_wd, tk_h[c]], [tkp[pb]], signal=(c == NFF - 1))
            P.stt("dve", X[:, d, ts], X[:, d, ts], ALPHA, ps[pb][:, :], ALU.mult, ALU.add, [tkp[pb]], [cx.tk_X[d][tt]])
        deferred.append((tt, ts))
    for (tt_, ts_) in deferred:
        emit_layernorm_tile(P, cx, X, [cx.tk_X[c][tt_] for c in range(8)], ts_, lng, lnb, 8,
                            [tkp[6], tkp[7]], ps[6], ps[7], lntmp, tk_ln)


def wload(P, dst, src2d, tok, q="pool"):
    P.dma(q, dst, src2d.rearrange("(kc p) f -> p kc f", p=128), writes=[tok])


def flat(ap3):
    return ap3.rearrange("p a b -> p (a b)")


def phase_mixer(cx, l, upto):
    P, X, ar = cx.P, cx.X, cx.ar
    stage = upto[0] if (upto is not None and upto[1] == l) else None
    P.barrier()
    ar.reset()
    a2 = Arena(X[:, :, :].rearrange("p c t -> p (c t)"), 8 * T)
    xbm = ar.bf16(8, T)
    outs = [ar.bf16(4, T) for _ in range(3)]
    tk_xbm = P.tok("xbm")
    tk_outs = P.toks(3, "bo")
    tk_xs = P.tok("xs")
    mark = ar.off
    for c in range(8):
        P.act(xbm[:, c, :], X[:, c, :], AF.Copy, cx.tk_X[c], [tk_xbm])
        P.dma("sp", cx.dr["xs"][:, c, :], X[:, c, :], reads=cx.tk_X[c], writes=[tk_xs])
    P.barrier()

    def dump(idx):
        P.barrier()
        a2.reset()
        tmp = a2.f32(4, T)
        tk = P.tok("dmp")
        for c in range(4):
            P.act(tmp[:, c, :], outs[idx][:, c, :], AF.Copy, [tk_outs[idx]], [tk])
            P.dma("sp", cx.dr["dbg"][:, c, :], tmp[:, c, :], reads=[tk], writes=[cx.tk_dbg])
        P.barrier()

    if stage in (None, "dn", "mix"):
        ar.off = mark
        a2.reset()
        branch_dn(cx, l, xbm, tk_xbm, outs[0], tk_outs[0], ar, a2)
        if stage == "dn":
            dump(0)
    if stage in (None, "cv", "mix"):
        P.barrier()
        ar.off = mark
        a2.reset()
        branch_cv(cx, l, xbm, tk_xbm, outs[1], tk_outs[1], ar, a2)
        if stage == "cv":
            dump(1)
    if stage in (None, "mla", "mix"):
        P.barrier()
        ar.off = mark
        a2.reset()
        branch_mla(cx, l, xbm, tk_xbm, outs[2], tk_outs[2], ar, a2)
        if stage == "mla":
            dump(2)
    P.barrier()
    ar.off = mark
    a2.reset()
    if stage in ("dn", "cv", "mla"):
        for c in range(8):
            P.dma("sp", X[:, c, :], cx.dr["xs"][:, c, :], reads=[tk_xs], writes=cx.tk_X[c])
        return
    merge_phase(cx, l, xbm, tk_xbm, outs, tk_outs, tk_xs, ar, a2)


def branch_dn(cx, l, xbm, tk_xbm, o_dn, tk_odn, a1, a2):
    P, CST, ps, tkp = cx.P, cx.CST, cx.ps, cx.tk_ps
    kc_ = cx.tk_const
    w_in = cx.dr["w_in"][l]
    ID, ONE, NEGONE = CST[:, K_ID, :], CST[:, K_ONE, :], CST[:, K_NEGONE, :]
    TRI, TRIU, MNEG, MNEGT, STRICT = (CST[:, k, :] for k in (K_TRI, K_TRIU, K_MNEG, K_MNEGT, K_STRICT))
    wab = a1.bf16(8, 8)
    wq = a1.bf16(4, 8, 128)
    sm = {n: a1.f32(64) for n in ("xa", "ax", "e", "ln", "sp", "g", "beta", "nbeta", "gc", "egc", "bege", "ekd")}
    egl = a1.f32(2, 64)
    ea = a1.f32(4)
    v3 = lambda a: a.rearrange("p (t h) -> p t h", h=4)
    pres = [a2.f32(T + 4) for _ in range(3)]
    qT = a2.f32(T)
    kT = a2.f32(T)
    vT = a2.f32(T)
    k_tm = a2.bf16(16, 128)
    v_tm = a2.bf16(16, 128)
    oT = pres[1][:, 0:T]
    sq = qT[:, 0:512]
    rin = qT[:, 512:1024]
    qbf, kbf, vbf = a1.bf16(T), a1.bf16(T), a1.bf16(T)
    f32_names = ("GTri", "decay", "decayT", "egrow", "t1", "MTa", "MTb")
    bf_names = ("Nm", "NT", "P0", "P1", "PT0", "PT1", "TT0", "TT1", "vb", "kb", "u", "w", "qkT", "QeffT", "kdec")
    tmp_names = f32_names + bf_names
    tb = {}

    def pick(nw):
        return a1 if (a1.n - a1.off) >= nw else a2
    for n in f32_names:
        tb[n] = pick(512).f32(4, 128)
    for n in bf_names:
        tb[n] = pick(256).bf16(4, 128)
    Sbuf = [pick(128).f32(128) for _ in range(2)]
    Sbf = [pick(64).bf16(128) for _ in range(2)]
    tk = {n: P.tok(n) for n in tmp_names}
    tk_wab, tk_sm, tk_q, tk_k, tk_v, tk_ktm, tk_vtm, tk_sq, tk_rin = (
        P.tok(n) for n in ("wab", "sm", "q", "k", "v", "ktm", "vtm", "sq", "rin"))
    tk_pres = P.toks(3, "pre")
    tk_oT = tk_pres[1]
    tk_pre = tk_pres[0]
    tk_wq = P.toks(4, "wq")
    tk_S = P.toks(2, "S")
    tk_Sb = P.toks(2, "Sb")
    tk_qb, tk_kb, tk_vb = P.tok("qb"), P.tok("kb_"), P.tok("vb_")
    tk_sq = tk_rin = tk_q

    P.dma("pool", wab, w_in[:, C_DA:C_DA + 8].rearrange("(kc p) f -> p kc f", p=128), writes=[tk_wab])
    ab = ps[0][:, 0:128].rearrange("p (t e) -> p t e", e=8)
    for t in range(16):
        for k in range(8):
            P.mm(ab[:, t, :], xbm[:, k, t * 128:(t + 1) * 128], wab[:, k, :], k == 0, k == 7, [tk_xbm, tk_wab], [tkp[0]],
                 signal=(k == 7))
    dtb = cx.col("dn_dt_bias", l)
    alog = cx.col("dn_a_log", l)
    for h in range(4):
        P.ts("dve", v3(sm["xa"])[:, :, h], ab[:, :, h], dtb[:, h:h + 1], None, ALU.add, None, [tkp[0], kc_], [tk_sm])
    P.act(sm["ax"], sm["xa"], AF.Abs, [tk_sm], [tk_sm])
    P.act(sm["e"], sm["ax"], AF.Exp, [tk_sm], [tk_sm], scale=-1.0)
    P.act(sm["ln"], sm["e"], AF.Ln, [tk_sm], [tk_sm], bias=cx.epscol(1.0), scale=1.0)
    P.stt("dve", sm["sp"], sm["xa"], 0.0, sm["ln"], ALU.max, ALU.add, [tk_sm], [tk_sm])
    P.act(ea, alog, AF.Exp, [kc_], [tk_sm])
    for h in range(4):
        P.ts("dve", v3(sm["g"])[:, :, h], v3(sm["sp"])[:, :, h], ea[:, h:h + 1], -1.0, ALU.mult, ALU.mult, [tk_sm], [tk_sm])
    P.act(v3(sm["beta"]), ab[:, :, 4:8], AF.Sigmoid, [tkp[0]], [tk_sm])
    P.ts("dve", sm["nbeta"], sm["beta"], -1.0, None, ALU.mult, None, [tk_sm], [tk_sm])
    P.mm(ps[1][:, 0:64], TRI, sm["g"], True, True, [tk_sm, kc_], [tkp[1]])
    P.mm(ps[1][:, 64:128], TRIU, sm["g"], True, True, [tk_sm, kc_], [tkp[1]])
    P.act(sm["gc"], ps[1][:, 0:64], AF.Copy, [tkp[1]], [tk_sm])
    P.act(sm["egc"], ps[1][:, 0:64], AF.Exp, [tkp[1]], [tk_sm])
    P.act(sm["ekd"], ps[1][:, 64:128], AF.Exp, [tkp[1]], [tk_sm])
    P.tt("dve", sm["bege"], sm["beta"], sm["egc"], ALU.mult, [tk_sm], [tk_sm])
    P.mm(ps[1][:, 128:192], CST[:, K_SEL63, :], sm["gc"], True, True, [tk_sm, kc_], [tkp[1]])
    P.mm(ps[1][:, 192:256], CST[:, K_SEL127, :], sm["gc"], True, True, [tk_sm, kc_], [tkp[1]])
    P.act(flat(egl), ps[1][:, 128:256], AF.Exp, [tkp[1]], [tk_sm])

    cw = cx.col("dn_conv_w", l)
    normw = cx.col("dn_norm_w", l)
    for h in range(4):
        for s_, c0 in enumerate((C_DQ, C_DK, C_DV, C_DZ)):
            wload(P, wq[:, s_], w_in[:, c0 + h * 128:c0 + (h + 1) * 128], tk_wq[s_])
        dsts = ((qT, tk_q), (kT, tk_k), (vT, tk_v))
        for s_ in range(3):
            P.call("dve", "memset", [], [tk_pres[s_]], True, ap=pres[s_][:, 0:3], constant=0.0)
            for tt in range(4):
                ts = slice(tt * 512, (tt + 1) * 512)
                bank = 6 + tt % 2
                for k in range(8):
                    P.mm(ps[bank][:, :], wq[:, s_, k, :], xbm[:, k, ts], k == 0, k == 7, [tk_wq[s_], tk_xbm], [tkp[bank]],
                         signal=(k == 7))
                P.act(pres[s_][:, 3 + tt * 512:3 + (tt + 1) * 512], ps[bank][:, :], AF.Copy, [tkp[bank]], [tk_pres[s_]])
        for s_ in range(3):
            cc = s_ * 4 + h
            dst, tkd = dsts[s_]
            pre = pres[s_]
            P.ts("dve", dst, pre[:, 0:T], cw[:, cc:cc + 1], None, ALU.mult, None, [tk_pres[s_], kc_], [tkd])
            for j in range(1, 4):
                P.stt("dve", dst, pre[:, j:j + T], cw[:, j * 12 + cc:j * 12 + cc + 1], dst, ALU.mult, ALU.add,
                      [tk_pres[s_], kc_], [tkd])
            if s_ < 2:
                P.act(dst, dst, AF.Silu, [], [tkd])
            else:
                P.act(vbf, dst, AF.Silu, [tkd], [tk_vb])
        for s_ in range(2):
            dst, tkd = dsts[s_]
            scr = pres[s_][:, 0:T]
            tks = tk_pres[s_]
            P.act(scr, dst, AF.Square, [tkd], [tks])
            for tt in range(4):
                ts = slice(tt * 512, (tt + 1) * 512)
                P.mm(ps[4 + tt][:, :], ONE, scr[:, ts], True, True, [tks, kc_], [tkp[4 + tt]])
            for tt in range(4):
                ts = slice(tt * 512, (tt + 1) * 512)
                P.act(scr[:, ts], ps[4 + tt][:, :], AF.Sqrt, [tkp[4 + tt], kc_], [tks], bias=cx.epscol(1e-6), scale=1.0)
            P.call("dve", "reciprocal", [tks], [tks], True, out=scr, in_=scr)
            if s_ == 0:
                P.stt("dve", qbf, dst, 128.0 ** -0.5, scr, ALU.mult, ALU.mult, [tks, tkd], [tk_qb])
            else:
                P.tt("dve", kbf, dst, scr, ALU.mult, [tks, tkd], [tk_kb])
        IDb_ = cx.CSTB[:, 0, :]
        for src_, dstm, tks, tkd in ((kbf, k_tm, tk_kb, tk_ktm), (vbf, v_tm, tk_vb, tk_vtm)):
            for g4 in range(4):
                bank = 6 + g4 % 2
                psb_ = ps[bank][:, 0:256].bitcast(BF16)
                for q in range(4):
                    t = g4 * 4 + q
                    P.tr(psb_[:, q * 128:(q + 1) * 128], src_[:, t * 128:(t + 1) * 128], IDb_, [tks, kc_], [tkp[bank]])
                P.act(flat(dstm[:, g4 * 4:(g4 + 1) * 4, :]), psb_, AF.Copy, [tkp[bank]], [tkd])
        P.call("dve", "memset", [], [tk_S[0]], True, ap=Sbuf[0], constant=0.0)
        P.call("dve", "memset", [], [tk_Sb[0]], True, ap=Sbf[0], constant=0.0)
        cur = 0
        IDb = cx.CSTB[:, 0, :]
        bc4 = lambda ap2: ap2.unsqueeze(1).to_broadcast([128, 4, 128])
        for G in range(4):
            gs = slice(G * 512, (G + 1) * 512)
            F = {n: flat(tb[n]) for n in tmp_names}
            ps4b = ps[4][:, 0:256].bitcast(BF16)
            for q in range(4):
                t = G * 4 + q
                gcol = v3(sm["g"])[:, t, h:h + 1]
                P.ts("dve", tb["GTri"][:, q, :], TRI, gcol, None, ALU.mult, None, [tk_sm, kc_], [tk["GTri"]])
            for q in range(4):
                t = G * 4 + q
                tq = slice(q * 128, (q + 1) * 128)
                tcol = slice(t * 128, (t + 1) * 128)
                P.mm(ps[2][:, tq], ONE, tb["GTri"][:, q, :], True, True, [tk["GTri"], kc_], [tkp[2]])
                P.mm(ps[3][:, tq], kbf[:, tcol], kbf[:, tcol], True, True, [tk_kb], [tkp[3]])
            P.act(F["t1"], ps[2][:, :], AF.Copy, [tkp[2]], [tk["t1"]])
            P.act(F["egrow"], F["t1"], AF.Exp, [tk["t1"]], [tk["egrow"]])
            for q in range(4):
                t = G * 4 + q
                gccol = v3(sm["gc"])[:, t, h:h + 1]
                P.act(tb["decay"][:, q, :], tb["t1"][:, q, :], AF.Identity, [tk["t1"], tk_sm], [tk["decay"]], bias=gccol, scale=-1.0)
                P.stt("dve", tb["decayT"][:, q, :], tb["t1"][:, q, :], gccol, MNEGT, ALU.subtract, ALU.min,
                      [tk["t1"], tk_sm, kc_], [tk["decayT"]])
            for q in range(4):
                P.tt("dve", tb["decay"][:, q, :], tb["decay"][:, q, :], MNEG, ALU.min, [kc_], [tk["decay"]])
            P.act(F["decayT"], F["decayT"], AF.Exp, [], [tk["decayT"]])
            P.act(F["decay"], F["decay"], AF.Exp, [], [tk["decay"]])
            P.tt("dve", F["t1"], ps[3][:, :], F["decay"], ALU.mult, [tkp[3], tk["decay"]], [tk["t1"]])
            if DN_STOP <= 1:
                continue
            for q in range(4):
                t = G * 4 + q
                nbcol = v3(sm["nbeta"])[:, t, h:h + 1]
                P.stt("dve", tb["Nm"][:, q, :], tb["t1"][:, q, :], nbcol, STRICT, ALU.mult, ALU.mult,
                      [tk["t1"], tk_sm, kc_], [tk["Nm"]])
            for q in range(4):
                P.tr(ps4b[:, q * 128:(q + 1) * 128], tb["Nm"][:, q, :], IDb, [tk["Nm"], kc_], [tkp[4]])
            P.act(F["NT"], ps4b, AF.Copy, [tkp[4]], [tk["NT"]])
            for q in range(4):
                P.tt("dve", tb["TT0"][:, q, :], tb["NT"][:, q, :], ID, ALU.add, [tk["NT"], kc_], [tk["TT0"]])
            if DN_STOP <= 2:
                continue
            Pc, PTc, TTc = "Nm", "NT", "TT0"
            for s_ in range(1, 6):
                Pn, PTn, TTn = ("P%d" % (s_ % 2)), ("PT%d" % (s_ % 2)), ("TT%d" % (s_ % 2))
                for q in range(4):
                    tq = slice(q * 128, (q + 1) * 128)
                    P.mm(ps[5][:, tq], tb[PTc][:, q, :], tb[Pc][:, q, :], True, True, [tk[PTc], tk[Pc]], [tkp[5]])
                if s_ < 5:
                    for q in range(4):
                        tq = slice(q * 128, (q + 1) * 128)
                        P.mm(ps[6][:, tq], tb[Pc][:, q, :], tb[PTc][:, q, :], True, True, [tk[PTc], tk[Pc]], [tkp[6]])
                P.act(F[Pn], ps[5][:, :], AF.Copy, [tkp[5]], [tk[Pn]])
                if s_ < 5:
                    P.copy("dve", F[PTn], ps[6][:, :], [tkp[6]], [tk[PTn]])
                for q in range(4):
                    tq = slice(q * 128, (q + 1) * 128)
                    P.mm(ps[7][:, tq], tb[Pn][:, q, :], tb[TTc][:, q, :], True, True, [tk[Pn], tk[TTc]], [tkp[7]])
                P.tt("dve", F[TTn], F[TTc], ps[7][:, :], ALU.add, [tk[TTc], tkp[7]], [tk[TTn]])
                Pc, PTc, TTc = Pn, PTn, TTn
            TT = tb[TTc]
            tkTT = tk[TTc]
            if DN_STOP <= 3:
                continue
            for q in range(4):
                t = G * 4 + q
                P.act(tb["vb"][:, q, :], v_tm[:, t, :], AF.Copy, [tk_vtm, tk_sm], [tk["vb"]], scale=v3(sm["beta"])[:, t, h:h + 1])
                P.act(tb["kb"][:, q, :], k_tm[:, t, :], AF.Copy, [tk_ktm, tk_sm], [tk["kb"]], scale=v3(sm["bege"])[:, t, h:h + 1])
                P.act(tb["kdec"][:, q, :], k_tm[:, t, :], AF.Copy, [tk_ktm, tk_sm], [tk["kdec"]], scale=v3(sm["ekd"])[:, t, h:h + 1])
            for q in range(4):
                t = G * 4 + q
                tq = slice(q * 128, (q + 1) * 128)
                tcol = slice(t * 128, (t + 1) * 128)
                P.mm(ps[0][:, tq], TT[:, q, :], tb["vb"][:, q, :], True, True, [tkTT, tk["vb"]], [tkp[0]])
                P.mm(ps[1][:, tq], TT[:, q, :], tb["kb"][:, q, :], True, True, [tkTT, tk["kb"]], [tkp[1]])
                P.mm(ps[2][:, tq], kbf[:, tcol], qbf[:, tcol], True, True, [tk_kb, tk_qb], [tkp[2]])
            P.act(F["u"], ps[0][:, :], AF.Copy, [tkp[0]], [tk["u"]])
            P.act(F["w"], ps[1][:, :], AF.Copy, [tkp[1]], [tk["w"]])
            P.tt("dve", F["qkT"], ps[2][:, :], F["decayT"], ALU.mult, [tkp[2], tk["decayT"]], [tk["qkT"]])
            if DN_STOP <= 4:
                continue
            for q in range(4):
                tq = slice(q * 128, (q + 1) * 128)
                P.mm(ps[3][:, tq], tb["w"][:, q, :], tb["qkT"][:, q, :], True, True, [tk["w"], tk["qkT"]], [tkp[3]])
            for q in range(4):
                tq = slice(q * 128, (q + 1) * 128)
                for ch in range(2):
                    r = slice(ch * 64, ch * 64 + 64)
                    P.mm(ps[4 + ch][:, tq], tb["w"][r, q, :], tb["kdec"][r, q, :], True, True, [tk["w"], tk["kdec"]], [tkp[4 + ch]])
            P.tt("dve", F["t1"], qbf[:, gs], F["egrow"], ALU.mult, [tk_qb, tk["egrow"], tk["Nm"]], [tk["t1"]])
            P.tt("dve", F["QeffT"], F["t1"], ps[3][:, :], ALU.subtract, [tkp[3], tk["t1"]], [tk["QeffT"]])
            for q in range(4):
                t = G * 4 + q
                tq = slice(q * 128, (q + 1) * 128)
                for ch in range(2):
                    eglcol = egl[:, ch, t * 4 + h:t * 4 + h + 1]
                    nm = "MTa" if ch == 0 else "MTb"
                    P.stt("dve", tb[nm][:, q, :], ID, eglcol, ps[4 + ch][:, tq], ALU.mult, ALU.subtract,
                          [tkp[4 + ch], tk_sm, kc_], [tk[nm]])
            if DN_STOP <= 5:
                continue
            for q in range(4):
                for ch in range(2):
                    r = slice(ch * 64, ch * 64 + 64)
                    tc = slice(q * 128 + ch * 64, q * 128 + ch * 64 + 64)
                    nm = "MTa" if ch == 0 else "MTb"
                    S_c, S_n = Sbuf[cur], Sbuf[1 - cur]
                    sb = 6 + cur
                    P.mm(ps[sb][:, 0:128], tb[nm][:, q, :], S_c, True, False, [tk[nm], tk_S[cur]], [tkp[sb]])
                    P.mm(ps[sb][:, 0:128], tb["kdec"][r, q, :], tb["u"][r, q, :], False, True, [tk["kdec"], tk["u"]], [tkp[sb]])
                    P.act(S_n, ps[sb][:, 0:128], AF.Copy, [tkp[sb]], [tk_S[1 - cur]])
                    P.copy("dve", Sbf[1 - cur], S_n, [tk_S[1 - cur]], [tk_Sb[1 - cur]])
                    P.mm(ps[3][:, tc], Sbf[cur], F["QeffT"][:, tc], True, False, [tk_Sb[cur], tk["QeffT"]], [tkp[3]])
                    P.mm(ps[3][:, tc], tb["u"][r, q, :], tb["qkT"][r, q, ch * 64:ch * 64 + 64], False, True,
                         [tk["u"], tk["qkT"]], [tkp[3]])
                    cur = 1 - cur
            P.act(oT[:, gs], ps[3][:, :], AF.Copy, [tkp[3]], [tk_oT])
        zs = pres[0][:, 0:T]
        sqf, rinf = qT, kT
        for tt in range(4):
            ts = slice(tt * 512, (tt + 1) * 512)
            bank = 6 + tt % 2
            for k in range(8):
                P.mm(ps[bank][:, :], wq[:, 3, k, :], xbm[:, k, ts], k == 0, k == 7, [tk_wq[3], tk_xbm], [tkp[bank]],
                     signal=(k == 7))
            P.act(zs[:, ts], ps[bank][:, :], AF.Silu, [tkp[bank]], [tk_pre])
        P.act(sqf, oT, AF.Square, [tk_oT], [tk_q])
        for tt in range(4):
            ts = slice(tt * 512, (tt + 1) * 512)
            P.mm(ps[tt][:, :], CST[:, K_I128, :], sqf[:, ts], True, True, [tk_q, kc_], [tkp[tt]])
        for tt in range(4):
            ts = slice(tt * 512, (tt + 1) * 512)
            P.act(rinf[:, ts], ps[tt][:, :], AF.Sqrt, [tkp[tt], kc_], [tk_k], bias=cx.epscol(EPS), scale=1.0)
        P.call("dve", "reciprocal", [tk_k], [tk_k], True, out=rinf, in_=rinf)
        P.tt("dve", rinf, oT, rinf, ALU.mult, [tk_oT, tk_k], [tk_k])
        P.stt("dve", o_dn[:, h, :], rinf, normw[:, 0:1], zs, ALU.mult, ALU.mult, [tk_k, tk_pre, kc_], [tk_odn])


def branch_cv(cx, l, xbm, tk_xbm, h_cv, tk_hcv, a1, a2):
    P, CST, ps, tkp = cx.P, cx.CST, cx.ps, cx.tk_ps
    kc_ = cx.tk_const
    w_in = cx.dr["w_in"][l]
    PADW = T + 32
    hpad = a2.bf16(4, PADW)
    Dg = a2.bf16(124, 128)
    wa = [a2.bf16(8, 128) for _ in range(2)]
    wg = [a2.bf16(8, 128) for _ in range(2)]
    ta = [a2.f32(512) for _ in range(2)]
    tsg = [a2.f32(512) for _ in range(2)]
    acc = a1.f32(4, T)
    lntmp = a1.f32(4, 512)
    tk_hp = P.toks(4, "hp")
    tk_acc = P.toks(4, "cacc")
    tk_wa, tk_wg = P.toks(2, "wa"), P.toks(2, "wg")
    tk_ta, tk_tsg = P.toks(2, "ta"), P.toks(2, "tsg")
    tk_ln = P.toks(4, "cln")
    tk_dg = P.tok("dg")
    glub = cx.col("cv_glu_b", l)
    dww = cx.col("cv_dw_w", l)
    dwb = cx.col("cv_dw_b", l)
    lng, lnb = cx.col("cv_ln_g", l), cx.col("cv_ln_b", l)
    ID = CST[:, K_ID, :]
    for idx in range(124):
        P.ts("dve", Dg[:, idx, :], ID, dww[:, idx:idx + 1], None, ALU.mult, None, [kc_], [tk_dg])
    it = 0
    ci = 0
    for cc in range(4):
        s = cc % 2
        wload(P, wa[s], w_in[:, C_GLU + cc * 128:C_GLU + (cc + 1) * 128], tk_wa[s])
        wload(P, wg[s], w_in[:, C_GLU + 512 + cc * 128:C_GLU + 512 + (cc + 1) * 128], tk_wg[s])
        P.call("dve", "memset", [], [tk_hp[cc]], True, ap=hpad[:, cc, 0:30], constant=0.0)
        for tt in range(4):
            ts = slice(tt * 512, (tt + 1) * 512)
            b0 = (it % 2) * 2
            u = it % 2
            it += 1
            for k in range(8):
                P.mm(ps[b0][:, :], wa[s][:, k, :], xbm[:, k, ts], k == 0, k == 7, [tk_wa[s], tk_xbm], [tkp[b0]], signal=(k == 7))
            for k in range(8):
                P.mm(ps[b0 + 1][:, :], wg[s][:, k, :], xbm[:, k, ts], k == 0, k == 7, [tk_wg[s], tk_xbm], [tkp[b0 + 1]],
                     signal=(k == 7))
            P.act(ta[u], ps[b0][:, :], AF.Identity, [tkp[b0], kc_], [tk_ta[u]], bias=glub[:, cc:cc + 1], scale=1.0)
            P.act(tsg[u], ps[b0 + 1][:, :], AF.Sigmoid, [tkp[b0 + 1], kc_], [tk_tsg[u]], bias=glub[:, 4 + cc:5 + cc], scale=1.0)
            P.tt("dve", hpad[:, cc, 30 + tt * 512:30 + (tt + 1) * 512], ta[u], tsg[u], ALU.mult, [tk_ta[u], tk_tsg[u]],
                 [tk_hp[cc]])
        for tt in range(4):
            ts = slice(tt * 512, (tt + 1) * 512)
            bank = 4 + ci % 2
            ci += 1
            for j in range(31):
                P.mm(ps[bank][:, :], Dg[:, j * 4 + cc, :], hpad[:, cc, j + tt * 512:j + tt * 512 + 512], j == 0, j == 30,
                     [tk_dg, tk_hp[cc]], [tkp[bank]], signal=(j == 30))
            P.act(acc[:, cc, ts], ps[bank][:, :], AF.Identity, [tkp[bank], kc_], [tk_acc[cc]], bias=dwb[:, cc:cc + 1], scale=1.0)
    for tt in range(4):
        ts = slice(tt * 512, (tt + 1) * 512)

        def out_fn(c, t_ap, tk_t, ts=ts):
            P.act(h_cv[:, c, ts], t_ap, AF.Silu, [tk_t, kc_], [tk_hcv], bias=lnb[:, c:c + 1], scale=lng[:, c:c + 1])
        emit_layernorm_tile(P, cx, acc, tk_acc, ts, lng, lnb, 4, [tkp[6], tkp[7]], ps[6], ps[7], lntmp, tk_ln, out_fn)


def branch_mla(cx, l, xbm, tk_xbm, o_mla, tk_omla, a1, a2):
    P, CST, CSTB, ps, tkp = cx.P, cx.CST, cx.CSTB, cx.ps, cx.tk_ps
    kc_ = cx.tk_const
    w_in, w_uq, w_ukv = cx.dr["w_in"][l], cx.dr["mla_w_uq"][l], cx.dr["mla_w_ukv"][l]
    IDb, AMb, ONEb = CSTB[:, 0, :], CSTB[:, 1, :], CSTB[:, 2, :]
    scale = 192.0 ** -0.5
    cqn = a2.bf16(3, T)
    ckvn = a2.bf16(2, T)
    CS = a2.f32(T)
    SS = a2.f32(T)
    krT = a2.bf16(T)
    Vaug = a2.bf16(64, 130)
    qnT, qrT, knT = a1.bf16(T), a1.bf16(T), a1.bf16(T)
    PT = a1.bf16(16, 512)
    wc = [a1.bf16(8, 128) for _ in range(2)]
    wuqn, wuqA, wuqB = a1.bf16(3, 128), a1.bf16(3, 64), a1.bf16(3, 64)
    wukv = a1.bf16(2, 1024)
    wkrA, wkrB = a1.bf16(8, 64), a1.bf16(8, 64)
    tmp = a1.f32(3, 512)
    sq, rstd = a1.f32(512), a1.f32(512)
    o_n = a2.bf16(128)
    o_n2 = [o_n, rstd.bitcast(BF16)[:, 0:128]]
    smx = a2.f32(16)
    posi = tmp[:, 0, :].bitcast(I32)
    t1, t2 = tmp[:, 1, :], tmp[:, 2, :]
    (tk_cqn, tk_ckvn, tk_cs, tk_kr, tk_v, tk_qn, tk_qr, tk_kn, tk_tmp, tk_sq, tk_rstd, tk_on, tk_smx, tk_wuq,
     tk_wukv, tk_wkr, tk_t1, tk_t2, tk_pos) = (P.tok(n) for n in (
         "cqn", "ckvn", "cs", "kr", "v", "qn", "qr", "kn", "tmp", "sq", "rstd", "on", "smx", "wuq", "wukv", "wkr",
         "t1", "t2", "pos"))
    tk_t1 = tk_t2 = tk_pos = tk_tmp
    tk_wc = P.toks(2, "wc")
    tk_PT = P.toks(16, "PT")
    tk_on2 = [tk_on, tk_rstd]
    tk_sqs = [P.tok("sqs0"), P.tok("sqs1"), P.tok("sqs2"), P.tok("sqs3")]
    qw, kvw = cx.col("mla_q_norm_w", l), cx.col("mla_kv_norm_w", l)
    invf, sgn = CST[:, K_MISC, 0:1], CST[:, K_MISC, 1:2]

    wload(P, wukv, w_ukv[:, :], tk_wukv)
    wload(P, wkrA, w_in[:, C_KR:C_KR + 64], tk_wkr)
    P.dma("pool", wkrB[:, :, 0:32], w_in[:, C_KR + 32:C_KR + 64].rearrange("(kc p) f -> p kc f", p=128), writes=[tk_wkr])
    P.dma("pool", wkrB[:, :, 32:64], w_in[:, C_KR:C_KR + 32].rearrange("(kc p) f -> p kc f", p=128), writes=[tk_wkr])
    P.call("dve", "memset", [], [tk_v], True, ap=Vaug[:, :, 128:130], constant=1.0)

    PTf = flat(PT).bitcast(F32)
    kvtmp = PTf[:, 0:1024].rearrange("p (a b) -> p a b", a=2)
    sq2, rstd2 = PTf[:, 1024:1536], PTf[:, 1536:2048]
    rs = [PTf[:, 2048 + i_ * 512:2048 + (i_ + 1) * 512] for i_ in range(4)]
    tk_kvtmp, tk_sq2, tk_rstd2 = P.tok("kvtmp"), P.tok("sq2"), P.tok("rstd2")
    tk_rs = P.tok("rs")
    def rope_tile(tt):
        ts = slice(tt * 512, (tt + 1) * 512)
        pi_, y_, A_, B_ = cx.POSI[:, :], rs[1][0:64, :], rs[2][0:64, :], rs[3][0:64, :]
        P.dma("sp", pi_, cx.dr["pos"][:, ts].partition_broadcast(64), reads=[tk_rs], writes=[tk_pos])
        P.copy("dve", y_, pi_, [tk_pos], [tk_rs])
        P.ts("dve", y_, y_, invf[0:64, :], None, ALU.mult, None, [kc_], [tk_rs])
        P.ts("dve", y_, y_, float(1.0 / (2 * np.pi)), None, ALU.mult, None, [], [tk_rs])
        P.copy("dve", pi_, y_, [tk_pos], [tk_rs])
        P.copy("dve", A_, pi_, [], [tk_rs])
        P.tt("dve", y_, y_, A_, ALU.subtract, [], [tk_rs])
        for (dstT, shift) in ((SS, 0.0), (CS, 0.25)):
            if shift != 0.0:
                P.ts("dve", y_, y_, shift, None, ALU.add, None, [], [tk_rs])
            P.ts("dve", A_, y_, 0.5, None, ALU.is_gt, None, [], [tk_rs])
            P.tt("dve", B_, y_, A_, ALU.subtract, [], [tk_rs])
            P.ts("dve", A_, y_, -0.5, None, ALU.is_lt, None, [], [tk_rs])
            P.tt("dve", B_, B_, A_, ALU.add, [], [tk_rs])
            P.act(dstT[0:64, ts], B_, AF.Sin, [tk_rs], [tk_cs], scale=float(2 * np.pi))
        P.ts("dve", SS[0:64, ts], SS[0:64, ts], sgn[0:64, :], None, ALU.mult, None, [kc_], [tk_cs])
    wi = 0
    for tt in range(4):
        ts = slice(tt * 512, (tt + 1) * 512)
        rope_tile(tt)
        for (c0, nch, dstn, tkd, wcol, inv, tbuf, tkt, sq_, tksq, rstd_, tkr, pb) in (
                (C_CQ, 3, cqn, tk_cqn, qw, CST[:, K_I384, :], tmp, tk_tmp, sq, tk_sq, rstd, tk_rstd, 2),
                (C_CKV, 2, ckvn, tk_ckvn, kvw, CST[:, K_I256, :], kvtmp, tk_kvtmp, sq2, tk_sq2, rstd2, tk_rstd2, 5)):
            for ch in range(nch):
                s = wi % 2
                wi += 1
                wload(P, wc[s], w_in[:, c0 + ch * 128:c0 + (ch + 1) * 128], tk_wc[s])
                for k in range(8):
                    P.mm(ps[s][:, :], wc[s][:, k, :], xbm[:, k, ts], k == 0, k == 7, [tk_wc[s], tk_xbm], [tkp[s]], signal=(k == 7))
                P.act(tbuf[:, ch, :], ps[s][:, :], AF.Copy, [tkp[s]], [tkt])
            for ch in range(nch):
                P.act(sq_, tbuf[:, ch, :], AF.Square, [tkt], [tksq])
                P.mm(ps[pb][:, :], inv, sq_, ch == 0, ch == nch - 1, [tksq, kc_], [tkp[pb]])
            P.act(rstd_, ps[pb][:, :], AF.Sqrt, [tkp[pb], kc_], [tkr], bias=cx.epscol(EPS), scale=1.0)
            P.call("dve", "reciprocal", [tkr], [tkr], True, out=rstd_, in_=rstd_)
            for ch in range(nch):
                P.tt("dve", tbuf[:, ch, :], tbuf[:, ch, :], rstd_, ALU.mult, [tkr], [tkt])
                P.act(dstn[:, ch, ts], tbuf[:, ch, :], AF.Copy, [tkt, kc_], [tkd], scale=wcol[:, ch:ch + 1])
    for tt in range(4):
        ts = slice(tt * 512, (tt + 1) * 512)
        for k in range(8):
            P.mm(ps[3][0:64, :], wkrA[:, k, :], xbm[:, k, ts], k == 0, k == 7, [tk_wkr, tk_xbm], [tkp[3]], signal=(k == 7))
        for k in range(8):
            P.mm(ps[4][0:64, :], wkrB[:, k, :], xbm[:, k, ts], k == 0, k == 7, [tk_wkr, tk_xbm], [tkp[4]], signal=(k == 7))
        P.tt("dve", rs[2][0:64, :], ps[3][0:64, :], CS[0:64, ts], ALU.mult, [tkp[3], tk_cs], [tk_rs])
        P.tt("dve", rs[3][0:64, :], ps[4][0:64, :], SS[0:64, ts], ALU.mult, [tkp[4], tk_cs], [tk_rs])
        P.tt("dve", krT[0:64, ts], rs[2][0:64, :], rs[3][0:64, :], ALU.add, [tk_rs], [tk_kr])
    for t in range(16):
        pb = 6 + t % 2
        for kc in range(2):
            P.mm(ps[pb][:, :].rearrange("p (h d) -> p h d", h=4), ckvn[:, kc, t * 128:(t + 1) * 128],
                 wukv[:, kc, :].rearrange("p (h e) -> p h e", h=4)[:, :, 128:256], kc == 0, kc == 1,
                 [tk_ckvn, tk_wukv], [tkp[pb]])
        vv = Vaug.rearrange("p (h t) d -> p h t d", h=4)[:, :, t, 0:128]
        P.act(vv, ps[pb][:, :].rearrange("p (h d) -> p h d", h=4), AF.Copy, [tkp[pb]], [tk_v])
    tk_dummy = P.tok("dummy")
    P.call("dve", "memset", [tk_kvtmp, tk_sq2, tk_rstd2, tk_rs], list(tk_PT) + [tk_dummy], True, ap=smx[:, 15:16], constant=0.0)
    P.call("dve", "memset", [tk_tmp, tk_sq], tk_sqs + [tk_dummy], True, ap=smx[:, 15:16], constant=0.0)

    for h in range(4):
        P.dma("pool", wuqn, w_uq[:, h * 192:h * 192 + 128].rearrange("(kc p) f -> p kc f", p=128), writes=[tk_wuq])
        P.dma("pool", wuqA, w_uq[:, h * 192 + 128:h * 192 + 192].rearrange("(kc p) f -> p kc f", p=128), writes=[tk_wuq])
        P.dma("pool", wuqB[:, :, 0:32], w_uq[:, h * 192 + 160:h * 192 + 192].rearrange("(kc p) f -> p kc f", p=128),
              writes=[tk_wuq])
        P.dma("pool", wuqB[:, :, 32:64], w_uq[:, h * 192 + 128:h * 192 + 160].rearrange("(kc p) f -> p kc f", p=128),
              writes=[tk_wuq])
        for tt in range(4):
            ts = slice(tt * 512, (tt + 1) * 512)
            for kc in range(3):
                P.mm(ps[0][:, :], wuqn[:, kc, :], cqn[:, kc, ts], kc == 0, kc == 2, [tk_wuq, tk_cqn], [tkp[0]])
            P.act(qnT[:, ts], ps[0][:, :], AF.Copy, [tkp[0]], [tk_qn])
            for kc in range(3):
                P.mm(ps[3][0:64, :], wuqA[:, kc, :], cqn[:, kc, ts], kc == 0, kc == 2, [tk_wuq, tk_cqn], [tkp[3]])
            for kc in range(3):
                P.mm(ps[4][0:64, :], wuqB[:, kc, :], cqn[:, kc, ts], kc == 0, kc == 2, [tk_wuq, tk_cqn], [tkp[4]])
            P.tt("dve", t1[0:64, :], ps[3][0:64, :], CS[0:64, ts], ALU.mult, [tkp[3], tk_cs], [tk_t1])
            P.tt("dve", t2[0:64, :], ps[4][0:64, :], SS[0:64, ts], ALU.mult, [tkp[4], tk_cs], [tk_t2])
            P.tt("dve", qrT[0:64, ts], t1[0:64, :], t2[0:64, :], ALU.add, [tk_t1, tk_t2], [tk_qr])
            for kc in range(2):
                P.mm(ps[1][:, :], wukv[:, kc, h * 256:h * 256 + 128], ckvn[:, kc, ts], kc == 0, kc == 1,
                     [tk_wukv, tk_ckvn], [tkp[1]])
            P.act(knT[:, ts], ps[1][:, :], AF.Copy, [tkp[1]], [tk_kn])
        sbufs = (tmp[:, 0, :].bitcast(BF16), sq.bitcast(BF16))
        for tt in range(4):
            ts = slice(tt * 512, (tt + 1) * 512)
            for bi, (nT, rT, tkn, tkr, col) in enumerate(((qnT, qrT, tk_qn, tk_qr, tt), (knT, krT, tk_kn, tk_kr, 4 + tt))):
                s1 = sbufs[bi][:, 0:512]
                s2 = sbufs[bi][:, 512:1024]
                pb = 2 if bi == 0 else 7
                P.act(s1, nT[:, ts], AF.Square, [tkn], [tk_sqs[2 * bi]])
                P.act(s2[0:64, :], rT[0:64, ts], AF.Square, [tkr], [tk_sqs[2 * bi + 1]])
                P.mm(ps[pb][:, :], ONEb, s1, True, False, [tk_sqs[2 * bi], kc_], [tkp[pb]])
                P.mm(ps[pb][:, :], ONEb[0:64, :], s2[0:64, :], False, True, [tk_sqs[2 * bi + 1], kc_], [tkp[pb]])
                P.call("dve", "tensor_reduce", [tkp[pb]], [tk_smx], True, out=smx[:, col:col + 1], in_=ps[pb][:, :],
                       axis=AX.X, op=ALU.max)
        P.call("dve", "tensor_reduce", [tk_smx], [tk_smx], True, out=smx[:, 8:9], in_=smx[:, 0:4], axis=AX.X, op=ALU.max)
        P.call("dve", "tensor_reduce", [tk_smx], [tk_smx], True, out=smx[:, 9:10], in_=smx[:, 4:8], axis=AX.X, op=ALU.max)
        P.tt("dve", smx[:, 10:11], smx[:, 8:9], smx[:, 9:10], ALU.mult, [tk_smx], [tk_smx])
        P.act(smx[:, 11:12], smx[:, 10:11], AF.Sqrt, [tk_smx], [tk_smx])
        P.ts("dve", smx[:, 12:13], smx[:, 11:12], -1.05 * scale, None, ALU.mult, None, [tk_smx], [tk_smx])
        negm = smx[:, 12:13]
        for G in range(4):
            for j in range(4 * G + 4):
                qs = max(j * 128, G * 512)
                n = (G + 1) * 512 - qs
                off = qs - G * 512
                bank = j % 4
                diag = j >= 4 * G
                kt = slice(j * 128, (j + 1) * 128)
                P.mm(ps[bank][:, 0:n], knT[:, kt], qnT[:, qs:qs + n], True, False, [tk_kn, tk_qn], [tkp[bank]])
                P.mm(ps[bank][:, 0:n], krT[0:64, kt], qrT[0:64, qs:qs + n], False, not diag, [tk_kr, tk_qr], [tkp[bank]])
                if diag:
                    P.mm(ps[bank][:, 0:128], IDb, AMb, False, True, [kc_], [tkp[bank]])
                P.act(PT[:, j, off:off + n], ps[bank][:, 0:n], AF.Exp, [tkp[bank], tk_smx], [tk_PT[j]], bias=negm, scale=scale)
            def epilogue(qb):
                i = 4 * G + qb
                bank = 4 + qb % 2
                rc = smx[:, 13 + qb % 2:14 + qb % 2]
                on = o_n2[qb % 2]
                P.call("dve", "reciprocal", [tkp[bank]], [tk_smx], True, out=rc, in_=ps[bank][:, 128:129])
                P.ts("dve", on, ps[bank][:, 0:128], rc, None, ALU.mult, None, [tkp[bank], tk_smx], [tk_on2[qb % 2]])
                return i, on

            def transpose_out(i, on, qb):
                psb = ps[6 + qb % 2][:, 0:64].bitcast(BF16)
                P.tr(psb, on, IDb, [tk_on2[qb % 2], kc_], [tkp[6 + qb % 2]])
                P.act(o_mla[:, h, i * 128:(i + 1) * 128], psb, AF.Copy, [tkp[6 + qb % 2]], [tk_omla])

            pend = None
            for qb in range(4):
                i = 4 * G + qb
                bank = 4 + qb % 2
                for j in range(i + 1):
                    P.mm(ps[bank][:, 0:129], PT[:, j, qb * 128:(qb + 1) * 128], Vaug[:, h * 16 + j, 0:129], j == 0, j == i,
                         [tk_PT[j], tk_v], [tkp[bank]], signal=(j == i))
                if pend is not None:
                    transpose_out(*pend)
                i_, on_ = epilogue(qb)
                pend = (i_, on_, qb)
            transpose_out(*pend)


def merge_phase(cx, l, xbm, tk_xbm, outs, tk_outs, tk_xs, a1, a2):
    P, X, ps, tkp = cx.P, cx.X, cx.ps, cx.tk_ps
    kc_ = cx.tk_const
    w_in = cx.dr["w_in"][l]
    wsrc = (cx.dr["dn_w_o"][l], cx.dr["cv_w_pw2"][l], cx.dr["mla_w_o"][l])
    merged = a1.bf16(8, T)
    mark1 = a1.off
    wgate = [a2.bf16(3, 8, 128) for _ in range(2)]
    wbo = [a2.bf16(3, 4, 128) for _ in range(2)]
    gt = [a2.f32(512) for _ in range(3)]
    macc, tmpm = a2.f32(512), a2.f32(512)
    tk_wg, tk_wb = P.toks(2, "mwg"), P.toks(2, "mwb")
    tk_gt = P.toks(3, "gt")
    tk_macc, tk_tmpm, tk_mg = P.tok("macc"), P.tok("tmpm"), P.tok("merged")
    bg, bpw = cx.col("b_gate", l), cx.col("cv_b_pw2", l)
    for dc in range(8):
        s = dc % 2
        for i in range(3):
            c0 = C_GATE + i * 1024 + dc * 128
            wload(P, wgate[s][:, i], w_in[:, c0:c0 + 128], tk_wg[s])
            wload(P, wbo[s][:, i], wsrc[i][:, dc * 128:(dc + 1) * 128], tk_wb[s])
        for tt in range(4):
            ts = slice(tt * 512, (tt + 1) * 512)
            for i in range(3):
                for k in range(8):
                    P.mm(ps[i][:, :], wgate[s][:, i, k, :], xbm[:, k, ts], k == 0, k == 7, [tk_wg[s], tk_xbm], [tkp[i]],
                         signal=(k == 7))
                for k in range(4):
                    P.mm(ps[3 + i][:, :], wbo[s][:, i, k, :], outs[i][:, k, ts], k == 0, k == 3, [tk_wb[s], tk_outs[i]],
                         [tkp[3 + i]], signal=(k == 3))
            for i in range(3):
                P.act(gt[i], ps[i][:, :], AF.Sigmoid, [tkp[i], kc_], [tk_gt[i]], bias=bg[:, i * 8 + dc:i * 8 + dc + 1], scale=1.0)
            P.tt("dve", macc, gt[0], ps[3][:, :], ALU.mult, [tk_gt[0], tkp[3]], [tk_macc])
            P.stt("dve", tmpm, ps[4][:, :], bpw[:, dc:dc + 1], gt[1], ALU.add, ALU.mult, [tkp[4], tk_gt[1], kc_], [tk_tmpm])
            P.tt("dve", macc, macc, tmpm, ALU.add, [tk_tmpm], [tk_macc])
            P.tt("dve", tmpm, gt[2], ps[5][:, :], ALU.mult, [tk_gt[2], tkp[5]], [tk_tmpm])
            P.tt("dve", merged[:, dc, ts], macc, tmpm, ALU.add, [tk_macc, tk_tmpm], [tk_mg])
    P.barrier()
    for tt in range(4):
        ts = slice(tt * 512, (tt + 1) * 512)
        for c in range(8):
            P.dma("sp", X[:, c, ts], cx.dr["xs"][:, c, ts], reads=[tk_xs], writes=[cx.tk_X[c][tt]])
    wo = flat(xbm)[:, 0:8 * 1024].rearrange("p (k d) -> p k d", k=8)
    lntmp = a1.f32(4, 512)
    tk_wo = P.tok("wo")
    tk_ln = P.toks(4, "mln")
    w_out = cx.dr["w_out"][l]
    for d0 in range(0, 1024, 256):
        P.dma("pool", wo[:, :, d0:d0 + 256], w_out[:, d0:d0 + 256].rearrange("(kc p) f -> p kc f", p=128), writes=[tk_wo])
    lng, lnb = cx.col("ln2_g", l), cx.col("ln2_b", l)
    it = 0
    pending = None
    ln_gen = None
    for tt in range(4):
        ts = slice(tt * 512, (tt + 1) * 512)
        for dc in range(8):
            bank = it % 4
            it += 1
            for k in range(8):
                P.mm(ps[bank][:, :], wo[:, k, dc * 128:(dc + 1) * 128], merged[:, k, ts], k == 0, k == 7, [tk_wo, tk_mg],
                     [tkp[bank]], signal=(k == 7))
            P.stt("dve", X[:, dc, ts], X[:, dc, ts], ALPHA, ps[bank][:, :], ALU.mult, ALU.add, [tkp[bank]], [cx.tk_X[dc][tt]])
            if dc == 1 and pending is not None:
                ptt, pts = pending
                ln_gen = layernorm_stages(P, cx, X, [cx.tk_X[c][ptt] for c in range(8)], pts, lng, lnb, 8,
                                          [tkp[6], tkp[7]], ps[6], ps[7], lntmp, tk_ln)
                pending = None
            if dc >= 1 and dc % 2 == 1 and ln_gen is not None:
                if next(ln_gen, "done") == "done":
                    ln_gen = None
        if ln_gen is not None:
            for _ in ln_gen:
                pass
            ln_gen = None
        pending = (tt, ts)
    ptt, pts = pending
    emit_layernorm_tile(P, cx, X, [cx.tk_X[c][ptt] for c in range(8)], pts, lng, lnb, 8, [tkp[6], tkp[7]], ps[6], ps[7],
                        lntmp, tk_ln)


def emit_rsqrt(P, cx, out, in_, eps, reads, writes):
    P.act(out, in_, AF.Sqrt, reads, writes, bias=cx.epscol(eps), scale=1.0)
    P.call("dve", "reciprocal", writes, writes, True, out=out, in_=out)


def layernorm_stages(P, cx, Xt, xtoks, tcols, g_ap, b_ap, nch, ps_toks, ps_s1, ps_s2, tmp, tmp_toks, out_fn=None):
    n = tcols.stop - tcols.start
    inv = cx.ones_inv[nch]
    sq, mean, rstd, tt = (tmp[:, i, 0:n] for i in range(4))
    tk_sq, tk_mean, tk_rstd, tk_t = tmp_toks
    for c in range(nch):
        sqc, tkc = (sq, tk_sq) if c % 2 == 0 else (tt, tk_t)
        P.act(sqc, Xt[:, c, tcols], AF.Square, [xtoks[c]], [tkc])
        P.mm(ps_s1[:, 0:n], inv, Xt[:, c, tcols], c == 0, c == nch - 1, [xtoks[c], cx.tk_const], [ps_toks[0]])
        P.mm(ps_s2[:, 0:n], inv, sqc, c == 0, c == nch - 1, [tkc, cx.tk_const], [ps_toks[1]])
    yield
    P.act(mean, ps_s1[:, 0:n], AF.Copy, [ps_toks[0]], [tk_mean])
    P.tt("dve", rstd, mean, mean, ALU.mult, [tk_mean], [tk_rstd])
    P.tt("dve", rstd, ps_s2[:, 0:n], rstd, ALU.subtract, [ps_toks[1], tk_rstd], [tk_rstd])
    emit_rsqrt(P, cx, rstd, rstd, EPS, [tk_rstd], [tk_rstd])
    yield
    for c in range(nch):
        if c == nch // 2:
            yield
        P.tt("dve", tt, Xt[:, c, tcols], mean, ALU.subtract, [xtoks[c], tk_mean], [tk_t])
        P.tt("dve", tt, tt, rstd, ALU.mult, [tk_t, tk_rstd], [tk_t])
        if out_fn is None:
            P.act(Xt[:, c, tcols], tt, AF.Identity, [tk_t, cx.tk_const], [xtoks[c]],
                  bias=b_ap[:, c:c + 1], scale=g_ap[:, c:c + 1])
        else:
            out_fn(c, tt, tk_t)


def emit_layernorm_tile(*args, **kw):
    for _ in layernorm_stages(*args, **kw):
        pass


_CACHE = {}


def make_in_maps(inputs):
    consts = make_consts()
    cols = make_cols(inputs).build()
    x = np.asarray(inputs["x"], np.float32)
    pos = np.asarray(inputs["positions"], np.int32)
    shared = {"consts": consts, "cols": cols}
    for k in WEIGHTS:
        shared[k] = np.ascontiguousarray(np.asarray(inputs[k], np.float32))
    maps = []
    for b in range(8):
        m = dict(shared)
        m["xT"] = np.ascontiguousarray(x[b].reshape(T, 8, 128).transpose(2, 1, 0))
        m["pos"] = np.ascontiguousarray(pos[b].reshape(1, T))
        maps.append(m)
    return maps


def from_fm(a):
    return np.ascontiguousarray(a.transpose(2, 1, 0).reshape(T, D))


def kernel(**inputs):
    if "nc" not in _CACHE:
        _CACHE["nc"] = build_program()
    nc = _CACHE["nc"]
    maps = make_in_maps(inputs)
    res = run_bass_kernel_spmd(nc, maps, core_ids=list(range(8)))
    out = np.stack([from_fm(np.asarray(r["outT"])) for r in res.results], axis=0)
    return out.astype(np.float32)
```

```python
from contextlib import ExitStack
DN_STOP = 9
import numpy as np
import concourse.bass as bass
import concourse.mybir as mybir
from concourse.bass_utils import run_bass_kernel_spmd

F32 = mybir.dt.float32
BF16 = mybir.dt.bfloat16
I32 = mybir.dt.int32
AF = mybir.ActivationFunctionType
ALU = mybir.AluOpType
AX = mybir.AxisListType

D = 1024
T = 2048
DEPTH = 2
DFF = 2816
NFF = DFF // 128
DIN = 6856
ALPHA = (2 * DEPTH) ** 0.25
EPS = 1e-5
NEG = -30000.0

C_DQ, C_DK, C_DV, C_DZ, C_DA, C_DB = 0, 512, 1024, 1536, 2048, 2052
C_GLU, C_CQ, C_CKV, C_KR, C_GATE = 2056, 3080, 3464, 3720, 3784


class Tok:
    __slots__ = ("name", "lw", "rs", "sem", "semv", "excl")

    def __init__(self, name="", excl=False):
        self.name = name
        self.excl = excl
        self.lw = None
        self.rs = []
        self.sem = None
        self.semv = 0


ENGS = ("pe", "act", "dve", "pool", "sp")


class Prog:
    def __init__(self, nc, stack):
        self.nc = nc
        self.stack = stack
        self.ops = {e: [] for e in ENGS}
        self.cnt = {e: 0 for e in ENGS}
        self.sem = {e: stack.enter_context(nc.semaphore("s_" + e)) for e in ENGS}
        self.known = {e: {} for e in ENGS}
        self.semobj = {}
        for e in ENGS:
            self.semobj[e] = self.sem[e]
        self.free_dma_sems = []
        self.used_dma_sems = []
        self.gen = 0
        self.ndma = 0
        self.bar = stack.enter_context(nc.semaphore("s_bar"))
        self.barv = 0
        self.out_waits = []
        self._semv = {}

    def tok(self, name=""):
        return Tok(name)

    def toks(self, n, name=""):
        return [Tok(name + str(i)) for i in range(n)]

    def _deps(self, eng, reads, writes):
        deps = {}

        def add(d):
            if d is None:
                return
            k, v = d
            if deps.get(k, 0) < v:
                deps[k] = v

        for b in reads:
            add(b.lw)
            if b.excl:
                for r in b.rs:
                    if r[0] != eng:
                        add(r)
        for b in writes:
            add(b.lw)
            for r in b.rs:
                add(r)
        out = []
        kn = self.known[eng]
        for k, v in deps.items():
            if k == eng and eng == "pe":
                continue
            if kn.get(k, 0) >= v:
                continue
            kn[k] = v
            out.append((k, v))
        return out

    def call(self, eng, method, reads, writes, signal=True, **kw):
        kw = {k: v for k, v in kw.items() if v is not None}
        return self.op(eng, lambda e, m=method, kw=kw: getattr(e, m)(**kw), reads, writes, signal)

    def mm(self, out, lhsT, rhs, start, stop, reads, writes, signal=True):
        return self.call("pe", "matmul", reads, writes, signal, out=out, lhsT=lhsT, rhs=rhs, start=start, stop=stop)

    def tr(self, out, in_, identity, reads, writes):
        return self.call("pe", "transpose", reads, writes, True, out=out, in_=in_, identity=identity)

    def act(self, out, in_, func, reads, writes, bias=None, scale=None, accum_out=None, eng="act"):
        return self.call(eng, "activation", reads, writes, True, out=out, in_=in_, func=func, bias=bias, scale=scale,
                         accum_out=accum_out)

    def tt(self, eng, out, in0, in1, op, reads, writes):
        return self.call(eng, "tensor_tensor", reads, writes, True, out=out, in0=in0, in1=in1, op=op)

    def ts(self, eng, out, in0, s1, s2, op0, op1, reads, writes, accum_out=None):
        kw = dict(out=out, in0=in0, scalar1=s1, scalar2=s2, op0=op0)
        if op1 is not None:
            kw["op1"] = op1
        if accum_out is not None:
            kw["accum_out"] = accum_out
        return self.op(eng, lambda e, kw=kw: e.tensor_scalar(**kw), reads, writes, True)

    def stt(self, eng, out, in0, scalar, in1, op0, op1, reads, writes):
        return self.call(eng, "scalar_tensor_tensor", reads, writes, True, out=out, in0=in0, scalar=scalar, in1=in1,
                         op0=op0, op1=op1)

    def copy(self, eng, out, in_, reads, writes):
        return self.call(eng, "tensor_copy", reads, writes, True, out=out, in_=in_)

    def op(self, eng, fn, reads=(), writes=(), signal=True):
        waits = self._deps(eng, reads, writes)
        if signal:
            self.cnt[eng] += 1
            me = (eng, self.cnt[eng])
        else:
            me = (eng, self.cnt[eng] + 1)
        self.ops[eng].append((waits, fn, signal, None))
        for b in writes:
            b.lw = me
            b.rs = []
        for b in reads:
            if b not in writes:
                b.rs.append(me)
        return me

    def dma(self, q, out, in_, reads=(), writes=()):
        wt = writes[0]
        if wt.sem is None or wt.sem[1] != self.gen:
            free = [k for k in self.free_dma_sems if k.startswith(q)]
            if free:
                key = free[-1]
                self.free_dma_sems.remove(key)
            else:
                key = "%s_d%d" % (q, self.ndma)
                self.ndma += 1
                self.semobj[key] = self.stack.enter_context(self.nc.semaphore("s_" + key))
            wt.sem = (key, self.gen)
            wt.semv = self._semv.get(key, 0)
            self.used_dma_sems.append(key)
        assert wt.sem[0].startswith(q), "a token's DMAs must stay on one queue between barriers"
        waits = []
        deps = {}
        for b in reads:
            if b.lw is not None:
                deps[b.lw[0]] = max(deps.get(b.lw[0], 0), b.lw[1])
        for b in writes:
            if b.lw is not None and (b.sem is None or b.lw[0] != b.sem[0]):
                deps[b.lw[0]] = max(deps.get(b.lw[0], 0), b.lw[1])
            for r in b.rs:
                deps[r[0]] = max(deps.get(r[0], 0), r[1])
        kn = self.known[q]
        for k, v in deps.items():
            if kn.get(k, 0) >= v:
                continue
            kn[k] = v
            waits.append((k, v))
        wt.semv += 16
        me = (wt.sem[0], wt.semv)
        self._semv[wt.sem[0]] = wt.semv
        self.ops[q].append((waits, lambda e, o=out, i=in_: e.dma_start(out=o, in_=i), False, me))
        for b in writes:
            if b.lw is None or b.sem is None or b.lw[0] != b.sem[0]:
                b.rs = []
            b.lw = me
        for b in reads:
            b.rs.append(me)
        return me

    def barrier(self):
        self.barv += 1
        target = self.barv * len(ENGS)
        drain = [(k, self._semv[k]) for k in self._semv]
        for e in ENGS:
            own = [(e, self.cnt[e])] if (e != "sp" and self.cnt[e] > 0) else []
            self.ops[e].append(("bar", target, own + (drain if e == "sp" else []), None))
        for e in ENGS:
            for e2 in ENGS:
                self.known[e][e2] = self.cnt[e2]
            for k, v in drain:
                self.known[e][k] = v
        self.gen += 1
        self.free_dma_sems.extend(self.used_dma_sems)
        self.used_dma_sems = []

    def flush(self, final_waits=()):
        nc = self.nc
        ops = self.ops
        semobj = self.semobj
        sems = self.sem
        bar = self.bar

        def replay(eng_name):
            def body(e):
                for waits, fn, signal, dmasem in ops[eng_name]:
                    if waits == "bar":
                        for k, v in signal:
                            e.wait_ge(semobj[k], v)
                        e.sem_inc(bar, 1)
                        e.wait_ge(bar, fn)
                        continue
                    for k, v in waits:
                        e.wait_ge(semobj[k], v)
                    ins = fn(e)
                    if dmasem is not None:
                        ins.then_inc(semobj[dmasem[0]], 16)
                    elif signal:
                        ins.then_inc(sems[eng_name], 1)
                if eng_name == "sp":
                    for k, v in final_waits:
                        e.wait_ge(semobj[k], v)
            return body

        with nc.Block() as block:
            block.tensor(replay("pe"))
            block.scalar(replay("act"))
            block.vector(replay("dve"))
            block.gpsimd(replay("pool"))
            block.sync(replay("sp"))
        self.ops = {e: [] for e in ENGS}


def col_chunks(n, step):
    return [(i, min(step, n - i)) for i in range(0, n, step)]


class Ctx:
    pass


(K_ID, K_ONE, K_I1024, K_I512, K_I384, K_I256, K_I128, K_TRI, K_TRIU, K_MNEG, K_MNEGT, K_STRICT,
 K_NEGONE, K_CHUNK, K_AMASK, K_MISC, K_SEL63, K_SEL127) = range(18)
NCONST = 18


def make_consts():
    c = np.zeros((NCONST, 128, 128), np.float32)
    i = np.arange(128)[:, None]
    j = np.arange(128)[None, :]
    same = (i // 64) == (j // 64)
    c[K_ID] = np.eye(128)
    c[K_ONE] = 1.0
    c[K_I1024] = 1.0 / 1024
    c[K_I512] = 1.0 / 512
    c[K_I384] = 1.0 / 384
    c[K_I256] = 1.0 / 256
    c[K_I128] = 1.0 / 128
    c[K_TRI] = (same & (i <= j))
    c[K_TRIU] = (same & (i > j))
    c[K_MNEG] = np.where(same & (j <= i), 0.0, NEG)
    c[K_MNEGT] = np.where(same & (j >= i), 0.0, NEG)
    c[K_STRICT] = (same & (j < i))
    c[K_NEGONE] = -1.0
    c[K_CHUNK] = ((i // 64) == j)
    c[K_AMASK] = np.where(i <= j, 0.0, NEG)
    half = 32
    inv_freq = (10000.0 ** (-np.arange(half, dtype=np.float32) / half)).astype(np.float32)
    c[K_MISC][:64, 0] = np.concatenate([inv_freq, inv_freq])
    c[K_MISC][:64, 1] = np.concatenate([-np.ones(32), np.ones(32)])
    c[K_SEL63][63, :] = 1.0
    c[K_SEL127][127, :] = 1.0
    return np.ascontiguousarray(c.transpose(1, 0, 2))


class ColMap:
    def __init__(self):
        self.n = 0
        self.off = {}
        self.parts = []

    def add(self, name, arr2d):
        self.off[name] = (self.n, arr2d.shape[1])
        self.n += arr2d.shape[1]
        self.parts.append(np.asarray(arr2d, np.float32))

    def build(self):
        return np.ascontiguousarray(np.concatenate(self.parts, axis=1))


def chunked(v):
    return np.asarray(v).reshape(-1, 128).T


def make_cols(inp):
    cm = ColMap()
    for l in range(DEPTH):
        for nm in ("ln1_g", "ln1_b", "ln2_g", "ln2_b", "ln3_g", "ln3_b", "b_gate", "cv_glu_b", "cv_dw_b",
                   "cv_ln_g", "cv_ln_b", "cv_b_pw2", "mla_q_norm_w", "mla_kv_norm_w", "dn_norm_w"):
            cm.add("%s%d" % (nm, l), chunked(inp[nm][l]))
        cw = np.asarray(inp["dn_conv_w"][l])
        cm.add("dn_conv_w%d" % l, cw.reshape(4, 12, 128).transpose(2, 0, 1).reshape(128, 48))
        dw = np.asarray(inp["cv_dw_w"][l])
        cm.add("cv_dw_w%d" % l, dw.reshape(31, 4, 128).transpose(2, 0, 1).reshape(128, 124))
        cm.add("dn_a_log%d" % l, np.broadcast_to(np.asarray(inp["dn_a_log"][l])[None, :], (128, 4)))
        cm.add("dn_dt_bias%d" % l, np.broadcast_to(np.asarray(inp["dn_dt_bias"][l])[None, :], (128, 4)))
    return cm


def colmap_layout():
    fake = {
        "ln1_g": np.zeros((2, 1024)), "ln1_b": np.zeros((2, 1024)), "ln2_g": np.zeros((2, 1024)),
        "ln2_b": np.zeros((2, 1024)), "ln3_g": np.zeros((2, 1024)), "ln3_b": np.zeros((2, 1024)),
        "b_gate": np.zeros((2, 3072)), "cv_glu_b": np.zeros((2, 1024)), "cv_dw_b": np.zeros((2, 512)),
        "cv_ln_g": np.zeros((2, 512)), "cv_ln_b": np.zeros((2, 512)), "cv_b_pw2": np.zeros((2, 1024)),
        "mla_q_norm_w": np.zeros((2, 384)), "mla_kv_norm_w": np.zeros((2, 256)), "dn_norm_w": np.zeros((2, 128)),
        "dn_conv_w": np.zeros((2, 4, 1536)), "cv_dw_w": np.zeros((2, 31, 512)),
        "dn_a_log": np.zeros((2, 4)), "dn_dt_bias": np.zeros((2, 4)),
    }
    return make_cols(fake)


WEIGHTS = {
    "ffn1_w_gate": (DEPTH, D, DFF), "ffn1_w_up": (DEPTH, D, DFF), "ffn1_w_down": (DEPTH, DFF, D),
    "ffn2_w_gate": (DEPTH, D, DFF), "ffn2_w_up": (DEPTH, D, DFF), "ffn2_w_down": (DEPTH, DFF, D),
    "w_in": (DEPTH, D, DIN), "dn_w_o": (DEPTH, 512, D), "cv_w_pw2": (DEPTH, 512, D),
    "mla_w_uq": (DEPTH, 384, 768), "mla_w_ukv": (DEPTH, 256, 1024), "mla_w_o": (DEPTH, 512, D),
    "w_out": (DEPTH, D, D),
}


class Arena:
    def __init__(self, handle, nwords):
        self.h = handle
        self.n = nwords
        self.off = 0

    def reset(self):
        self.off = 0

    def f32(self, *shape):
        n = int(np.prod(shape))
        assert self.off + n <= self.n, ("arena overflow", self.off + n, self.n)
        v = self.h[:, self.off:self.off + n]
        self.off += n
        return self._shape(v, shape)

    def bf16(self, *shape):
        n = int(np.prod(shape))
        w = (n + 1) // 2
        assert self.off + w <= self.n, ("arena overflow", self.off + w, self.n)
        v = self.h[:, self.off:self.off + w].bitcast(BF16)
        if 2 * w != n:
            v = v[:, 0:n]
        self.off += w
        return self._shape(v, shape)

    @staticmethod
    def _shape(v, shape):
        if len(shape) == 1:
            return v
        if len(shape) == 2:
            return v.rearrange("p (a b) -> p a b", a=shape[0])
        if len(shape) == 3:
            return v.rearrange("p (a b c) -> p a b c", a=shape[0], b=shape[1])
        raise ValueError(shape)


AR_WORDS = 33200


def build_program(upto=None, dbg=None):
    nc = bass.Bass("TRN2", target_bir_lowering=False)
    stack = ExitStack()
    cx = Ctx()
    cx.nc = nc
    cx.dbg = dbg
    cx.cm = colmap_layout()
    dr = {}
    dr["xT"] = nc.dram_tensor("xT", [128, 8, T], F32, kind="ExternalInput").ap()
    dr["pos"] = nc.dram_tensor("pos", [1, T], I32, kind="ExternalInput").ap()
    dr["consts"] = nc.dram_tensor("consts", [128, NCONST, 128], F32, kind="ExternalInput").ap()
    dr["cols"] = nc.dram_tensor("cols", [128, cx.cm.n], F32, kind="ExternalInput").ap()
    for k, shp in WEIGHTS.items():
        dr[k] = nc.dram_tensor(k, list(shp), F32, kind="ExternalInput").ap()
    dr["outT"] = nc.dram_tensor("outT", [128, 8, T], F32, kind="ExternalOutput").ap()
    dr["xs"] = nc.dram_tensor("xs", [128, 8, T], F32).ap()
    if dbg is not None:
        dr["dbg"] = nc.dram_tensor("dbg", [128, 8, T], F32, kind="ExternalOutput").ap()
    cx.dr = dr

    X = stack.enter_context(nc.sbuf_tensor("X", [128, 8, T], F32))
    CST = stack.enter_context(nc.sbuf_tensor("CST", [128, NCONST, 128], F32))
    CSTB = stack.enter_context(nc.sbuf_tensor("CSTB", [128, 3, 128], BF16))
    COLS = stack.enter_context(nc.sbuf_tensor("COLS", [128, cx.cm.n], F32))
    ARH = stack.enter_context(nc.sbuf_tensor("AR", [128, AR_WORDS], F32))
    EPSC = stack.enter_context(nc.sbuf_tensor("EPSC", [128, 8], F32))
    cx.POSI = stack.enter_context(nc.sbuf_tensor("POSI", [64, 512], I32))
    cx.EPSC = EPSC
    cx.X, cx.CST, cx.CSTB, cx.COLS = X, CST, CSTB, COLS
    cx.ar = Arena(ARH, AR_WORDS)
    cx.ps = [stack.enter_context(nc.psum_tensor("ps%d" % i, [128, 512], F32)) for i in range(8)]

    P = Prog(nc, stack)
    cx.P = P
    cx.tk_ps = [Tok("ps%d" % i, excl=True) for i in range(8)]
    cx.tk_const = P.tok("const")
    cx.tk_X = [[P.tok("X%d_%d" % (c, t)) for t in range(4)] for c in range(8)]
    cx.tk_out = P.tok("out")
    cx.tk_dbg = P.tok("dbg")
    cx.ones_inv = {8: CST[:, K_I1024, :], 4: CST[:, K_I512, :], 3: CST[:, K_I384, :], 2: CST[:, K_I256, :],
                   1: CST[:, K_I128, :]}

    def col(name, l):
        o, n = cx.cm.off["%s%d" % (name, l)]
        return COLS[:, o:o + n]
    cx.col = col

    eps_vals = [EPS, 1e-6, 0.0, 1.0, -np.pi, 0.0, 0.0, 0.0]
    for i, v in enumerate(eps_vals):
        P.call("dve", "memset", [], [cx.tk_const], True, ap=EPSC[:, i:i + 1], constant=float(v))
    cx.epscol = lambda eps: EPSC[:, eps_vals.index(eps):eps_vals.index(eps) + 1]
    P.dma("sp", CST[:], dr["consts"], writes=[cx.tk_const])
    P.dma("sp", COLS[:], dr["cols"], writes=[cx.tk_const])
    for c in range(8):
        P.dma("sp", X[:, c, 0:512], dr["xT"][:, c, 0:512], writes=[cx.tk_X[c][0]])
    cx.pending_x = [1, 2, 3]
    P.copy("dve", CSTB[:, 0, :], CST[:, K_ID, :], [cx.tk_const], [cx.tk_const])
    P.copy("dve", CSTB[:, 1, :], CST[:, K_AMASK, :], [cx.tk_const], [cx.tk_const])
    P.copy("dve", CSTB[:, 2, :], CST[:, K_ONE, :], [cx.tk_const], [cx.tk_const])

    def done():
        return finish(cx, stack)

    for l in range(DEPTH):
        phase_ffn(cx, l, 1)
        if upto == ("ffn1", l):
            return done()
        phase_mixer(cx, l, upto)
        if upto is not None and upto[1] == l and upto[0] in ("dn", "cv", "mla", "mix"):
            return done()
        phase_ffn(cx, l, 2)
        if upto == ("ffn2", l):
            return done()
    return done()


def finish(cx, stack):
    P = cx.P
    P.barrier()
    last = None
    for c in range(8):
        last = P.dma("sp", cx.dr["outT"][:, c, :], cx.X[:, c, :], reads=cx.tk_X[c], writes=[cx.tk_out])
    fw = [last]
    if cx.dbg is not None and cx.tk_dbg.lw is not None:
        fw.append(cx.tk_dbg.lw)
    P.flush(final_waits=fw)
    stack.close()
    return cx.nc


def dbg_dump(cx, src_ap, reads, c0=0):
    n = src_ap.shape[-1]
    cx.P.dma("sp", cx.dr["dbg"][:, c0, 0:n], src_ap, reads=reads, writes=[cx.tk_dbg])


def phase_ffn(cx, l, which):
    P, X, ar = cx.P, cx.X, cx.ar
    P.barrier()
    ar.reset()
    pre = "ffn%d_" % which
    wg_d, wu_d, wd_d = cx.dr[pre + "w_gate"][l], cx.dr[pre + "w_up"][l], cx.dr[pre + "w_down"][l]
    lng = cx.col("ln1_g" if which == 1 else "ln3_g", l)
    lnb = cx.col("ln1_b" if which == 1 else "ln3_b", l)
    xb = ar.bf16(8, 512)
    hT = ar.bf16(NFF, 512)
    wd = ar.bf16(NFF, 1024)
    GW = 512
    groups = col_chunks(DFF, GW)
    wgu = [[ar.bf16(8, GW) for _ in range(2)] for _ in range(2)]
    sg = [ar.f32(512) for _ in range(2)]
    lntmp = ar.f32(4, 512)
    tk_xb, tk_wd = P.tok("xb"), P.tok("wd")
    tk_h = P.toks(NFF, "h")
    tk_w = [[P.tok("wg"), P.tok("wu")] for _ in range(2)]
    tk_sg = P.toks(2, "sg")
    tk_ln = P.toks(4, "ln")
    ps, tkp = cx.ps, cx.tk_ps

    wdv = wd_d.rearrange("(c p) d -> p c d", p=128)
    wd_loaded = [False]

    def load_wd():
        if not wd_loaded[0]:
            wd_loaded[0] = True
            for c0, n in col_chunks(NFF, 6):
                P.dma("pool", wd[:, c0:c0 + n, :], wdv[:, c0:c0 + n, :], writes=[tk_wd])

    gi = 0
    deferred = []
    ln_gen = None
    for tt in range(4):
        ts = slice(tt * 512, (tt + 1) * 512)
        for c in range(8):
            P.act(xb[:, c, :], X[:, c, ts], AF.Copy, [cx.tk_X[c][tt]], [tk_xb])
        for gidx, (f0, fn_) in enumerate(groups):
            if gidx == 1 and deferred:
                tt_, ts_ = deferred.pop(0)
                ln_gen = layernorm_stages(P, cx, X, [cx.tk_X[c][tt_] for c in range(8)], ts_, lng, lnb, 8,
                                          [tkp[6], tkp[7]], ps[6], ps[7], lntmp, tk_ln)
            if gidx >= 1 and ln_gen is not None:
                if next(ln_gen, "done") == "done":
                    ln_gen = None
            b = gi % 2
            gi += 1
            P.dma("pool", wgu[b][0][:, :, 0:fn_], wg_d[:, f0:f0 + fn_].rearrange("(kc p) f -> p kc f", p=128),
                  writes=[tk_w[b][0]])
            P.dma("pool", wgu[b][1][:, :, 0:fn_], wu_d[:, f0:f0 + fn_].rearrange("(kc p) f -> p kc f", p=128),
                  writes=[tk_w[b][1]])
            if gi == 2:
                for t_ in cx.pending_x:
                    for c in range(8):
                        P.dma("pool", X[:, c, t_ * 512:(t_ + 1) * 512], cx.dr["xT"][:, c, t_ * 512:(t_ + 1) * 512],
                              writes=[cx.tk_X[c][t_]])
                cx.pending_x = []
                load_wd()
            for ci in range(fn_ // 128):
                c = f0 // 128 + ci
                pb = (c % 2) * 2
                for k in range(8):
                    P.mm(ps[pb][:, :], wgu[b][0][:, k, ci * 128:(ci + 1) * 128], xb[:, k, :], k == 0, k == 7,
                         [tk_w[b][0], tk_xb], [tkp[pb]], signal=(k == 7))
                for k in range(8):
                    P.mm(ps[pb + 1][:, :], wgu[b][1][:, k, ci * 128:(ci + 1) * 128], xb[:, k, :], k == 0, k == 7,
                         [tk_w[b][1], tk_xb], [tkp[pb + 1]], signal=(k == 7))
                s = c % 2
                P.act(sg[s], ps[pb][:, :], AF.Silu, [tkp[pb]], [tk_sg[s]])
                P.stt("dve", hT[:, c, :], sg[s], 0.5, ps[pb + 1][:, :], ALU.mult, ALU.mult, [tk_sg[s], tkp[pb + 1]], [tk_h[c]])
        if ln_gen is not None:
            for _ in ln_gen:
                pass
            ln_gen = None
        for d in range(8):
            pb = 4 + d % 2
            for c in range(NFF):
                P.mm(ps[pb][:, :], wd[:, c, d * 128:(d + 1) * 128], hT[:, c, :], c == 0, c == NFF - 1,
                     [tk_wd, tk_h[c]], [tkp[pb]], signal=(c == NFF - 1))
            P.stt("dve", X[:, d, ts], X[:, d, ts], ALPHA, ps[pb][:, :], ALU.mult, ALU.add, [tkp[pb]], [cx.tk_X[d][tt]])
        deferred.append((tt, ts))
    for (tt_, ts_) in deferred:
        emit_layernorm_tile(P, cx, X, [cx.tk_X[c][tt_] for c in range(8)], ts_, lng, lnb, 8,
                            [tkp[6], tkp[7]], ps[6], ps[7], lntmp, tk_ln)


def wload(P, dst, src2d, tok, q="pool"):
    P.dma(q, dst, src2d.rearrange("(kc p) f -> p kc f", p=128), writes=[tok])


def flat(ap3):
    return ap3.rearrange("p a b -> p (a b)")


def phase_mixer(cx, l, upto):
    P, X, ar = cx.P, cx.X, cx.ar
    stage = upto[0] if (upto is not None and upto[1] == l) else None
    P.barrier()
    ar.reset()
    a2 = Arena(X[:, :, :].rearrange("p c t -> p (c t)"), 8 * T)
    xbm = ar.bf16(8, T)
    outs = [ar.bf16(4, T) for _ in range(3)]
    tk_xbm = P.tok("xbm")
    tk_outs = P.toks(3, "bo")
    tk_xs = P.tok("xs")
    mark = ar.off
    for c in range(8):
        P.act(xbm[:, c, :], X[:, c, :], AF.Copy, cx.tk_X[c], [tk_xbm])
        P.dma("sp", cx.dr["xs"][:, c, :], X[:, c, :], reads=cx.tk_X[c], writes=[tk_xs])
    P.barrier()

    def dump(idx):
        P.barrier()
        a2.reset()
        tmp = a2.f32(4, T)
        tk = P.tok("dmp")
        for c in range(4):
            P.act(tmp[:, c, :], outs[idx][:, c, :], AF.Copy, [tk_outs[idx]], [tk])
            P.dma("sp", cx.dr["dbg"][:, c, :], tmp[:, c, :], reads=[tk], writes=[cx.tk_dbg])
        P.barrier()

    if stage in (None, "dn", "mix"):
        ar.off = mark
        a2.reset()
        branch_dn(cx, l, xbm, tk_xbm, outs[0], tk_outs[0], ar, a2)
        if stage == "dn":
            dump(0)
    if stage in (None, "cv", "mix"):
        P.barrier()
        ar.off = mark
        a2.reset()
        branch_cv(cx, l, xbm, tk_xbm, outs[1], tk_outs[1], ar, a2)
        if stage == "cv":
            dump(1)
    if stage in (None, "mla", "mix"):
        P.barrier()
        ar.off = mark
        a2.reset()
        branch_mla(cx, l, xbm, tk_xbm, outs[2], tk_outs[2], ar, a2)
        if stage == "mla":
            dump(2)
    P.barrier()
    ar.off = mark
    a2.reset()
    if stage in ("dn", "cv", "mla"):
        for c in range(8):
            P.dma("sp", X[:, c, :], cx.dr["xs"][:, c, :], reads=[tk_xs], writes=cx.tk_X[c])
        return
    merge_phase(cx, l, xbm, tk_xbm, outs, tk_outs, tk_xs, ar, a2)


def branch_dn(cx, l, xbm, tk_xbm, o_dn, tk_odn, a1, a2):
    P, CST, ps, tkp = cx.P, cx.CST, cx.ps, cx.tk_ps
    kc_ = cx.tk_const
    w_in = cx.dr["w_in"][l]
    ID, ONE, NEGONE = CST[:, K_ID, :], CST[:, K_ONE, :], CST[:, K_NEGONE, :]
    TRI, TRIU, MNEG, MNEGT, STRICT = (CST[:, k, :] for k in (K_TRI, K_TRIU, K_MNEG, K_MNEGT, K_STRICT))
    wab = a1.bf16(8, 8)
    wq = a1.bf16(4, 8, 128)
    sm = {n: a1.f32(64) for n in ("xa", "ax", "e", "ln", "sp", "g", "beta", "nbeta", "gc", "egc", "bege", "ekd")}
    egl = a1.f32(2, 64)
    ea = a1.f32(4)
    v3 = lambda a: a.rearrange("p (t h) -> p t h", h=4)
    pres = [a2.f32(T + 4) for _ in range(3)]
    qT = a2.f32(T)
    kT = a2.f32(T)
    vT = a2.f32(T)
    k_tm = a2.bf16(16, 128)
    v_tm = a2.bf16(16, 128)
    oT = pres[1][:, 0:T]
    sq = qT[:, 0:512]
    rin = qT[:, 512:1024]
    qbf, kbf, vbf = a1.bf16(T), a1.bf16(T), a1.bf16(T)
    f32_names = ("GTri", "decay", "decayT", "egrow", "t1", "MTa", "MTb")
    bf_names = ("Nm", "NT", "P0", "P1", "PT0", "PT1", "TT0", "TT1", "vb", "kb", "u", "w", "qkT", "QeffT", "kdec")
    tmp_names = f32_names + bf_names
    tb = {}

    def pick(nw):
        return a1 if (a1.n - a1.off) >= nw else a2
    for n in f32_names:
        tb[n] = pick(512).f32(4, 128)
    for n in bf_names:
        tb[n] = pick(256).bf16(4, 128)
    Sbuf = [pick(128).f32(128) for _ in range(2)]
    Sbf = [pick(64).bf16(128) for _ in range(2)]
    tk = {n: P.tok(n) for n in tmp_names}
    tk_wab, tk_sm, tk_q, tk_k, tk_v, tk_ktm, tk_vtm, tk_sq, tk_rin = (
        P.tok(n) for n in ("wab", "sm", "q", "k", "v", "ktm", "vtm", "sq", "rin"))
    tk_pres = P.toks(3, "pre")
    tk_oT = tk_pres[1]
    tk_pre = tk_pres[0]
    tk_wq = P.toks(4, "wq")
    tk_S = P.toks(2, "S")
    tk_Sb = P.toks(2, "Sb")
    tk_qb, tk_kb, tk_vb = P.tok("qb"), P.tok("kb_"), P.tok("vb_")
    tk_sq = tk_rin = tk_q

    P.dma("pool", wab, w_in[:, C_DA:C_DA + 8].rearrange("(kc p) f -> p kc f", p=128), writes=[tk_wab])
    ab = ps[0][:, 0:128].rearrange("p (t e) -> p t e", e=8)
    for t in range(16):
        for k in range(8):
            P.mm(ab[:, t, :], xbm[:, k, t * 128:(t + 1) * 128], wab[:, k, :], k == 0, k == 7, [tk_xbm, tk_wab], [tkp[0]],
                 signal=(k == 7))
    dtb = cx.col("dn_dt_bias", l)
    alog = cx.col("dn_a_log", l)
    for h in range(4):
        P.ts("dve", v3(sm["xa"])[:, :, h], ab[:, :, h], dtb[:, h:h + 1], None, ALU.add, None, [tkp[0], kc_], [tk_sm])
    P.act(sm["ax"], sm["xa"], AF.Abs, [tk_sm], [tk_sm])
    P.act(sm["e"], sm["ax"], AF.Exp, [tk_sm], [tk_sm], scale=-1.0)
    P.act(sm["ln"], sm["e"], AF.Ln, [tk_sm], [tk_sm], bias=cx.epscol(1.0), scale=1.0)
    P.stt("dve", sm["sp"], sm["xa"], 0.0, sm["ln"], ALU.max, ALU.add, [tk_sm], [tk_sm])
    P.act(ea, alog, AF.Exp, [kc_], [tk_sm])
    for h in range(4):
        P.ts("dve", v3(sm["g"])[:, :, h], v3(sm["sp"])[:, :, h], ea[:, h:h + 1], -1.0, ALU.mult, ALU.mult, [tk_sm], [tk_sm])
    P.act(v3(sm["beta"]), ab[:, :, 4:8], AF.Sigmoid, [tkp[0]], [tk_sm])
    P.ts("dve", sm["nbeta"], sm["beta"], -1.0, None, ALU.mult, None, [tk_sm], [tk_sm])
    P.mm(ps[1][:, 0:64], TRI, sm["g"], True, True, [tk_sm, kc_], [tkp[1]])
    P.mm(ps[1][:, 64:128], TRIU, sm["g"], True, True, [tk_sm, kc_], [tkp[1]])
    P.act(sm["gc"], ps[1][:, 0:64], AF.Copy, [tkp[1]], [tk_sm])
    P.act(sm["egc"], ps[1][:, 0:64], AF.Exp, [tkp[1]], [tk_sm])
    P.act(sm["ekd"], ps[1][:, 64:128], AF.Exp, [tkp[1]], [tk_sm])
    P.tt("dve", sm["bege"], sm["beta"], sm["egc"], ALU.mult, [tk_sm], [tk_sm])
    P.mm(ps[1][:, 128:192], CST[:, K_SEL63, :], sm["gc"], True, True, [tk_sm, kc_], [tkp[1]])
    P.mm(ps[1][:, 192:256], CST[:, K_SEL127, :], sm["gc"], True, True, [tk_sm, kc_], [tkp[1]])
    P.act(flat(egl), ps[1][:, 128:256], AF.Exp, [tkp[1]], [tk_sm])

    cw = cx.col("dn_conv_w", l)
    normw = cx.col("dn_norm_w", l)
    for h in range(4):
        for s_, c0 in enumerate((C_DQ, C_DK, C_DV, C_DZ)):
            wload(P, wq[:, s_], w_in[:, c0 + h * 128:c0 + (h + 1) * 128], tk_wq[s_])
        dsts = ((qT, tk_q), (kT, tk_k), (vT, tk_v))
        for s_ in range(3):
            P.call("dve", "memset", [], [tk_pres[s_]], True, ap=pres[s_][:, 0:3], constant=0.0)
            for tt in range(4):
                ts = slice(tt * 512, (tt + 1) * 512)
                bank = 6 + tt % 2
                for k in range(8):
                    P.mm(ps[bank][:, :], wq[:, s_, k, :], xbm[:, k, ts], k == 0, k == 7, [tk_wq[s_], tk_xbm], [tkp[bank]],
                         signal=(k == 7))
                P.act(pres[s_][:, 3 + tt * 512:3 + (tt + 1) * 512], ps[bank][:, :], AF.Copy, [tkp[bank]], [tk_pres[s_]])
        for s_ in range(3):
            cc = s_ * 4 + h
            dst, tkd = dsts[s_]
            pre = pres[s_]
            P.ts("dve", dst, pre[:, 0:T], cw[:, cc:cc + 1], None, ALU.mult, None, [tk_pres[s_], kc_], [tkd])
            for j in range(1, 4):
                P.stt("dve", dst, pre[:, j:j + T], cw[:, j * 12 + cc:j * 12 + cc + 1], dst, ALU.mult, ALU.add,
                      [tk_pres[s_], kc_], [tkd])
            if s_ < 2:
                P.act(dst, dst, AF.Silu, [], [tkd])
            else:
                P.act(vbf, dst, AF.Silu, [tkd], [tk_vb])
        for s_ in range(2):
            dst, tkd = dsts[s_]
            scr = pres[s_][:, 0:T]
            tks = tk_pres[s_]
            P.act(scr, dst, AF.Square, [tkd], [tks])
            for tt in range(4):
                ts = slice(tt * 512, (tt + 1) * 512)
                P.mm(ps[4 + tt][:, :], ONE, scr[:, ts], True, True, [tks, kc_], [tkp[4 + tt]])
            for tt in range(4):
                ts = slice(tt * 512, (tt + 1) * 512)
                P.act(scr[:, ts], ps[4 + tt][:, :], AF.Sqrt, [tkp[4 + tt], kc_], [tks], bias=cx.epscol(1e-6), scale=1.0)
            P.call("dve", "reciprocal", [tks], [tks], True, out=scr, in_=scr)
            if s_ == 0:
                P.stt("dve", qbf, dst, 128.0 ** -0.5, scr, ALU.mult, ALU.mult, [tks, tkd], [tk_qb])
            else:
                P.tt("dve", kbf, dst, scr, ALU.mult, [tks, tkd], [tk_kb])
        IDb_ = cx.CSTB[:, 0, :]
        for src_, dstm, tks, tkd in ((kbf, k_tm, tk_kb, tk_ktm), (vbf, v_tm, tk_vb, tk_vtm)):
            for g4 in range(4):
                bank = 6 + g4 % 2
                psb_ = ps[bank][:, 0:256].bitcast(BF16)
                for q in range(4):
                    t = g4 * 4 + q
                    P.tr(psb_[:, q * 128:(q + 1) * 128], src_[:, t * 128:(t + 1) * 128], IDb_, [tks, kc_], [tkp[bank]])
                P.act(flat(dstm[:, g4 * 4:(g4 + 1) * 4, :]), psb_, AF.Copy, [tkp[bank]], [tkd])
        P.call("dve", "memset", [], [tk_S[0]], True, ap=Sbuf[0], constant=0.0)
        P.call("dve", "memset", [], [tk_Sb[0]], True, ap=Sbf[0], constant=0.0)
        cur = 0
        IDb = cx.CSTB[:, 0, :]
        bc4 = lambda ap2: ap2.unsqueeze(1).to_broadcast([128, 4, 128])
        for G in range(4):
            gs = slice(G * 512, (G + 1) * 512)
            F = {n: flat(tb[n]) for n in tmp_names}
            ps4b = ps[4][:, 0:256].bitcast(BF16)
            for q in range(4):
                t = G * 4 + q
                gcol = v3(sm["g"])[:, t, h:h + 1]
                P.ts("dve", tb["GTri"][:, q, :], TRI, gcol, None, ALU.mult, None, [tk_sm, kc_], [tk["GTri"]])
            for q in range(4):
                t = G * 4 + q
                tq = slice(q * 128, (q + 1) * 128)
                tcol = slice(t * 128, (t + 1) * 128)
                P.mm(ps[2][:, tq], ONE, tb["GTri"][:, q, :], True, True, [tk["GTri"], kc_], [tkp[2]])
                P.mm(ps[3][:, tq], kbf[:, tcol], kbf[:, tcol], True, True, [tk_kb], [tkp[3]])
            P.act(F["t1"], ps[2][:, :], AF.Copy, [tkp[2]], [tk["t1"]])
            P.act(F["egrow"], F["t1"], AF.Exp, [tk["t1"]], [tk["egrow"]])
            for q in range(4):
                t = G * 4 + q
                gccol = v3(sm["gc"])[:, t, h:h + 1]
                P.act(tb["decay"][:, q, :], tb["t1"][:, q, :], AF.Identity, [tk["t1"], tk_sm], [tk["decay"]], bias=gccol, scale=-1.0)
                P.stt("dve", tb["decayT"][:, q, :], tb["t1"][:, q, :], gccol, MNEGT, ALU.subtract, ALU.min,
                      [tk["t1"], tk_sm, kc_], [tk["decayT"]])
            for q in range(4):
                P.tt("dve", tb["decay"][:, q, :], tb["decay"][:, q, :], MNEG, ALU.min, [kc_], [tk["decay"]])
            P.act(F["decayT"], F["decayT"], AF.Exp, [], [tk["decayT"]])
            P.act(F["decay"], F["decay"], AF.Exp, [], [tk["decay"]])
            P.tt("dve", F["t1"], ps[3][:, :], F["decay"], ALU.mult, [tkp[3], tk["decay"]], [tk["t1"]])
            if DN_STOP <= 1:
                continue
            for q in range(4):
                t = G * 4 + q
                nbcol = v3(sm["nbeta"])[:, t, h:h + 1]
                P.stt("dve", tb["Nm"][:, q, :], tb["t1"][:, q, :], nbcol, STRICT, ALU.mult, ALU.mult,
                      [tk["t1"], tk_sm, kc_], [tk["Nm"]])
            for q in range(4):
                P.tr(ps4b[:, q * 128:(q + 1) * 128], tb["Nm"][:, q, :], IDb, [tk["Nm"], kc_], [tkp[4]])
            P.act(F["NT"], ps4b, AF.Copy, [tkp[4]], [tk["NT"]])
            for q in range(4):
                P.tt("dve", tb["TT0"][:, q, :], tb["NT"][:, q, :], ID, ALU.add, [tk["NT"], kc_], [tk["TT0"]])
            if DN_STOP <= 2:
                continue
            Pc, PTc, TTc = "Nm", "NT", "TT0"
            for s_ in range(1, 6):
                Pn, PTn, TTn = ("P%d" % (s_ % 2)), ("PT%d" % (s_ % 2)), ("TT%d" % (s_ % 2))
                for q in range(4):
                    tq = slice(q * 128, (q + 1) * 128)
                    P.mm(ps[5][:, tq], tb[PTc][:, q, :], tb[Pc][:, q, :], True, True, [tk[PTc], tk[Pc]], [tkp[5]])
                if s_ < 5:
                    for q in range(4):
                        tq = slice(q * 128, (q + 1) * 128)
                        P.mm(ps[6][:, tq], tb[Pc][:, q, :], tb[PTc][:, q, :], True, True, [tk[PTc], tk[Pc]], [tkp[6]])
                P.act(F[Pn], ps[5][:, :], AF.Copy, [tkp[5]], [tk[Pn]])
                if s_ < 5:
                    P.copy("dve", F[PTn], ps[6][:, :], [tkp[6]], [tk[PTn]])
                for q in range(4):
                    tq = slice(q * 128, (q + 1) * 128)
                    P.mm(ps[7][:, tq], tb[Pn][:, q, :], tb[TTc][:, q, :], True, True, [tk[Pn], tk[TTc]], [tkp[7]])
                P.tt("dve", F[TTn], F[TTc], ps[7][:, :], ALU.add, [tk[TTc], tkp[7]], [tk[TTn]])
                Pc, PTc, TTc = Pn, PTn, TTn
            TT = tb[TTc]
            tkTT = tk[TTc]
            if DN_STOP <= 3:
                continue
            for q in range(4):
                t = G * 4 + q
                P.act(tb["vb"][:, q, :], v_tm[:, t, :], AF.Copy, [tk_vtm, tk_sm], [tk["vb"]], scale=v3(sm["beta"])[:, t, h:h + 1])
                P.act(tb["kb"][:, q, :], k_tm[:, t, :], AF.Copy, [tk_ktm, tk_sm], [tk["kb"]], scale=v3(sm["bege"])[:, t, h:h + 1])
                P.act(tb["kdec"][:, q, :], k_tm[:, t, :], AF.Copy, [tk_ktm, tk_sm], [tk["kdec"]], scale=v3(sm["ekd"])[:, t, h:h + 1])
            for q in range(4):
                t = G * 4 + q
                tq = slice(q * 128, (q + 1) * 128)
                tcol = slice(t * 128, (t + 1) * 128)
                P.mm(ps[0][:, tq], TT[:, q, :], tb["vb"][:, q, :], True, True, [tkTT, tk["vb"]], [tkp[0]])
                P.mm(ps[1][:, tq], TT[:, q, :], tb["kb"][:, q, :], True, True, [tkTT, tk["kb"]], [tkp[1]])
                P.mm(ps[2][:, tq], kbf[:, tcol], qbf[:, tcol], True, True, [tk_kb, tk_qb], [tkp[2]])
            P.act(F["u"], ps[0][:, :], AF.Copy, [tkp[0]], [tk["u"]])
            P.act(F["w"], ps[1][:, :], AF.Copy, [tkp[1]], [tk["w"]])
            P.tt("dve", F["qkT"], ps[2][:, :], F["decayT"], ALU.mult, [tkp[2], tk["decayT"]], [tk["qkT"]])
            if DN_STOP <= 4:
                continue
            for q in range(4):
                tq = slice(q * 128, (q + 1) * 128)
                P.mm(ps[3][:, tq], tb["w"][:, q, :], tb["qkT"][:, q, :], True, True, [tk["w"], tk["qkT"]], [tkp[3]])
            for q in range(4):
                tq = slice(q * 128, (q + 1) * 128)
                for ch in range(2):
                    r = slice(ch * 64, ch * 64 + 64)
                    P.mm(ps[4 + ch][:, tq], tb["w"][r, q, :], tb["kdec"][r, q, :], True, True, [tk["w"], tk["kdec"]], [tkp[4 + ch]])
            P.tt("dve", F["t1"], qbf[:, gs], F["egrow"], ALU.mult, [tk_qb, tk["egrow"], tk["Nm"]], [tk["t1"]])
            P.tt("dve", F["QeffT"], F["t1"], ps[3][:, :], ALU.subtract, [tkp[3], tk["t1"]], [tk["QeffT"]])
            for q in range(4):
                t = G * 4 + q
                tq = slice(q * 128, (q + 1) * 128)
                for ch in range(2):
                    eglcol = egl[:, ch, t * 4 + h:t * 4 + h + 1]
                    nm = "MTa" if ch == 0 else "MTb"
                    P.stt("dve", tb[nm][:, q, :], ID, eglcol, ps[4 + ch][:, tq], ALU.mult, ALU.subtract,
                          [tkp[4 + ch], tk_sm, kc_], [tk[nm]])
            if DN_STOP <= 5:
                continue
            for q in range(4):
                for ch in range(2):
                    r = slice(ch * 64, ch * 64 + 64)
                    tc = slice(q * 128 + ch * 64, q * 128 + ch * 64 + 64)
                    nm = "MTa" if ch == 0 else "MTb"
                    S_c, S_n = Sbuf[cur], Sbuf[1 - cur]
                    sb = 6 + cur
                    P.mm(ps[sb][:, 0:128], tb[nm][:, q, :], S_c, True, False, [tk[nm], tk_S[cur]], [tkp[sb]])
                    P.mm(ps[sb][:, 0:128], tb["kdec"][r, q, :], tb["u"][r, q, :], False, True, [tk["kdec"], tk["u"]], [tkp[sb]])
                    P.act(S_n, ps[sb][:, 0:128], AF.Copy, [tkp[sb]], [tk_S[1 - cur]])
                    P.copy("dve", Sbf[1 - cur], S_n, [tk_S[1 - cur]], [tk_Sb[1 - cur]])
                    P.mm(ps[3][:, tc], Sbf[cur], F["QeffT"][:, tc], True, False, [tk_Sb[cur], tk["QeffT"]], [tkp[3]])
                    P.mm(ps[3][:, tc], tb["u"][r, q, :], tb["qkT"][r, q, ch * 64:ch * 64 + 64], False, True,
                         [tk["u"], tk["qkT"]], [tkp[3]])
                    cur = 1 - cur
            P.act(oT[:, gs], ps[3][:, :], AF.Copy, [tkp[3]], [tk_oT])
        zs = pres[0][:, 0:T]
        sqf, rinf = qT, kT
        for tt in range(4):
            ts = slice(tt * 512, (tt + 1) * 512)
            bank = 6 + tt % 2
            for k in range(8):
                P.mm(ps[bank][:, :], wq[:, 3, k, :], xbm[:, k, ts], k == 0, k == 7, [tk_wq[3], tk_xbm], [tkp[bank]],
                     signal=(k == 7))
            P.act(zs[:, ts], ps[bank][:, :], AF.Silu, [tkp[bank]], [tk_pre])
        P.act(sqf, oT, AF.Square, [tk_oT], [tk_q])
        for tt in range(4):
            ts = slice(tt * 512, (tt + 1) * 512)
            P.mm(ps[tt][:, :], CST[:, K_I128, :], sqf[:, ts], True, True, [tk_q, kc_], [tkp[tt]])
        for tt in range(4):
            ts = slice(tt * 512, (tt + 1) * 512)
            P.act(rinf[:, ts], ps[tt][:, :], AF.Sqrt, [tkp[tt], kc_], [tk_k], bias=cx.epscol(EPS), scale=1.0)
        P.call("dve", "reciprocal", [tk_k], [tk_k], True, out=rinf, in_=rinf)
        P.tt("dve", rinf, oT, rinf, ALU.mult, [tk_oT, tk_k], [tk_k])
        P.stt("dve", o_dn[:, h, :], rinf, normw[:, 0:1], zs, ALU.mult, ALU.mult, [tk_k, tk_pre, kc_], [tk_odn])


def branch_cv(cx, l, xbm, tk_xbm, h_cv, tk_hcv, a1, a2):
    P, CST, ps, tkp = cx.P, cx.CST, cx.ps, cx.tk_ps
    kc_ = cx.tk_const
    w_in = cx.dr["w_in"][l]
    PADW = T + 32
    hpad = a2.bf16(4, PADW)
    Dg = a2.bf16(124, 128)
    wa = [a2.bf16(8, 128) for _ in range(2)]
    wg = [a2.bf16(8, 128) for _ in range(2)]
    ta = [a2.f32(512) for _ in range(2)]
    tsg = [a2.f32(512) for _ in range(2)]
    acc = a1.f32(4, T)
    lntmp = a1.f32(4, 512)
    tk_hp = P.toks(4, "hp")
    tk_acc = P.toks(4, "cacc")
    tk_wa, tk_wg = P.toks(2, "wa"), P.toks(2, "wg")
    tk_ta, tk_tsg = P.toks(2, "ta"), P.toks(2, "tsg")
    tk_ln = P.toks(4, "cln")
    tk_dg = P.tok("dg")
    glub = cx.col("cv_glu_b", l)
    dww = cx.col("cv_dw_w", l)
    dwb = cx.col("cv_dw_b", l)
    lng, lnb = cx.col("cv_ln_g", l), cx.col("cv_ln_b", l)
    ID = CST[:, K_ID, :]
    for idx in range(124):
        P.ts("dve", Dg[:, idx, :], ID, dww[:, idx:idx + 1], None, ALU.mult, None, [kc_], [tk_dg])
    it = 0
    ci = 0
    for cc in range(4):
        s = cc % 2
        wload(P, wa[s], w_in[:, C_GLU + cc * 128:C_GLU + (cc + 1) * 128], tk_wa[s])
        wload(P, wg[s], w_in[:, C_GLU + 512 + cc * 128:C_GLU + 512 + (cc + 1) * 128], tk_wg[s])
        P.call("dve", "memset", [], [tk_hp[cc]], True, ap=hpad[:, cc, 0:30], constant=0.0)
        for tt in range(4):
            ts = slice(tt * 512, (tt + 1) * 512)
            b0 = (it % 2) * 2
            u = it % 2
            it += 1
            for k in range(8):
                P.mm(ps[b0][:, :], wa[s][:, k, :], xbm[:, k, ts], k == 0, k == 7, [tk_wa[s], tk_xbm], [tkp[b0]], signal=(k == 7))
            for k in range(8):
                P.mm(ps[b0 + 1][:, :], wg[s][:, k, :], xbm[:, k, ts], k == 0, k == 7, [tk_wg[s], tk_xbm], [tkp[b0 + 1]],
                     signal=(k == 7))
            P.act(ta[u], ps[b0][:, :], AF.Identity, [tkp[b0], kc_], [tk_ta[u]], bias=glub[:, cc:cc + 1], scale=1.0)
            P.act(tsg[u], ps[b0 + 1][:, :], AF.Sigmoid, [tkp[b0 + 1], kc_], [tk_tsg[u]], bias=glub[:, 4 + cc:5 + cc], scale=1.0)
            P.tt("dve", hpad[:, cc, 30 + tt * 512:30 + (tt + 1) * 512], ta[u], tsg[u], ALU.mult, [tk_ta[u], tk_tsg[u]],
                 [tk_hp[cc]])
        for tt in range(4):
            ts = slice(tt * 512, (tt + 1) * 512)
            bank = 4 + ci % 2
            ci += 1
            for j in range(31):
                P.mm(ps[bank][:, :], Dg[:, j * 4 + cc, :], hpad[:, cc, j + tt * 512:j + tt * 512 + 512], j == 0, j == 30,
                     [tk_dg, tk_hp[cc]], [tkp[bank]], signal=(j == 30))
            P.act(acc[:, cc, ts], ps[bank][:, :], AF.Identity, [tkp[bank], kc_], [tk_acc[cc]], bias=dwb[:, cc:cc + 1], scale=1.0)
    for tt in range(4):
        ts = slice(tt * 512, (tt + 1) * 512)

        def out_fn(c, t_ap, tk_t, ts=ts):
            P.act(h_cv[:, c, ts], t_ap, AF.Silu, [tk_t, kc_], [tk_hcv], bias=lnb[:, c:c + 1], scale=lng[:, c:c + 1])
        emit_layernorm_tile(P, cx, acc, tk_acc, ts, lng, lnb, 4, [tkp[6], tkp[7]], ps[6], ps[7], lntmp, tk_ln, out_fn)


def branch_mla(cx, l, xbm, tk_xbm, o_mla, tk_omla, a1, a2):
    P, CST, CSTB, ps, tkp = cx.P, cx.CST, cx.CSTB, cx.ps, cx.tk_ps
    kc_ = cx.tk_const
    w_in, w_uq, w_ukv = cx.dr["w_in"][l], cx.dr["mla_w_uq"][l], cx.dr["mla_w_ukv"][l]
    IDb, AMb, ONEb = CSTB[:, 0, :], CSTB[:, 1, :], CSTB[:, 2, :]
    scale = 192.0 ** -0.5
    cqn = a2.bf16(3, T)
    ckvn = a2.bf16(2, T)
    CS = a2.f32(T)
    SS = a2.f32(T)
    krT = a2.bf16(T)
    Vaug = a2.bf16(64, 130)
    qnT, qrT, knT = a1.bf16(T), a1.bf16(T), a1.bf16(T)
    PT = a1.bf16(16, 512)
    wc = [a1.bf16(8, 128) for _ in range(2)]
    wuqn, wuqA, wuqB = a1.bf16(3, 128), a1.bf16(3, 64), a1.bf16(3, 64)
    wukv = a1.bf16(2, 1024)
    wkrA, wkrB = a1.bf16(8, 64), a1.bf16(8, 64)
    tmp = a1.f32(3, 512)
    sq, rstd = a1.f32(512), a1.f32(512)
    o_n = a2.bf16(128)
    o_n2 = [o_n, rstd.bitcast(BF16)[:, 0:128]]
    smx = a2.f32(16)
    posi = tmp[:, 0, :].bitcast(I32)
    t1, t2 = tmp[:, 1, :], tmp[:, 2, :]
    (tk_cqn, tk_ckvn, tk_cs, tk_kr, tk_v, tk_qn, tk_qr, tk_kn, tk_tmp, tk_sq, tk_rstd, tk_on, tk_smx, tk_wuq,
     tk_wukv, tk_wkr, tk_t1, tk_t2, tk_pos) = (P.tok(n) for n in (
         "cqn", "ckvn", "cs", "kr", "v", "qn", "qr", "kn", "tmp", "sq", "rstd", "on", "smx", "wuq", "wukv", "wkr",
         "t1", "t2", "pos"))
    tk_t1 = tk_t2 = tk_pos = tk_tmp
    tk_wc = P.toks(2, "wc")
    tk_PT = P.toks(16, "PT")
    tk_on2 = [tk_on, tk_rstd]
    tk_sqs = [P.tok("sqs0"), P.tok("sqs1"), P.tok("sqs2"), P.tok("sqs3")]
    qw, kvw = cx.col("mla_q_norm_w", l), cx.col("mla_kv_norm_w", l)
    invf, sgn = CST[:, K_MISC, 0:1], CST[:, K_MISC, 1:2]

    wload(P, wukv, w_ukv[:, :], tk_wukv)
    wload(P, wkrA, w_in[:, C_KR:C_KR + 64], tk_wkr)
    P.dma("pool", wkrB[:, :, 0:32], w_in[:, C_KR + 32:C_KR + 64].rearrange("(kc p) f -> p kc f", p=128), writes=[tk_wkr])
    P.dma("pool", wkrB[:, :, 32:64], w_in[:, C_KR:C_KR + 32].rearrange("(kc p) f -> p kc f", p=128), writes=[tk_wkr])
    P.call("dve", "memset", [], [tk_v], True, ap=Vaug[:, :, 128:130], constant=1.0)

    PTf = flat(PT).bitcast(F32)
    kvtmp = PTf[:, 0:1024].rearrange("p (a b) -> p a b", a=2)
    sq2, rstd2 = PTf[:, 1024:1536], PTf[:, 1536:2048]
    rs = [PTf[:, 2048 + i_ * 512:2048 + (i_ + 1) * 512] for i_ in range(4)]
    tk_kvtmp, tk_sq2, tk_rstd2 = P.tok("kvtmp"), P.tok("sq2"), P.tok("rstd2")
    tk_rs = P.tok("rs")
    def rope_tile(tt):
        ts = slice(tt * 512, (tt + 1) * 512)
        pi_, y_, A_, B_ = cx.POSI[:, :], rs[1][0:64, :], rs[2][0:64, :], rs[3][0:64, :]
        P.dma("sp", pi_, cx.dr["pos"][:, ts].partition_broadcast(64), reads=[tk_rs], writes=[tk_pos])
        P.copy("dve", y_, pi_, [tk_pos], [tk_rs])
        P.ts("dve", y_, y_, invf[0:64, :], None, ALU.mult, None, [kc_], [tk_rs])
        P.ts("dve", y_, y_, float(1.0 / (2 * np.pi)), None, ALU.mult, None, [], [tk_rs])
        P.copy("dve", pi_, y_, [tk_pos], [tk_rs])
        P.copy("dve", A_, pi_, [], [tk_rs])
        P.tt("dve", y_, y_, A_, ALU.subtract, [], [tk_rs])
        for (dstT, shift) in ((SS, 0.0), (CS, 0.25)):
            if shift != 0.0:
                P.ts("dve", y_, y_, shift, None, ALU.add, None, [], [tk_rs])
            P.ts("dve", A_, y_, 0.5, None, ALU.is_gt, None, [], [tk_rs])
            P.tt("dve", B_, y_, A_, ALU.subtract, [], [tk_rs])
            P.ts("dve", A_, y_, -0.5, None, ALU.is_lt, None, [], [tk_rs])
            P.tt("dve", B_, B_, A_, ALU.add, [], [tk_rs])
            P.act(dstT[0:64, ts], B_, AF.Sin, [tk_rs], [tk_cs], scale=float(2 * np.pi))
        P.ts("dve", SS[0:64, ts], SS[0:64, ts], sgn[0:64, :], None, ALU.mult, None, [kc_], [tk_cs])
    wi = 0
    for tt in range(4):
        ts = slice(tt * 512, (tt + 1) * 512)
        rope_tile(tt)
        for (c0, nch, dstn, tkd, wcol, inv, tbuf, tkt, sq_, tksq, rstd_, tkr, pb) in (
                (C_CQ, 3, cqn, tk_cqn, qw, CST[:, K_I384, :], tmp, tk_tmp, sq, tk_sq, rstd, tk_rstd, 2),
                (C_CKV, 2, ckvn, tk_ckvn, kvw, CST[:, K_I256, :], kvtmp, tk_kvtmp, sq2, tk_sq2, rstd2, tk_rstd2, 5)):
            for ch in range(nch):
                s = wi % 2
                wi += 1
                wload(P, wc[s], w_in[:, c0 + ch * 128:c0 + (ch + 1) * 128], tk_wc[s])
                for k in range(8):
                    P.mm(ps[s][:, :], wc[s][:, k, :], xbm[:, k, ts], k == 0, k == 7, [tk_wc[s], tk_xbm], [tkp[s]], signal=(k == 7))
                P.act(tbuf[:, ch, :], ps[s][:, :], AF.Copy, [tkp[s]], [tkt])
            for ch in range(nch):
                P.act(sq_, tbuf[:, ch, :], AF.Square, [tkt], [tksq])
                P.mm(ps[pb][:, :], inv, sq_, ch == 0, ch == nch - 1, [tksq, kc_], [tkp[pb]])
            P.act(rstd_, ps[pb][:, :], AF.Sqrt, [tkp[pb], kc_], [tkr], bias=cx.epscol(EPS), scale=1.0)
            P.call("dve", "reciprocal", [tkr], [tkr], True, out=rstd_, in_=rstd_)
            for ch in range(nch):
                P.tt("dve", tbuf[:, ch, :], tbuf[:, ch, :], rstd_, ALU.mult, [tkr], [tkt])
                P.act(dstn[:, ch, ts], tbuf[:, ch, :], AF.Copy, [tkt, kc_], [tkd], scale=wcol[:, ch:ch + 1])
    for tt in range(4):
        ts = slice(tt * 512, (tt + 1) * 512)
        for k in range(8):
            P.mm(ps[3][0:64, :], wkrA[:, k, :], xbm[:, k, ts], k == 0, k == 7, [tk_wkr, tk_xbm], [tkp[3]], signal=(k == 7))
        for k in range(8):
            P.mm(ps[4][0:64, :], wkrB[:, k, :], xbm[:, k, ts], k == 0, k == 7, [tk_wkr, tk_xbm], [tkp[4]], signal=(k == 7))
        P.tt("dve", rs[2][0:64, :], ps[3][0:64, :], CS[0:64, ts], ALU.mult, [tkp[3], tk_cs], [tk_rs])
        P.tt("dve", rs[3][0:64, :], ps[4][0:64, :], SS[0:64, ts], ALU.mult, [tkp[4], tk_cs], [tk_rs])
        P.tt("dve", krT[0:64, ts], rs[2][0:64, :], rs[3][0:64, :], ALU.add, [tk_rs], [tk_kr])
    for t in range(16):
        pb = 6 + t % 2
        for kc in range(2):
            P.mm(ps[pb][:, :].rearrange("p (h d) -> p h d", h=4), ckvn[:, kc, t * 128:(t + 1) * 128],
                 wukv[:, kc, :].rearrange("p (h e) -> p h e", h=4)[:, :, 128:256], kc == 0, kc == 1,
                 [tk_ckvn, tk_wukv], [tkp[pb]])
        vv = Vaug.rearrange("p (h t) d -> p h t d", h=4)[:, :, t, 0:128]
        P.act(vv, ps[pb][:, :].rearrange("p (h d) -> p h d", h=4), AF.Copy, [tkp[pb]], [tk_v])
    tk_dummy = P.tok("dummy")
    P.call("dve", "memset", [tk_kvtmp, tk_sq2, tk_rstd2, tk_rs], list(tk_PT) + [tk_dummy], True, ap=smx[:, 15:16], constant=0.0)
    P.call("dve", "memset", [tk_tmp, tk_sq], tk_sqs + [tk_dummy], True, ap=smx[:, 15:16], constant=0.0)

    for h in range(4):
        P.dma("pool", wuqn, w_uq[:, h * 192:h * 192 + 128].rearrange("(kc p) f -> p kc f", p=128), writes=[tk_wuq])
        P.dma("pool", wuqA, w_uq[:, h * 192 + 128:h * 192 + 192].rearrange("(kc p) f -> p kc f", p=128), writes=[tk_wuq])
        P.dma("pool", wuqB[:, :, 0:32], w_uq[:, h * 192 + 160:h * 192 + 192].rearrange("(kc p) f -> p kc f", p=128),
              writes=[tk_wuq])
        P.dma("pool", wuqB[:, :, 32:64], w_uq[:, h * 192 + 128:h * 192 + 160].rearrange("(kc p) f -> p kc f", p=128),
              writes=[tk_wuq])
        for tt in range(4):
            ts = slice(tt * 512, (tt + 1) * 512)
            for kc in range(3):
                P.mm(ps[0][:, :], wuqn[:, kc, :], cqn[:, kc, ts], kc == 0, kc == 2, [tk_wuq, tk_cqn], [tkp[0]])
            P.act(qnT[:, ts], ps[0][:, :], AF.Copy, [tkp[0]], [tk_qn])
            for kc in range(3):
                P.mm(ps[3][0:64, :], wuqA[:, kc, :], cqn[:, kc, ts], kc == 0, kc == 2, [tk_wuq, tk_cqn], [tkp[3]])
            for kc in range(3):
                P.mm(ps[4][0:64, :], wuqB[:, kc, :], cqn[:, kc, ts], kc == 0, kc == 2, [tk_wuq, tk_cqn], [tkp[4]])
            P.tt("dve", t1[0:64, :], ps[3][0:64, :], CS[0:64, ts], ALU.mult, [tkp[3], tk_cs], [tk_t1])
            P.tt("dve", t2[0:64, :], ps[4][0:64, :], SS[0:64, ts], ALU.mult, [tkp[4], tk_cs], [tk_t2])
            P.tt("dve", qrT[0:64, ts], t1[0:64, :], t2[0:64, :], ALU.add, [tk_t1, tk_t2], [tk_qr])
            for kc in range(2):
                P.mm(ps[1][:, :], wukv[:, kc, h * 256:h * 256 + 128], ckvn[:, kc, ts], kc == 0, kc == 1,
                     [tk_wukv, tk_ckvn], [tkp[1]])
            P.act(knT[:, ts], ps[1][:, :], AF.Copy, [tkp[1]], [tk_kn])
        sbufs = (tmp[:, 0, :].bitcast(BF16), sq.bitcast(BF16))
        for tt in range(4):
            ts = slice(tt * 512, (tt + 1) * 512)
            for bi, (nT, rT, tkn, tkr, col) in enumerate(((qnT, qrT, tk_qn, tk_qr, tt), (knT, krT, tk_kn, tk_kr, 4 + tt))):
                s1 = sbufs[bi][:, 0:512]
                s2 = sbufs[bi][:, 512:1024]
                pb = 2 if bi == 0 else 7
                P.act(s1, nT[:, ts], AF.Square, [tkn], [tk_sqs[2 * bi]])
                P.act(s2[0:64, :], rT[0:64, ts], AF.Square, [tkr], [tk_sqs[2 * bi + 1]])
                P.mm(ps[pb][:, :], ONEb, s1, True, False, [tk_sqs[2 * bi], kc_], [tkp[pb]])
                P.mm(ps[pb][:, :], ONEb[0:64, :], s2[0:64, :], False, True, [tk_sqs[2 * bi + 1], kc_], [tkp[pb]])
                P.call("dve", "tensor_reduce", [tkp[pb]], [tk_smx], True, out=smx[:, col:col + 1], in_=ps[pb][:, :],
                       axis=AX.X, op=ALU.max)
        P.call("dve", "tensor_reduce", [tk_smx], [tk_smx], True, out=smx[:, 8:9], in_=smx[:, 0:4], axis=AX.X, op=ALU.max)
        P.call("dve", "tensor_reduce", [tk_smx], [tk_smx], True, out=smx[:, 9:10], in_=smx[:, 4:8], axis=AX.X, op=ALU.max)
        P.tt("dve", smx[:, 10:11], smx[:, 8:9], smx[:, 9:10], ALU.mult, [tk_smx], [tk_smx])
        P.act(smx[:, 11:12], smx[:, 10:11], AF.Sqrt, [tk_smx], [tk_smx])
        P.ts("dve", smx[:, 12:13], smx[:, 11:12], -1.05 * scale, None, ALU.mult, None, [tk_smx], [tk_smx])
        negm = smx[:, 12:13]
        for G in range(4):
            for j in range(4 * G + 4):
                qs = max(j * 128, G * 512)
                n = (G + 1) * 512 - qs
                off = qs - G * 512
                bank = j % 4
                diag = j >= 4 * G
                kt = slice(j * 128, (j + 1) * 128)
                P.mm(ps[bank][:, 0:n], knT[:, kt], qnT[:, qs:qs + n], True, False, [tk_kn, tk_qn], [tkp[bank]])
                P.mm(ps[bank][:, 0:n], krT[0:64, kt], qrT[0:64, qs:qs + n], False, not diag, [tk_kr, tk_qr], [tkp[bank]])
                if diag:
                    P.mm(ps[bank][:, 0:128], IDb, AMb, False, True, [kc_], [tkp[bank]])
                P.act(PT[:, j, off:off + n], ps[bank][:, 0:n], AF.Exp, [tkp[bank], tk_smx], [tk_PT[j]], bias=negm, scale=scale)
            def epilogue(qb):
                i = 4 * G + qb
                bank = 4 + qb % 2
                rc = smx[:, 13 + qb % 2:14 + qb % 2]
                on = o_n2[qb % 2]
                P.call("dve", "reciprocal", [tkp[bank]], [tk_smx], True, out=rc, in_=ps[bank][:, 128:129])
                P.ts("dve", on, ps[bank][:, 0:128], rc, None, ALU.mult, None, [tkp[bank], tk_smx], [tk_on2[qb % 2]])
                return i, on

            def transpose_out(i, on, qb):
                psb = ps[6 + qb % 2][:, 0:64].bitcast(BF16)
                P.tr(psb, on, IDb, [tk_on2[qb % 2], kc_], [tkp[6 + qb % 2]])
                P.act(o_mla[:, h, i * 128:(i + 1) * 128], psb, AF.Copy, [tkp[6 + qb % 2]], [tk_omla])

            pend = None
            for qb in range(4):
                i = 4 * G + qb
                bank = 4 + qb % 2
                for j in range(i + 1):
                    P.mm(ps[bank][:, 0:129], PT[:, j, qb * 128:(qb + 1) * 128], Vaug[:, h * 16 + j, 0:129], j == 0, j == i,
                         [tk_PT[j], tk_v], [tkp[bank]], signal=(j == i))
                if pend is not None:
                    transpose_out(*pend)
                i_, on_ = epilogue(qb)
                pend = (i_, on_, qb)
            transpose_out(*pend)


def merge_phase(cx, l, xbm, tk_xbm, outs, tk_outs, tk_xs, a1, a2):
    P, X, ps, tkp = cx.P, cx.X, cx.ps, cx.tk_ps
    kc_ = cx.tk_const
    w_in = cx.dr["w_in"][l]
    wsrc = (cx.dr["dn_w_o"][l], cx.dr["cv_w_pw2"][l], cx.dr["mla_w_o"][l])
    merged = a1.bf16(8, T)
    mark1 = a1.off
    wgate = [a2.bf16(3, 8, 128) for _ in range(2)]
    wbo = [a2.bf16(3, 4, 128) for _ in range(2)]
    gt = [a2.f32(512) for _ in range(3)]
    macc, tmpm = a2.f32(512), a2.f32(512)
    tk_wg, tk_wb = P.toks(2, "mwg"), P.toks(2, "mwb")
    tk_gt = P.toks(3, "gt")
    tk_macc, tk_tmpm, tk_mg = P.tok("macc"), P.tok("tmpm"), P.tok("merged")
    bg, bpw = cx.col("b_gate", l), cx.col("cv_b_pw2", l)
    for dc in range(8):
        s = dc % 2
        for i in range(3):
            c0 = C_GATE + i * 1024 + dc * 128
            wload(P, wgate[s][:, i], w_in[:, c0:c0 + 128], tk_wg[s])
            wload(P, wbo[s][:, i], wsrc[i][:, dc * 128:(dc + 1) * 128], tk_wb[s])
        for tt in range(4):
            ts = slice(tt * 512, (tt + 1) * 512)
            for i in range(3):
                for k in range(8):
                    P.mm(ps[i][:, :], wgate[s][:, i, k, :], xbm[:, k, ts], k == 0, k == 7, [tk_wg[s], tk_xbm], [tkp[i]],
                         signal=(k == 7))
                for k in range(4):
                    P.mm(ps[3 + i][:, :], wbo[s][:, i, k, :], outs[i][:, k, ts], k == 0, k == 3, [tk_wb[s], tk_outs[i]],
                         [tkp[3 + i]], signal=(k == 3))
            for i in range(3):
                P.act(gt[i], ps[i][:, :], AF.Sigmoid, [tkp[i], kc_], [tk_gt[i]], bias=bg[:, i * 8 + dc:i * 8 + dc + 1], scale=1.0)
            P.tt("dve", macc, gt[0], ps[3][:, :], ALU.mult, [tk_gt[0], tkp[3]], [tk_macc])
            P.stt("dve", tmpm, ps[4][:, :], bpw[:, dc:dc + 1], gt[1], ALU.add, ALU.mult, [tkp[4], tk_gt[1], kc_], [tk_tmpm])
            P.tt("dve", macc, macc, tmpm, ALU.add, [tk_tmpm], [tk_macc])
            P.tt("dve", tmpm, gt[2], ps[5][:, :], ALU.mult, [tk_gt[2], tkp[5]], [tk_tmpm])
            P.tt("dve", merged[:, dc, ts], macc, tmpm, ALU.add, [tk_macc, tk_tmpm], [tk_mg])
    P.barrier()
    for tt in range(4):
        ts = slice(tt * 512, (tt + 1) * 512)
        for c in range(8):
            P.dma("sp", X[:, c, ts], cx.dr["xs"][:, c, ts], reads=[tk_xs], writes=[cx.tk_X[c][tt]])
    wo = flat(xbm)[:, 0:8 * 1024].rearrange("p (k d) -> p k d", k=8)
    lntmp = a1.f32(4, 512)
    tk_wo = P.tok("wo")
    tk_ln = P.toks(4, "mln")
    w_out = cx.dr["w_out"][l]
    for d0 in range(0, 1024, 256):
        P.dma("pool", wo[:, :, d0:d0 + 256], w_out[:, d0:d0 + 256].rearrange("(kc p) f -> p kc f", p=128), writes=[tk_wo])
    lng, lnb = cx.col("ln2_g", l), cx.col("ln2_b", l)
    it = 0
    pending = None
    ln_gen = None
    for tt in range(4):
        ts = slice(tt * 512, (tt + 1) * 512)
        for dc in range(8):
            bank = it % 4
            it += 1
            for k in range(8):
                P.mm(ps[bank][:, :], wo[:, k, dc * 128:(dc + 1) * 128], merged[:, k, ts], k == 0, k == 7, [tk_wo, tk_mg],
                     [tkp[bank]], signal=(k == 7))
            P.stt("dve", X[:, dc, ts], X[:, dc, ts], ALPHA, ps[bank][:, :], ALU.mult, ALU.add, [tkp[bank]], [cx.tk_X[dc][tt]])
            if dc == 1 and pending is not None:
                ptt, pts = pending
                ln_gen = layernorm_stages(P, cx, X, [cx.tk_X[c][ptt] for c in range(8)], pts, lng, lnb, 8,
                                          [tkp[6], tkp[7]], ps[6], ps[7], lntmp, tk_ln)
                pending = None
            if dc >= 1 and dc % 2 == 1 and ln_gen is not None:
                if next(ln_gen, "done") == "done":
                    ln_gen = None
        if ln_gen is not None:
            for _ in ln_gen:
                pass
            ln_gen = None
        pending = (tt, ts)
    ptt, pts = pending
    emit_layernorm_tile(P, cx, X, [cx.tk_X[c][ptt] for c in range(8)], pts, lng, lnb, 8, [tkp[6], tkp[7]], ps[6], ps[7],
                        lntmp, tk_ln)


def emit_rsqrt(P, cx, out, in_, eps, reads, writes):
    P.act(out, in_, AF.Sqrt, reads, writes, bias=cx.epscol(eps), scale=1.0)
    P.call("dve", "reciprocal", writes, writes, True, out=out, in_=out)


def layernorm_stages(P, cx, Xt, xtoks, tcols, g_ap, b_ap, nch, ps_toks, ps_s1, ps_s2, tmp, tmp_toks, out_fn=None):
    n = tcols.stop - tcols.start
    inv = cx.ones_inv[nch]
    sq, mean, rstd, tt = (tmp[:, i, 0:n] for i in range(4))
    tk_sq, tk_mean, tk_rstd, tk_t = tmp_toks
    for c in range(nch):
        sqc, tkc = (sq, tk_sq) if c % 2 == 0 else (tt, tk_t)
        P.act(sqc, Xt[:, c, tcols], AF.Square, [xtoks[c]], [tkc])
        P.mm(ps_s1[:, 0:n], inv, Xt[:, c, tcols], c == 0, c == nch - 1, [xtoks[c], cx.tk_const], [ps_toks[0]])
        P.mm(ps_s2[:, 0:n], inv, sqc, c == 0, c == nch - 1, [tkc, cx.tk_const], [ps_toks[1]])
    yield
    P.act(mean, ps_s1[:, 0:n], AF.Copy, [ps_toks[0]], [tk_mean])
    P.tt("dve", rstd, mean, mean, ALU.mult, [tk_mean], [tk_rstd])
    P.tt("dve", rstd, ps_s2[:, 0:n], rstd, ALU.subtract, [ps_toks[1], tk_rstd], [tk_rstd])
    emit_rsqrt(P, cx, rstd, rstd, EPS, [tk_rstd], [tk_rstd])
    yield
    for c in range(nch):
        if c == nch // 2:
            yield
        tb_, tkb_ = (tt, tk_t) if c % 2 == 0 else (sq, tk_sq)
        P.tt("dve", tb_, Xt[:, c, tcols], mean, ALU.subtract, [xtoks[c], tk_mean], [tkb_])
        P.tt("dve", tb_, tb_, rstd, ALU.mult, [tkb_, tk_rstd], [tkb_])
        if out_fn is None:
            P.act(Xt[:, c, tcols], tb_, AF.Identity, [tkb_, cx.tk_const], [xtoks[c]],
                  bias=b_ap[:, c:c + 1], scale=g_ap[:, c:c + 1])
        else:
            out_fn(c, tb_, tkb_)


def emit_layernorm_tile(*args, **kw):
    for _ in layernorm_stages(*args, **kw):
        pass


_CACHE = {}


def make_in_maps(inputs):
    consts = make_consts()
    cols = make_cols(inputs).build()
    x = np.asarray(inputs["x"], np.float32)
    pos = np.asarray(inputs["positions"], np.int32)
    shared = {"consts": consts, "cols": cols}
    for k in WEIGHTS:
        shared[k] = np.ascontiguousarray(np.asarray(inputs[k], np.float32))
    maps = []
    for b in range(8):
        m = dict(shared)
        m["xT"] = np.ascontiguousarray(x[b].reshape(T, 8, 128).transpose(2, 1, 0))
        m["pos"] = np.ascontiguousarray(pos[b].reshape(1, T))
        maps.append(m)
    return maps


def from_fm(a):
    return np.ascontiguousarray(a.transpose(2, 1, 0).reshape(T, D))


def kernel(**inputs):
    if "nc" not in _CACHE:
        _CACHE["nc"] = build_program()
    nc = _CACHE["nc"]
    maps = make_in_maps(inputs)
    res = run_bass_kernel_spmd(nc, maps, core_ids=list(range(8)))
    out = np.stack([from_fm(np.asarray(r["outT"])) for r in res.results], axis=0)
    return out.astype(np.float32)
```
